# Optimizing a Trainium2 kernel written in Bass

```python
import math
import jax, jax.numpy as jnp
from jax import lax
import numpy as np

D_MODEL = 4096
BATCH = 8
SEQ = 2048
DEPTH = 2

MIX_WIDTH = D_MODEL
S5_WIDTH = MIX_WIDTH // 2
S5_GROUP = 16
S5_GROUPS = S5_WIDTH // S5_GROUP
S5_STATE = 64
S5_DT_MIN = 1e-3
S5_DT_MAX = 1e-1
GLA_WIDTH = MIX_WIDTH - S5_WIDTH
GLA_HEADS = 8
GLA_DV = GLA_WIDTH // GLA_HEADS
GLA_DK = GLA_DV // 2
GLA_KEY_WIDTH = GLA_HEADS * GLA_DK
GLA_GATE_RANK = 16
GLA_TAU = 16.0
HG_EXPAND = 128
HG_HEADS = MIX_WIDTH // HG_EXPAND
HG_DK = HG_EXPAND
HG_DV = MIX_WIDTH // HG_HEADS
HG_KEY_WIDTH = HG_HEADS * HG_DK
HG_VAL_WIDTH = HG_HEADS * HG_DV
CHUNK = 64
FFN_HIDDEN = -(-8 * D_MODEL // (3 * 256)) * 256
N_EVEN = (DEPTH + 1) // 2
N_ODD = DEPTH // 2
DEEPNORM_ALPHA = (2.0 * DEPTH) ** 0.25
DEEPNORM_BETA = (8.0 * DEPTH) ** -0.25
NORM_EPS = 1e-5
EVEN_SPLITS = (S5_WIDTH,
               S5_WIDTH + GLA_KEY_WIDTH,
               S5_WIDTH + 2 * GLA_KEY_WIDTH,
               S5_WIDTH + 2 * GLA_KEY_WIDTH + GLA_WIDTH,
               S5_WIDTH + 2 * GLA_KEY_WIDTH + 2 * GLA_WIDTH)
EVEN_IN = S5_WIDTH + 2 * GLA_KEY_WIDTH + 2 * GLA_WIDTH + GLA_GATE_RANK
ODD_SPLITS = (HG_KEY_WIDTH, 2 * HG_KEY_WIDTH, 2 * HG_KEY_WIDTH + HG_VAL_WIDTH)
ODD_IN = 2 * HG_KEY_WIDTH + HG_VAL_WIDTH + MIX_WIDTH

kernel_name = "hybrid_s5_gla_hgrn2_deepnorm"


def layer_norm(x, g, b):
    x32 = x.astype(jnp.float32)
    mu = jnp.mean(x32, axis=-1, keepdims=True)
    var = jnp.mean(jnp.square(x32 - mu), axis=-1, keepdims=True)
    return ((x32 - mu) * lax.rsqrt(var + NORM_EPS) * g + b).astype(x.dtype)


def gated_head_rmsnorm(o, gain, gate):
    o32 = o.astype(jnp.float32)
    o32 = o32 * lax.rsqrt(jnp.mean(jnp.square(o32), axis=-1, keepdims=True) + NORM_EPS) * gain
    return o32.reshape(gate.shape) * jax.nn.silu(gate.astype(jnp.float32))


def chunked_gated_linear_recurrence(q, k, v, log_a):
    bsz, t, h, dk = q.shape
    dv = v.shape[-1]
    n = t // CHUNK
    to_chunks = lambda a: a.astype(jnp.float32).reshape(bsz, n, CHUNK, h, a.shape[-1])
    q, k, v, log_a = to_chunks(q), to_chunks(k), to_chunks(v), to_chunks(log_a)
    b = jnp.cumsum(log_a, axis=2)
    b_end = b[:, :, -1:]
    q_dec = q * jnp.exp(b)
    k_inv = k * jnp.exp(-b)
    causal = jnp.tril(jnp.ones((CHUNK, CHUNK), dtype=bool))
    scores = jnp.einsum('bnlhd,bnmhd->bnhlm', q_dec, k_inv)
    scores = jnp.where(causal, scores, 0.0)
    o_intra = jnp.einsum('bnhlm,bnmhv->bnlhv', scores, v)
    k_end = k * jnp.exp(b_end - b)
    chunk_kv = jnp.einsum('bnlhd,bnlhv->bnhdv', k_end, v)
    chunk_decay = jnp.exp(b_end[:, :, 0])

    def step(state, inp):
        dec, kv = inp
        return dec[..., None] * state + kv, state

    s0 = jnp.zeros((bsz, h, dk, dv), jnp.float32)
    _, s_prev = lax.scan(step, s0, (jnp.moveaxis(chunk_decay, 1, 0), jnp.moveaxis(chunk_kv, 1, 0)))
    s_prev = jnp.moveaxis(s_prev, 0, 1)
    o_inter = jnp.einsum('bnlhd,bnhdv->bnlhv', q_dec, s_prev)
    return (o_intra + o_inter).reshape(bsz, t, h, dv)


def _complex_affine_combine(e1, e2):
    a1r, a1i, b1r, b1i = e1
    a2r, a2i, b2r, b2i = e2
    return (a2r * a1r - a2i * a1i,
            a2r * a1i + a2i * a1r,
            a2r * b1r - a2i * b1i + b2r,
            a2r * b1i + a2i * b1r + b2i)


def s5_mixer(u, lam_re, lam_im, log_dt, b_re, b_im, c_re, c_im, d_skip, w_glu):
    bsz, t, _ = u.shape
    uf = u.astype(jnp.float32)
    ug = uf.reshape(bsz, t, S5_GROUPS, S5_GROUP)
    dt = jnp.exp(log_dt.astype(jnp.float32))[:, None]
    lr, li = lam_re.astype(jnp.float32), lam_im.astype(jnp.float32)
    mag = jnp.exp(lr * dt)
    abar_re, abar_im = mag * jnp.cos(li * dt), mag * jnp.sin(li * dt)
    nr, ni = abar_re - 1.0, abar_im
    den = lr * lr + li * li
    fr, fi = (nr * lr + ni * li) / den, (ni * lr - nr * li) / den
    br, bi = b_re.astype(jnp.float32), b_im.astype(jnp.float32)
    bb_re = fr[..., None] * br - fi[..., None] * bi
    bb_im = fr[..., None] * bi + fi[..., None] * br
    bu_re = jnp.einsum('btgi,gpi->btgp', ug, bb_re)
    bu_im = jnp.einsum('btgi,gpi->btgp', ug, bb_im)
    a_re = jnp.broadcast_to(abar_re, (1, t, S5_GROUPS, S5_STATE))
    a_im = jnp.broadcast_to(abar_im, (1, t, S5_GROUPS, S5_STATE))
    _, _, s_re, s_im = lax.associative_scan(_complex_affine_combine, (a_re, a_im, bu_re, bu_im), axis=1)
    y = (jnp.einsum('btgp,gip->btgi', s_re, c_re.astype(jnp.float32))
         - jnp.einsum('btgp,gip->btgi', s_im, c_im.astype(jnp.float32)))
    y = y.reshape(bsz, t, S5_WIDTH) + d_skip.astype(jnp.float32) * uf
    y = jax.nn.gelu(y)
    y = y * jax.nn.sigmoid(y @ w_glu.astype(jnp.float32))
    return y.astype(u.dtype)


def even_mixer(x, w_in, w_out, lam_re, lam_im, log_dt, b_re, b_im, c_re, c_im, d_skip, w_glu,
               w_alpha, b_alpha, norm_g):
    bsz, t, _ = x.shape
    h = x @ w_in
    u, q, k, v, g, a_lr = jnp.split(h, EVEN_SPLITS, axis=-1)
    y_s5 = s5_mixer(u, lam_re, lam_im, log_dt, b_re, b_im, c_re, c_im, d_skip, w_glu)
    q = q.reshape(bsz, t, GLA_HEADS, GLA_DK) * (GLA_DK ** -0.5)
    k = k.reshape(bsz, t, GLA_HEADS, GLA_DK)
    v = v.reshape(bsz, t, GLA_HEADS, GLA_DV)
    log_a = jax.nn.log_sigmoid((a_lr @ w_alpha + b_alpha).astype(jnp.float32)) / GLA_TAU
    o = chunked_gated_linear_recurrence(q, k, v, log_a.reshape(bsz, t, GLA_HEADS, GLA_DK))
    y_gla = gated_head_rmsnorm(o, norm_g, g).astype(x.dtype)
    y = jnp.concatenate([y_s5, y_gla], axis=-1)
    return y @ w_out


def odd_mixer(x, w_in, w_out, lower_bound, norm_g):
    bsz, t, _ = x.shape
    h = x @ w_in
    q, f_logit, i, g = jnp.split(h, ODD_SPLITS, axis=-1)
    lb = lower_bound.astype(jnp.float32)
    f = lb + (1.0 - lb) * jax.nn.sigmoid(f_logit.astype(jnp.float32))
    k = 1.0 - f
    i = jax.nn.silu(i.astype(jnp.float32))
    o = chunked_gated_linear_recurrence(
        q.reshape(bsz, t, HG_HEADS, HG_DK),
        k.reshape(bsz, t, HG_HEADS, HG_DK),
        i.reshape(bsz, t, HG_HEADS, HG_DV),
        jnp.log(f).reshape(bsz, t, HG_HEADS, HG_DK))
    y = gated_head_rmsnorm(o, norm_g, g).astype(x.dtype)
    return y @ w_out


def swiglu_ffn(x, w_gate, w_up, w_down):
    return (jax.nn.silu(x @ w_gate) * (x @ w_up)) @ w_down


def setup_inputs(seed: int = 0) -> dict:
    key = jax.random.key(seed)
    ks = jax.random.split(key, 26)
    f32 = jnp.float32
    nrm = lambda k, shape, std: std * jax.random.normal(k, shape, f32)
    n_idx = jnp.arange(S5_STATE, dtype=f32)
    return {
        "x": nrm(ks[0], (BATCH, SEQ, D_MODEL), 1.0),
        "ev_w_in": nrm(ks[1], (N_EVEN, D_MODEL, EVEN_IN), D_MODEL ** -0.5),
        "ev_w_out": nrm(ks[2], (N_EVEN, MIX_WIDTH, D_MODEL), DEEPNORM_BETA * MIX_WIDTH ** -0.5),
        "s5_lam_re": -0.5 + nrm(ks[3], (N_EVEN, S5_GROUPS, S5_STATE), 0.01),
        "s5_lam_im": math.pi * n_idx + nrm(ks[4], (N_EVEN, S5_GROUPS, S5_STATE), 0.01),
        "s5_log_dt": jax.random.uniform(ks[5], (N_EVEN, S5_GROUPS), f32,
                                        math.log(S5_DT_MIN), math.log(S5_DT_MAX)),
        "s5_b_re": nrm(ks[6], (N_EVEN, S5_GROUPS, S5_STATE, S5_GROUP), (2 * S5_GROUP) ** -0.5),
        "s5_b_im": nrm(ks[7], (N_EVEN, S5_GROUPS, S5_STATE, S5_GROUP), (2 * S5_GROUP) ** -0.5),
        "s5_c_re": nrm(ks[8], (N_EVEN, S5_GROUPS, S5_GROUP, S5_STATE), S5_STATE ** -0.5),
        "s5_c_im": nrm(ks[9], (N_EVEN, S5_GROUPS, S5_GROUP, S5_STATE), S5_STATE ** -0.5),
        "s5_d": nrm(ks[10], (N_EVEN, S5_WIDTH), 1.0),
        "s5_w_glu": nrm(ks[11], (N_EVEN, S5_WIDTH, S5_WIDTH), S5_WIDTH ** -0.5),
        "gla_w_alpha": nrm(ks[12], (N_EVEN, GLA_GATE_RANK, GLA_KEY_WIDTH), GLA_GATE_RANK ** -0.5),
        "gla_b_alpha": nrm(ks[13], (N_EVEN, GLA_KEY_WIDTH), 0.1),
        "gla_norm_g": 1.0 + nrm(ks[14], (N_EVEN, GLA_DV), 0.01),
        "od_w_in": nrm(ks[15], (N_ODD, D_MODEL, ODD_IN), D_MODEL ** -0.5),
        "od_w_out": nrm(ks[16], (N_ODD, MIX_WIDTH, D_MODEL), DEEPNORM_BETA * MIX_WIDTH ** -0.5),
        "hg_lb_table": nrm(ks[17], (DEPTH, HG_KEY_WIDTH), 0.1),
        "hg_norm_g": 1.0 + nrm(ks[18], (N_ODD, HG_DV), 0.01),
        "ln_mix_g": 1.0 + nrm(ks[19], (DEPTH, D_MODEL), 0.01),
        "ln_mix_b": nrm(ks[20], (DEPTH, D_MODEL), 0.01),
        "ln_ffn_g": 1.0 + nrm(ks[21], (DEPTH, D_MODEL), 0.01),
        "ln_ffn_b": nrm(ks[22], (DEPTH, D_MODEL), 0.01),
        "ffn_w_gate": nrm(ks[23], (DEPTH, D_MODEL, FFN_HIDDEN), D_MODEL ** -0.5),
        "ffn_w_up": nrm(ks[24], (DEPTH, D_MODEL, FFN_HIDDEN), D_MODEL ** -0.5),
        "ffn_w_down": nrm(ks[25], (DEPTH, FFN_HIDDEN, D_MODEL), DEEPNORM_BETA * FFN_HIDDEN ** -0.5),
    }


def reference(x, ev_w_in, ev_w_out, s5_lam_re, s5_lam_im, s5_log_dt, s5_b_re, s5_b_im,
              s5_c_re, s5_c_im, s5_d, s5_w_glu, gla_w_alpha, gla_b_alpha, gla_norm_g,
              od_w_in, od_w_out, hg_lb_table, hg_norm_g, ln_mix_g, ln_mix_b, ln_ffn_g, ln_ffn_b,
              ffn_w_gate, ffn_w_up, ffn_w_down):
    lb_soft = jax.nn.softmax(hg_lb_table.astype(jnp.float32), axis=0)
    lower_bounds = jnp.cumsum(lb_soft, axis=0) - lb_soft[0]
    for layer in range(DEPTH):
        if layer % 2 == 0:
            e = layer // 2
            sub = even_mixer(x, ev_w_in[e], ev_w_out[e], s5_lam_re[e], s5_lam_im[e], s5_log_dt[e],
                             s5_b_re[e], s5_b_im[e], s5_c_re[e], s5_c_im[e], s5_d[e], s5_w_glu[e],
                             gla_w_alpha[e], gla_b_alpha[e], gla_norm_g[e])
        else:
            o = layer // 2
            sub = odd_mixer(x, od_w_in[o], od_w_out[o], lower_bounds[layer], hg_norm_g[o])
        x = layer_norm(DEEPNORM_ALPHA * x + sub, ln_mix_g[layer], ln_mix_b[layer])
        x = layer_norm(DEEPNORM_ALPHA * x + swiglu_ffn(x, ffn_w_gate[layer], ffn_w_up[layer], ffn_w_down[layer]),
                       ln_ffn_g[layer], ln_ffn_b[layer])
    return x
```

```python
import math
import os
import numpy as np
_KD = os.environ.get('KDBG', '')
import concourse.bass as bass
import concourse.mybir as mybir
from concourse.bass_utils import run_bass_kernel_spmd

F32 = mybir.dt.float32
BF16 = mybir.dt.bfloat16
I32 = mybir.dt.int32
AF = mybir.ActivationFunctionType
ALU = mybir.AluOpType

ENGS = ["pe", "act", "dve", "pool", "sp"]
ALPHA = 4.0 ** 0.25
EPS = 1e-5
TT = 512
TWO_PI = 2.0 * math.pi * (1.0 - 1e-6)


class Res:
    __slots__ = ("name", "writer", "readers", "excl")

    def __init__(self, name, excl=False):
        self.name = name
        self.writer = None
        self.readers = []
        self.excl = excl


class DmaSem:
    __slots__ = ("sem", "count", "key")

    def __init__(self, sem, key):
        self.sem = sem
        self.count = 0
        self.key = key


class Op:
    __slots__ = ("eng", "fn", "deps", "needs_inc", "sem", "val", "is_dma", "key")


class Prog:
    def __init__(self, nc, same_engine_sync=("act", "dve", "pool")):
        self.nc = nc
        self.ops = {e: [] for e in ENGS}
        self.same = set(same_engine_sync)
        self.esem = {}
        self._stack = []

    def new_sem(self, name):
        cm = self.nc.semaphore(name)
        s = cm.__enter__()
        self._stack.append(cm)
        return s

    def dma_sem(self, name):
        return DmaSem(self.new_sem(name), ("d", name))

    def sbuf(self, name, shape, dt):
        cm = self.nc.sbuf_tensor("sb_" + name, shape, dt)
        t = cm.__enter__()
        self._stack.append(cm)
        return t

    def psum(self, name, shape, dt):
        cm = self.nc.psum_tensor("pt_" + name, shape, dt)
        t = cm.__enter__()
        self._stack.append(cm)
        return t

    def _track(self, op, reads, writes):
        ex = [r for r in reads if r.excl and r not in writes]
        if ex:
            reads = [r for r in reads if not r.excl]
            writes = list(writes) + ex
        deps = []
        for r in list(reads) + list(writes):
            if r.writer is not None:
                deps.append(r.writer)
        for w in writes:
            deps.extend(w.readers)
        for r in reads:
            r.readers.append(op)
        for w in writes:
            w.writer = op
            w.readers = []
        out = []
        seen = set()
        for d in deps:
            if d is op or id(d) in seen:
                continue
            seen.add(id(d))
            if d.eng == op.eng and not d.is_dma and op.eng not in self.same:
                continue
            out.append(d)
            d.needs_inc = True
        op.deps = out

    def op(self, eng, fn, reads=(), writes=()):
        o = Op()
        o.eng = eng
        o.fn = fn
        o.needs_inc = False
        o.is_dma = False
        o.sem = None
        o.val = None
        o.key = ("e", eng)
        self._track(o, reads, writes)
        self.ops[eng].append(o)
        return o

    def dma(self, eng, fn, dsem, n, reads=(), writes=()):
        o = Op()
        o.eng = eng
        o.fn = fn
        o.needs_inc = False
        o.is_dma = True
        dsem.count += 16 * n
        o.sem = dsem.sem
        o.val = dsem.count
        o.key = dsem.key
        self._track(o, reads, writes)
        self.ops[eng].append(o)
        return o

    def emit(self, final_waits=()):
        nc = self.nc
        for e in ENGS:
            self.esem[e] = self.new_sem("es_" + e)
        for e in ENGS:
            c = 0
            for o in self.ops[e]:
                if o.is_dma:
                    continue
                if o.needs_inc:
                    c += 1
                    o.val = c
                    o.sem = self.esem[e]
        bname = {"pe": "tensor", "act": "scalar", "dve": "vector", "pool": "gpsimd", "sp": "sync"}
        self.stats = {}
        with nc.Block() as block:
            for e in ENGS:
                def body(eng, e=e):
                    waited = {}
                    nw = 0
                    for o in self.ops[e]:
                        for d in o.deps:
                            if waited.get(d.key, 0) < d.val:
                                eng.wait_ge(d.sem, d.val)
                                waited[d.key] = d.val
                                nw += 1
                        if o.is_dma:
                            o.fn(eng, o.sem)
                        else:
                            ins = o.fn(eng)
                            if o.needs_inc:
                                ins.then_inc(o.sem, 1)
                    if e == "sp":
                        for d in final_waits:
                            eng.wait_ge(d.sem, d.val)
                    self.stats[e] = (len(self.ops[e]), nw)
                getattr(block, bname[e])(body)

    def close(self):
        for cm in reversed(self._stack):
            cm.__exit__(None, None, None)
        self._stack = []


PV = {}
_off = 0
for _n, _w in [("mix_g0", 32), ("mix_b0", 32), ("ffn_g0", 32), ("ffn_b0", 32),
               ("mix_g1", 32), ("mix_b1", 32), ("ffn_g1", 32), ("ffn_b1", 32),
               ("lbt0", 32), ("lbt1", 32), ("gla_b", 8), ("gla_g", 2), ("hg_g", 1), ("s5_d", 16),
               ("lam_re", 64), ("lam_im", 64), ("logdt", 16)]:
    PV[_n] = _off
    _off += _w
NPV = _off
NLN = 256


def build(NT=4, stop_after=None):
    nc = bass.Bass("TRN2", target_bir_lowering=False)
    P = Prog(nc)
    NTOK = NT * TT

    def din(name, shape, dt=F32):
        if 'nowt' in _KD and (name.startswith("ev_w") or name.startswith("od_w") or name.startswith("ffn_w") or name == "s5_w_glu"):
            return None
        return nc.dram_tensor(name, shape, dt, kind="ExternalInput").ap()

    x_d = din("x", [2048, 4096])
    ev_w_in = din("ev_w_in", [4096, 8208])
    ev_w_out = din("ev_w_out", [4096, 4096])
    w_glu = din("s5_w_glu", [2048, 2048])
    od_w_in = din("od_w_in", [4096, 16384])
    od_w_out = din("od_w_out", [4096, 4096])
    w_gate = din("ffn_w_gate", [2, 4096, 11008])
    w_up = din("ffn_w_up", [2, 4096, 11008])
    w_down = din("ffn_w_down", [2, 11008, 4096])
    w_alpha_d = din("gla_w_alpha", [16, 1024])
    pvec_d = din("pvec", [128, NPV])
    cst_d = din("cst", [128, 768])
    s5bd_d = din("s5bd", [16, 128, 2, 512])
    s5cd_d = din("s5cd", [16, 128, 2, 512])
    out_d = nc.dram_tensor("out", [NTOK, 4096], F32, kind="ExternalOutput").ap()
    s5c_d = nc.dram_tensor("s5c_scr", [16, 128, 3, 512], BF16, kind="Internal").ap()
    stg_d = nc.dram_tensor("st_gla", [8, 128, 256], F32, kind="Internal").ap()
    sth_d = nc.dram_tensor("st_hg", [32, 128, 128], F32, kind="Internal").ap()
    R_s5c = [Res(f"s5c{c}") for c in range(16)]
    R_stg = [Res(f"stg{h}") for h in range(8)]
    R_sth = [Res(f"sth{h}") for h in range(32)]

    xres = P.sbuf("xres", [128, 32, TT], F32)
    R_x = [Res(f"x{c}") for c in range(32)]
    xbf = P.sbuf("xbf", [128, 32, TT], BF16)
    R_xb = [Res(f"xb{c}") for c in range(32)]
    ybuf = P.sbuf("ybuf", [128, 32, TT], BF16)
    R_y = [Res(f"y{c}") for c in range(32)]
    NSLOT = 2
    wsl = [P.sbuf(f"wsl{i}", [128, 32, 256], BF16) for i in range(NSLOT)]
    R_w = [Res(f"w{i}") for i in range(NSLOT)]
    S_w = [P.dma_sem(f"wsem{i}") for i in range(NSLOT)]
    wctr = [0]

    pvec = P.sbuf("pvec", [128, NPV], F32)
    R_pv = Res("pvec")
    cst = P.sbuf("cst", [128, 768], F32)
    R_cst = Res("cst")
    ident_f = cst[:, 0:128]
    maskT = cst[:, 128:256]
    iota = cst[:, 256:768]
    ident_b = P.sbuf("identb", [128, 128], BF16)
    ones_f = P.sbuf("onesf", [128, 128], F32)
    ones_b = P.sbuf("onesb", [128, 128], BF16)
    rmask = P.sbuf("rmask", [128, TT], BF16)
    lbt = P.sbuf("lbt", [128, 3, 32], F32)
    s5k = P.sbuf("s5k", [128, 8, 64], F32)
    s5L = P.sbuf("s5L", [128, 4, 64], F32)
    R_s5k = Res("s5k")
    R_s5L = Res("s5L")
    R_init = Res("s5init")
    walpha = P.sbuf("walpha", [16, 1024], BF16)
    R_wal = Res("walpha")
    alrT = P.sbuf("alrT", [16, TT], BF16)
    R_alr = Res("alrT")
    NWB = 8
    W = [P.sbuf(f"W{i}", [128, TT], F32) for i in range(NWB)]
    R_W = [Res(f"W{i}") for i in range(NWB)]
    NBB = 7
    B = [P.sbuf(f"B{i}", [128, TT], BF16) for i in range(NBB)]
    R_B = [Res(f"B{i}") for i in range(NBB)]
    s5w = P.sbuf("s5w", [128, 5, 512], BF16)
    R_s5wb = Res("s5wb")
    R_s5wc = Res("s5wc")
    vtm = P.sbuf("vtm", [128, 4, 256], BF16)
    R_vtm = Res("vtm")
    ketm = P.sbuf("ketm", [128, 4, 128], BF16)
    R_ketm = Res("ketm")
    Sst = P.sbuf("Sst", [128, 256], F32)
    R_S = Res("Sst")
    Sbf = P.sbuf("Sbf", [128, 256], BF16)
    R_Sb = Res("Sbf")
    smallc = P.sbuf("smallc", [128, 32], F32)
    R_sm = Res("smallc")

    ps_mm = [P.psum(f"psmm{i}", [128, 512], F32) for i in range(3)]
    R_pmm = [Res(f"psmm{i}", True) for i in range(3)]
    mmctr = [0]
    ps_o = [P.psum(f"pso{i}", [128, 512], F32) for i in range(2)]
    R_po = [Res(f"pso{i}", True) for i in range(2)]
    ps_sc = P.psum("pssc", [128, 512], F32)
    R_psc = Res("pssc", True)
    ps_kv = P.psum("pskv", [128, 512], F32)
    R_pkv = Res("pskv", True)
    ps_tr = P.psum("pstr", [128, 512], F32)
    R_ptr = Res("pstr", True)

    S_misc = {}

    def msem(name):
        if name not in S_misc:
            S_misc[name] = P.dma_sem(name)
        return S_misc[name]

    def V(eng, fn, reads=(), writes=()):
        return P.op(eng, fn, reads, writes)

    def load_w(src, kc, ncols, parts=None):
        i = wctr[0] % NSLOT
        wctr[0] += 1
        if parts is None:
            parts = [(src, 0, ncols)]
        n = len(parts)

        def fn(e, s, i=i, parts=parts, kc=kc):
            for (sv, c0, ncl) in parts:
                e.dma_start(out=wsl[i][:, 0:kc, c0:c0 + ncl],
                            in_=sv.rearrange("(kc p) n -> p kc n", p=128)).then_inc(s, 16)
        P.dma("pool", fn, S_w[i], n, writes=[R_w[i]])
        return i

    def next_mm():
        i = mmctr[0] % 3
        mmctr[0] += 1
        return i

    def mm_fm(slot, col0, kc, rhs, rhs_res, ps, ps_res, m=128):
        def fn(e):
            ins = None
            for k in range(kc):
                ins = e.matmul(ps[0:m, :], lhsT=wsl[slot][:, k, col0:col0 + m], rhs=rhs(k),
                               start=(k == 0), stop=(k == kc - 1))
            return ins
        V("pe", fn, reads=[R_w[slot]] + list(rhs_res), writes=[ps_res])

    xb_rhs = lambda k: xbf[:, k, :]

    P.dma("sp", lambda e, s: e.dma_start(out=pvec[:], in_=pvec_d).then_inc(s, 16), msem("pv"), 1, writes=[R_pv])
    P.dma("sp", lambda e, s: e.dma_start(out=cst[:], in_=cst_d).then_inc(s, 16), msem("cst"), 1, writes=[R_cst])
    if 'nowal' not in _KD:
        P.dma("pool", lambda e, s: e.dma_start(out=walpha[:], in_=w_alpha_d).then_inc(s, 16), msem("wal"), 1, writes=[R_wal])
    R_c2 = Res("consts2")
    V("dve", lambda e: e.tensor_copy(out=ident_b[:], in_=ident_f), reads=[R_cst], writes=[R_c2])
    V("dve", lambda e: e.memset(ones_f[:], 1.0), writes=[R_c2])
    V("dve", lambda e: e.memset(ones_b[:], 1.0), writes=[R_c2])
    V("dve", lambda e: e.memset(rmask[:], 1.0), writes=[R_c2])
    V("dve", lambda e: e.memset(rmask[:].rearrange("p (c l) -> p c l", l=64)[:, :, 0:1], 0.0), writes=[R_c2])
    V("dve", lambda e: e.tensor_tensor(out=lbt[:, 0, :], in0=pvec[:, PV["lbt1"]:PV["lbt1"] + 32],
                                       in1=pvec[:, PV["lbt0"]:PV["lbt0"] + 32], op=ALU.subtract), reads=[R_pv], writes=[R_c2])
    V("act", lambda e: e.activation(out=lbt[:, 0, :], in_=lbt[:, 0, :], func=AF.Sigmoid), reads=[R_c2], writes=[R_c2])
    V("dve", lambda e: e.tensor_scalar(out=lbt[:, 1, :], in0=lbt[:, 0, :], scalar1=-1.0, scalar2=1.0, op0=ALU.mult, op1=ALU.add),
      reads=[R_c2], writes=[R_c2])
    V("dve", lambda e: e.tensor_scalar(out=lbt[:, 2, :], in0=lbt[:, 1, :], scalar1=-1.0, scalar2=None, op0=ALU.mult),
      reads=[R_c2], writes=[R_c2])

    lre = pvec[:, PV["lam_re"]:PV["lam_re"] + 64]
    lim = pvec[:, PV["lam_im"]:PV["lam_im"] + 64]
    K = lambda i: s5k[:, i, :]
    t6, t7 = K(6), K(7)
    xi64 = P.sbuf("xi64", [128, 64], I32)
    t8 = P.sbuf("t8", [128, 64], F32)
    t9 = P.sbuf("t9", [128, 64], F32)

    def frac_sin(out, xin, eng_reads):
        V("dve", lambda e: e.tensor_copy(out=xi64[:], in_=xin), reads=eng_reads, writes=[R_s5k])
        V("dve", lambda e: e.tensor_tensor(out=t9[:], in0=xin, in1=xi64[:], op=ALU.subtract), reads=[R_s5k], writes=[R_s5k])
        V("act", lambda e: e.activation(out=out, in_=t9[:], func=AF.Sin, scale=TWO_PI), reads=[R_s5k], writes=[R_s5k])

    if 'nos5k' not in _KD:
        V("act", lambda e: e.activation(out=t8[:, 0:16], in_=pvec[:, PV["logdt"]:PV["logdt"] + 16], func=AF.Exp), reads=[R_pv], writes=[R_s5k])
        V("dve", lambda e: e.tensor_copy(out=t7.rearrange("p (c s) -> p c s", s=4),
                                         in_=t8[:, 0:16].unsqueeze(2).to_broadcast([128, 16, 4])), reads=[R_s5k], writes=[R_s5k])
        V("dve", lambda e: e.tensor_tensor(out=t6, in0=lre, in1=t7, op=ALU.mult), reads=[R_s5k, R_pv], writes=[R_s5k])
        V("act", lambda e: e.activation(out=K(1), in_=t6, func=AF.Exp), reads=[R_s5k], writes=[R_s5k])
        V("dve", lambda e: e.tensor_tensor(out=t6, in0=lim, in1=t7, op=ALU.mult), reads=[R_s5k, R_pv], writes=[R_s5k])
        V("dve", lambda e: e.tensor_scalar(out=t6, in0=t6, scalar1=1.0 / (2.0 * math.pi), scalar2=None, op0=ALU.mult), reads=[R_s5k], writes=[R_s5k])
        V("dve", lambda e: e.tensor_scalar(out=K(0), in0=t6, scalar1=1.0, scalar2=None, op0=ALU.add), reads=[R_s5k], writes=[R_s5k])
        frac_sin(K(3), t6, [R_s5k])
        V("dve", lambda e: e.tensor_scalar(out=t8[:], in0=t6, scalar1=0.25, scalar2=None, op0=ALU.add), reads=[R_s5k], writes=[R_s5k])
        frac_sin(K(2), t8[:], [R_s5k])
        V("dve", lambda e: e.tensor_tensor(out=K(2), in0=K(2), in1=K(1), op=ALU.mult), reads=[R_s5k], writes=[R_s5k])
        V("dve", lambda e: e.tensor_scalar(out=K(2), in0=K(2), scalar1=-1.0, scalar2=None, op0=ALU.add), reads=[R_s5k], writes=[R_s5k])
        V("dve", lambda e: e.tensor_tensor(out=K(3), in0=K(3), in1=K(1), op=ALU.mult), reads=[R_s5k], writes=[R_s5k])
        V("dve", lambda e: e.tensor_tensor(out=t7, in0=lre, in1=lre, op=ALU.mult), reads=[R_s5k, R_pv], writes=[R_s5k])
        V("dve", lambda e: e.tensor_tensor(out=t8[:], in0=lim, in1=lim, op=ALU.mult), reads=[R_s5k, R_pv], writes=[R_s5k])
        V("dve", lambda e: e.tensor_tensor(out=t7, in0=t7, in1=t8[:], op=ALU.add), reads=[R_s5k], writes=[R_s5k])
        V("dve", lambda e: e.reciprocal(out=t7, in_=t7), reads=[R_s5k], writes=[R_s5k])
        V("dve", lambda e: e.tensor_tensor(out=K(4), in0=K(2), in1=lre, op=ALU.mult), reads=[R_s5k, R_pv], writes=[R_s5k])
        V("dve", lambda e: e.tensor_tensor(out=t8[:], in0=K(3), in1=lim, op=ALU.mult), reads=[R_s5k, R_pv], writes=[R_s5k])
        V("dve", lambda e: e.tensor_tensor(out=K(4), in0=K(4), in1=t8[:], op=ALU.add), reads=[R_s5k], writes=[R_s5k])
        V("dve", lambda e: e.tensor_tensor(out=K(4), in0=K(4), in1=t7, op=ALU.mult), reads=[R_s5k], writes=[R_s5k])
        V("dve", lambda e: e.tensor_tensor(out=K(5), in0=K(3), in1=lre, op=ALU.mult), reads=[R_s5k, R_pv], writes=[R_s5k])
        V("dve", lambda e: e.tensor_tensor(out=t8[:], in0=K(2), in1=lim, op=ALU.mult), reads=[R_s5k, R_pv], writes=[R_s5k])
        V("dve", lambda e: e.tensor_tensor(out=K(5), in0=K(5), in1=t8[:], op=ALU.subtract), reads=[R_s5k], writes=[R_s5k])
        V("dve", lambda e: e.tensor_tensor(out=K(5), in0=K(5), in1=t7, op=ALU.mult), reads=[R_s5k], writes=[R_s5k])
        V("dve", lambda e: e.tensor_scalar(out=t6, in0=K(0), scalar1=float(TT), scalar2=None, op0=ALU.mult), reads=[R_s5k], writes=[R_s5k])
        frac_sin(K(3), t6, [R_s5k])
        V("dve", lambda e: e.tensor_scalar(out=t8[:], in0=t6, scalar1=0.25, scalar2=None, op0=ALU.add), reads=[R_s5k], writes=[R_s5k])
        frac_sin(K(2), t8[:], [R_s5k])
    V("dve", lambda e: e.memset(s5L[:], 0.0), writes=[R_s5L, R_init])

    for ch in range(0 if 'nocd' in _KD else 16):
        cdr, cdi = W[0], W[1]
        P.dma("sp", lambda e, s, ch=ch: (e.dma_start(out=W[0][:], in_=s5cd_d[ch, :, 0, :]).then_inc(s, 16),
                                        e.dma_start(out=W[1][:], in_=s5cd_d[ch, :, 1, :]).then_inc(s, 16)),
              msem("cdld"), 2, writes=[R_W[0], R_W[1]])
        for s in range(4):
            col = ch * 4 + s
            sl = slice(128 * s, 128 * s + 128)
            fr = s5k[:, 4, col:col + 1]
            fi = s5k[:, 5, col:col + 1]
            V("dve", lambda e, sl=sl, fi=fi: e.tensor_scalar(out=W[2][:, sl], in0=W[1][:, sl], scalar1=fi, scalar2=None, op0=ALU.mult),
              reads=[R_W[1], R_s5k], writes=[R_W[2]])
            V("dve", lambda e, sl=sl, fr=fr: e.scalar_tensor_tensor(out=W[3][:, sl], in0=W[0][:, sl], scalar=fr, in1=W[2][:, sl],
                                                                   op0=ALU.mult, op1=ALU.subtract),
              reads=[R_W[0], R_W[2], R_s5k], writes=[R_W[3]])
            V("dve", lambda e, sl=sl, fi=fi: e.tensor_scalar(out=W[2][:, sl], in0=W[0][:, sl], scalar1=fi, scalar2=None, op0=ALU.mult),
              reads=[R_W[0], R_s5k], writes=[R_W[2]])
            V("dve", lambda e, sl=sl, fr=fr: e.scalar_tensor_tensor(out=W[4][:, sl], in0=W[1][:, sl], scalar=fr, in1=W[2][:, sl],
                                                                   op0=ALU.mult, op1=ALU.add),
              reads=[R_W[1], R_W[2], R_s5k], writes=[R_W[4]])
        V("act", lambda e: e.activation(out=s5w[:, 2, :], in_=W[3][:], func=AF.Copy), reads=[R_W[3]], writes=[R_s5wc])
        V("act", lambda e: e.activation(out=s5w[:, 3, :], in_=W[3][:], func=AF.Copy, scale=-1.0), reads=[R_W[3]], writes=[R_s5wc])
        V("act", lambda e: e.activation(out=s5w[:, 4, :], in_=W[4][:], func=AF.Copy, scale=-1.0), reads=[R_W[4]], writes=[R_s5wc])
        P.dma("sp", lambda e, s, ch=ch: e.dma_start(out=s5c_d[ch], in_=s5w[:, 2:5, :]).then_inc(s, 16), msem("cdst"), 1,
              reads=[R_s5wc], writes=[R_s5c[ch]])
    V("dve", lambda e: e.memset(Sst[:], 0.0), writes=[R_S])
    for h in range(0 if 'nostz' in _KD else 8):
        P.dma("sp", lambda e, s, h=h: e.dma_start(out=stg_d[h], in_=Sst[:]).then_inc(s, 16), msem("stz"), 1, reads=[R_S], writes=[R_stg[h]])
    for h in range(0 if 'nostz' in _KD else 32):
        P.dma("sp", lambda e, s, h=h: e.dma_start(out=sth_d[h], in_=Sst[:, 0:128]).then_inc(s, 16), msem("stz"), 1, reads=[R_S], writes=[R_sth[h]])

    def load_x_tile(t):
        k = 0
        for tb in range(int(os.environ.get('LOOPT', '4'))):
            for cb in range(int(os.environ.get('LOOPN', '8'))):
                wi = k % 2
                k += 1
                r0 = t * TT + tb * 128
                P.dma("sp", lambda e, s, wi=wi, r0=r0, cb=cb: e.dma_start(out=W[wi][:], in_=x_d[r0:r0 + 128, cb * 512:(cb + 1) * 512]).then_inc(s, 16),
                      msem(f"xin{wi}"), 1, writes=[R_W[wi]])

                def tr(e, wi=wi):
                    ins = None
                    for j in range(4):
                        ins = e.transpose(out=ps_tr[:, 128 * j:128 * j + 128], in_=W[wi][:, 128 * j:128 * j + 128], identity=ident_f)
                    return ins
                V("pe", tr, reads=[R_W[wi], R_cst], writes=[R_ptr])
                dst = xres[:, cb * 4:cb * 4 + 4, tb * 128:tb * 128 + 128]
                dstb = xbf[:, cb * 4:cb * 4 + 4, tb * 128:tb * 128 + 128]
                src = ps_tr[:].rearrange("p (j l) -> p j l", l=128)
                V("act", lambda e, dst=dst, src=src: e.activation(out=dst, in_=src, func=AF.Copy, scale=ALPHA),
                  reads=[R_ptr], writes=R_x[cb * 4:cb * 4 + 4])
                V("dve", lambda e, dstb=dstb, src=src: e.tensor_copy(out=dstb, in_=src), reads=[R_ptr], writes=R_xb[cb * 4:cb * 4 + 4])

    def store_tile(t, dram, scale=1.0):
        k = 0
        last = []
        for tb in range(int(os.environ.get('LOOPT', '4'))):
            for cb in range(int(os.environ.get('STN', '8'))):
                wi = k % 2
                k += 1

                def tr(e, cb=cb, tb=tb):
                    ins = None
                    for j in range(4):
                        ins = e.transpose(out=ps_tr[:, 128 * j:128 * j + 128], in_=xres[:, cb * 4 + j, tb * 128:tb * 128 + 128], identity=ident_f)
                    return ins
                V("pe", tr, reads=R_x[cb * 4:cb * 4 + 4] + [R_cst], writes=[R_ptr])
                V("act", lambda e, wi=wi: e.activation(out=W[wi][:], in_=ps_tr[:], func=AF.Copy, scale=scale), reads=[R_ptr], writes=[R_W[wi]])
                r0 = t * TT + tb * 128
                o = P.dma("sp", lambda e, s, wi=wi, r0=r0, cb=cb: e.dma_start(out=dram[r0:r0 + 128, cb * 512:(cb + 1) * 512], in_=W[wi][:]).then_inc(s, 16),
                          msem(f"xout{wi}"), 1, reads=[R_W[wi]])
                last.append(o)
        return last[-2:]

    def layer_norm(gname, bname, final=False):
        s1, s2 = ps_o[0], ps_o[1]
        for c in range(32):
            wi = c % 2
            V("act", lambda e, c=c, wi=wi: e.activation(out=W[wi][:], in_=xres[:, c, :], func=AF.Square), reads=[R_x[c]], writes=[R_W[wi]])
            V("pe", lambda e, c=c: e.matmul(s1[:], lhsT=ones_f[:], rhs=xres[:, c, :], start=(c == 0), stop=(c == 31)),
              reads=[R_x[c], R_c2], writes=[R_po[0]])
            V("pe", lambda e, c=c, wi=wi: e.matmul(s2[:], lhsT=ones_f[:], rhs=W[wi][:], start=(c == 0), stop=(c == 31)),
              reads=[R_W[wi], R_c2], writes=[R_po[1]])
        mean, rstd, tmp = W[2], W[3], W[4]
        V("act", lambda e: e.activation(out=mean[:], in_=s1[:], func=AF.Copy, scale=1.0 / 4096.0), reads=[R_po[0]], writes=[R_W[2]])
        V("dve", lambda e: e.tensor_tensor(out=tmp[:], in0=mean[:], in1=mean[:], op=ALU.mult), reads=[R_W[2]], writes=[R_W[4]])
        V("dve", lambda e: e.scalar_tensor_tensor(out=rstd[:], in0=s2[:], scalar=1.0 / 4096.0, in1=tmp[:], op0=ALU.mult, op1=ALU.subtract),
          reads=[R_po[1], R_W[4]], writes=[R_W[3]])
        V("act", lambda e: e.activation(out=rstd[:], in_=rstd[:], func=AF.Sqrt, bias=EPS), reads=[R_W[3]], writes=[R_W[3]])
        V("dve", lambda e: e.reciprocal(out=rstd[:], in_=rstd[:]), reads=[R_W[3]], writes=[R_W[3]])
        g0, b0 = PV[gname], PV[bname]
        for c in range(32):
            wi = 5 + (c % 2)
            V("dve", lambda e, c=c, wi=wi: e.tensor_tensor(out=W[wi][:], in0=xres[:, c, :], in1=mean[:], op=ALU.subtract),
              reads=[R_x[c], R_W[2]], writes=[R_W[wi]])
            V("dve", lambda e, wi=wi: e.tensor_tensor(out=W[wi][:], in0=W[wi][:], in1=rstd[:], op=ALU.mult),
              reads=[R_W[3]], writes=[R_W[wi]])
            V("act", lambda e, c=c, wi=wi: e.activation(out=xres[:, c, :], in_=W[wi][:], func=AF.Identity,
                                                        scale=pvec[:, g0 + c:g0 + c + 1], bias=pvec[:, b0 + c:b0 + c + 1]),
              reads=[R_W[wi], R_pv], writes=[R_x[c]])
            V("act", lambda e, c=c: e.activation(out=xbf[:, c, :], in_=xres[:, c, :], func=AF.Copy), reads=[R_x[c]], writes=[R_xb[c]])
            if not final:
                V("dve", lambda e, c=c: e.tensor_scalar(out=xres[:, c, :], in0=xres[:, c, :], scalar1=ALPHA, scalar2=None, op0=ALU.mult),
                  reads=[R_xb[c]], writes=[R_x[c]])

    def gemm_residual(wsrc, nk, rhs, rhs_res_fn, krows0=0):
        for mp in range(16):
            sl = load_w(wsrc[krows0:krows0 + nk * 128, mp * 256:(mp + 1) * 256], nk, 256)
            for j in range(2):
                m = mp * 2 + j
                pi = next_mm()
                mm_fm(sl, 128 * j, nk, rhs, rhs_res_fn(), ps_mm[pi], R_pmm[pi])
                V("dve", lambda e, m=m, pi=pi: e.tensor_tensor(out=xres[:, m, :], in0=xres[:, m, :], in1=ps_mm[pi][:], op=ALU.add),
                  reads=[R_pmm[pi]], writes=[R_x[m]])

    def ffn(l):
        groups = [(0, 22), (22, 44), (44, 65), (65, 86)]
        for (g0, g1) in groups:
            c = g0
            while c < g1:
                nch = min(2, g1 - c)
                sg = load_w(w_gate[l, :, c * 128:(c + nch) * 128], 32, nch * 128)
                su = load_w(w_up[l, :, c * 128:(c + nch) * 128], 32, nch * 128)
                pgs = []
                for j in range(nch):
                    pg = next_mm()
                    mm_fm(sg, 128 * j, 32, xb_rhs, R_xb, ps_mm[pg], R_pmm[pg])
                    pgs.append(pg)
                for j in range(nch):
                    pg = pgs[j]
                    wi = 6 + (j % 2)
                    V("act", lambda e, pg=pg, wi=wi: e.activation(out=W[wi][:], in_=ps_mm[pg][:], func=AF.Silu), reads=[R_pmm[pg]], writes=[R_W[wi]])
                for j in range(nch):
                    pu = next_mm()
                    mm_fm(su, 128 * j, 32, xb_rhs, R_xb, ps_mm[pu], R_pmm[pu])
                    wi = 6 + (j % 2)
                    hi = c + j - g0
                    V("dve", lambda e, pu=pu, wi=wi, hi=hi: e.tensor_tensor(out=ybuf[:, hi, :], in0=W[wi][:], in1=ps_mm[pu][:], op=ALU.mult),
                      reads=[R_W[wi], R_pmm[pu]], writes=[R_y[hi]])
                c += nch
            nk = g1 - g0
            gemm_residual(w_down[l], nk, lambda k: ybuf[:, k, :], lambda nk=nk: R_y[0:nk], krows0=g0 * 128)

    def recurrence(qf, R_qf, kf, R_kf, la, R_la, vt_col0, dv, st_dram, R_st, qscale, gate_fn, gain_col, ychunk0):
        nsl = dv // 128
        P.dma("sp", lambda e, s: e.dma_start(out=Sst[:, 0:dv], in_=st_dram).then_inc(s, 16), msem("stld"), 1, reads=[R_st], writes=[R_S])
        V("act", lambda e: e.activation(out=Sbf[:, 0:dv], in_=Sst[:, 0:dv], func=AF.Copy), reads=[R_S], writes=[R_Sb])
        bb, R_bb = la, R_la
        V("dve", lambda e: e.tensor_tensor_scan(out=bb[:], data0=rmask[:], data1=la[:], initial=0.0, op0=ALU.mult, op1=ALU.add),
          reads=[R_c2], writes=[R_bb])
        Et, R_Et = W[6], R_W[6]
        qd, R_qd = B[0], R_B[0]
        ki, R_ki = B[1], R_B[1]
        ke, R_ke = B[2], R_B[2]
        V("act", lambda e: e.activation(out=Et[:], in_=bb[:], func=AF.Exp), reads=[R_bb], writes=[R_Et])
        V("dve", lambda e: e.scalar_tensor_tensor(out=qd[:], in0=qf[:], scalar=qscale, in1=Et[:], op0=ALU.mult, op1=ALU.mult),
          reads=[R_qf, R_Et], writes=[R_qd])
        V("act", lambda e: e.activation(out=Et[:], in_=bb[:], func=AF.Exp, scale=-1.0), reads=[R_bb, R_qd], writes=[R_Et])
        V("dve", lambda e: e.tensor_tensor(out=ki[:], in0=kf[:], in1=Et[:], op=ALU.mult), reads=[R_kf, R_Et], writes=[R_ki])
        b3 = bb[:].rearrange("p (c l) -> p c l", l=64)
        V("dve", lambda e: e.tensor_copy(out=smallc[:, 0:8], in_=b3[:, :, 63]), reads=[R_bb], writes=[R_sm])
        V("act", lambda e: e.activation(out=smallc[:, 8:16], in_=smallc[:, 0:8], func=AF.Exp), reads=[R_sm], writes=[R_sm])
        V("dve", lambda e: e.tensor_tensor(out=Et[:].rearrange("p (c l) -> p c l", l=64),
                                           in0=smallc[:, 0:8].unsqueeze(2).to_broadcast([128, 8, 64]), in1=b3, op=ALU.subtract),
          reads=[R_sm, R_bb, R_ki], writes=[R_Et])
        V("act", lambda e: e.activation(out=Et[:], in_=Et[:], func=AF.Exp), reads=[R_Et], writes=[R_Et])
        V("dve", lambda e: e.tensor_tensor(out=ke[:], in0=kf[:], in1=Et[:], op=ALU.mult), reads=[R_kf, R_Et], writes=[R_ke])
        ptb = ps_tr[:].bitcast(BF16)

        def trk(e):
            ins = None
            for j in range(4):
                ins = e.transpose(out=ptb[:, 128 * j:128 * j + 128], in_=ke[:, 128 * j:128 * j + 128], identity=ident_b[:])
            return ins
        V("pe", trk, reads=[R_ke, R_c2], writes=[R_ptr])
        V("act", lambda e: e.activation(out=ketm[:].rearrange("p j d -> p (j d)"), in_=ptb[:, 0:512], func=AF.Copy), reads=[R_ptr], writes=[R_ketm])
        def sc(e):
            ins = None
            for j in range(4):
                cs = slice(128 * j, 128 * j + 128)
                ins = e.matmul(ps_sc[:, cs], lhsT=ki[:, cs], rhs=qd[:, cs], start=True, stop=True)
            return ins
        V("pe", sc, reads=[R_ki, R_qd], writes=[R_psc])
        sT, R_sT = B[3], R_B[3]
        V("dve", lambda e: e.tensor_tensor(out=sT[:].rearrange("p (j l) -> p j l", l=128), in0=ps_sc[:].rearrange("p (j l) -> p j l", l=128),
                                           in1=maskT.unsqueeze(1).to_broadcast([128, 4, 128]), op=ALU.mult),
          reads=[R_psc, R_cst], writes=[R_sT])
        for c in range(8):
            j, hb = c // 2, (c % 2) * 64
            cs = slice(64 * c, 64 * c + 64)

            kvb, R_kvb = (ps_kv, R_pkv) if c % 2 == 0 else (ps_tr, R_ptr)
            V("pe", lambda e, j=j, hb=hb, kvb=kvb: e.matmul(kvb[:, 0:dv], lhsT=ketm[hb:hb + 64, j, :], rhs=vtm[hb:hb + 64, j, vt_col0:vt_col0 + dv], start=True, stop=True),
              reads=[R_ketm, R_vtm], writes=[R_kvb])

            def inter(e, cs=cs, c=c, j=j, hb=hb):
                ins = None
                for s in range(nsl):
                    e.matmul(ps_o[s][:, cs], lhsT=vtm[hb:hb + 64, j, vt_col0 + 128 * s:vt_col0 + 128 * s + 128], rhs=sT[hb:hb + 64, cs],
                             start=True, stop=False)
                    ins = e.matmul(ps_o[s][:, cs], lhsT=Sbf[:, 128 * s:128 * s + 128], rhs=qd[:, cs], start=False, stop=True)
                return ins
            V("pe", inter, reads=[R_Sb, R_qd, R_vtm, R_sT], writes=R_po[0:nsl])
            V("dve", lambda e, c=c, kvb=kvb: e.scalar_tensor_tensor(out=Sst[:, 0:dv], in0=Sst[:, 0:dv], scalar=smallc[:, 8 + c:9 + c], in1=kvb[:, 0:dv],
                                                                    op0=ALU.mult, op1=ALU.add),
              reads=[R_kvb, R_sm], writes=[R_S])
            if c < 7:
                V("act", lambda e: e.activation(out=Sbf[:, 0:dv], in_=Sst[:, 0:dv], func=AF.Copy), reads=[R_S], writes=[R_Sb])
        P.dma("sp", lambda e, s: e.dma_start(out=st_dram, in_=Sst[:, 0:dv]).then_inc(s, 16), msem("stst"), 1, reads=[R_S], writes=[R_st])
        osq, R_osq = B[4], R_B[4]
        for s in range(nsl):
            V("act", lambda e, s=s: e.activation(out=osq[:], in_=ps_o[s][:], func=AF.Square), reads=[R_po[s]], writes=[R_osq])
            V("pe", lambda e, s=s: e.matmul(ps_sc[:], lhsT=ones_b[:], rhs=osq[:], start=(s == 0), stop=(s == nsl - 1)),
              reads=[R_osq, R_c2], writes=[R_psc])
        rs, R_rs = W[6], R_W[6]
        V("act", lambda e: e.activation(out=rs[:], in_=ps_sc[:], func=AF.Sqrt, bias=EPS, scale=1.0 / dv), reads=[R_psc], writes=[R_rs])
        V("dve", lambda e: e.reciprocal(out=rs[:], in_=rs[:]), reads=[R_rs], writes=[R_rs])
        for s in range(nsl):
            gs, R_gs = gate_fn(s)
            yt, R_yt = W[7], R_W[7]
            V("dve", lambda e, s=s: e.scalar_tensor_tensor(out=yt[:], in0=ps_o[s][:], scalar=pvec[:, gain_col + s:gain_col + s + 1], in1=rs[:],
                                                           op0=ALU.mult, op1=ALU.mult),
              reads=[R_po[s], R_rs, R_pv], writes=[R_yt])
            V("dve", lambda e, s=s, gs=gs: e.tensor_tensor(out=ybuf[:, ychunk0 + s, :], in0=yt[:], in1=gs, op=ALU.mult),
              reads=[R_yt, R_gs], writes=[R_y[ychunk0 + s]])

    def v_token_major(slot, func):
        for tb in range(4):
            pi = next_mm()

            def fn(e, tb=tb, pi=pi):
                ins = None
                for k in range(32):
                    ins = e.matmul(ps_mm[pi][:, 0:256], lhsT=xbf[:, k, tb * 128:tb * 128 + 128], rhs=wsl[slot][:, k, 0:256],
                                   start=(k == 0), stop=(k == 31))
                return ins
            V("pe", fn, reads=[R_w[slot]] + R_xb, writes=[R_pmm[pi]])
            V("act", lambda e, tb=tb, pi=pi: e.activation(out=vtm[:, tb, :], in_=ps_mm[pi][:, 0:256], func=func), reads=[R_pmm[pi]], writes=[R_vtm])

    def odd_mixer():
        for hp in range(16):
            c0 = hp * 256
            sq_ = load_w(od_w_in[:, c0:c0 + 256], 32, 256)
            for hh in range(2):
                pi = next_mm()
                mm_fm(sq_, 128 * hh, 32, xb_rhs, R_xb, ps_mm[pi], R_pmm[pi])
                V("act", lambda e, hh=hh, pi=pi: e.activation(out=W[hh][:], in_=ps_mm[pi][:], func=AF.Copy), reads=[R_pmm[pi]], writes=[R_W[hh]])
            sf = load_w(od_w_in[:, 4096 + c0:4096 + c0 + 256], 32, 256)
            for hh in range(2):
                h = hp * 2 + hh
                pi = next_mm()
                mm_fm(sf, 128 * hh, 32, xb_rhs, R_xb, ps_mm[pi], R_pmm[pi])
                sg_, R_sg = W[7], R_W[7]
                V("act", lambda e, pi=pi: e.activation(out=sg_[:], in_=ps_mm[pi][:], func=AF.Sigmoid), reads=[R_pmm[pi]], writes=[R_sg])
                V("dve", lambda e, hh=hh, h=h: e.tensor_scalar(out=W[2 + hh][:], in0=sg_[:], scalar1=lbt[:, 2, h:h + 1], scalar2=lbt[:, 1, h:h + 1],
                                                               op0=ALU.mult, op1=ALU.add), reads=[R_sg, R_c2], writes=[R_W[2 + hh]])
                V("dve", lambda e, hh=hh, h=h: e.tensor_scalar(out=W[4 + hh][:], in0=sg_[:], scalar1=lbt[:, 1, h:h + 1], scalar2=lbt[:, 0, h:h + 1],
                                                               op0=ALU.mult, op1=ALU.add), reads=[R_sg, R_c2], writes=[R_W[4 + hh]])
                V("act", lambda e, hh=hh: e.activation(out=W[4 + hh][:], in_=W[4 + hh][:], func=AF.Ln), reads=[R_W[4 + hh]], writes=[R_W[4 + hh]])
            si = load_w(od_w_in[:, 8192 + c0:8192 + c0 + 256], 32, 256)
            v_token_major(si, AF.Silu)
            sg = load_w(od_w_in[:, 12288 + c0:12288 + c0 + 256], 32, 256)
            for hh in range(2):
                pi = next_mm()
                mm_fm(sg, 128 * hh, 32, xb_rhs, R_xb, ps_mm[pi], R_pmm[pi])
                V("act", lambda e, hh=hh, pi=pi: e.activation(out=B[5 + hh][:], in_=ps_mm[pi][:], func=AF.Silu), reads=[R_pmm[pi]], writes=[R_B[5 + hh]])
            for hh in range(2):
                h = hp * 2 + hh
                recurrence(W[hh], R_W[hh], W[2 + hh], R_W[2 + hh], W[4 + hh], R_W[4 + hh], 128 * hh, 128, sth_d[h], R_sth[h], 1.0,
                           lambda s, hh=hh: (B[5 + hh][:], R_B[5 + hh]), PV["hg_g"], h)
        gemm_residual(od_w_out, 32, lambda k: ybuf[:, k, :], lambda: R_y)

    def even_mixer(t):
        sa = load_w(ev_w_in[:, 8192:8208], 32, 16)
        pi = next_mm()
        mm_fm(sa, 0, 32, xb_rhs, R_xb, ps_mm[pi], R_pmm[pi], m=16)
        V("act", lambda e, pi=pi: e.activation(out=alrT[:], in_=ps_mm[pi][0:16, :], func=AF.Copy), reads=[R_pmm[pi]], writes=[R_alr])
        Lre, Lim, Ire, Iim = s5L[:, 0, :], s5L[:, 1, :], s5L[:, 2, :], s5L[:, 3, :]
        cT, sT_ = s5k[:, 2, :], s5k[:, 3, :]
        V("dve", lambda e: e.tensor_tensor(out=Ire, in0=Lre, in1=cT, op=ALU.mult), reads=[R_s5L, R_s5k], writes=[R_init])
        V("dve", lambda e: e.tensor_tensor(out=t8[:], in0=Lim, in1=sT_, op=ALU.mult), reads=[R_s5L, R_s5k], writes=[R_s5k])
        V("dve", lambda e: e.tensor_tensor(out=Ire, in0=Ire, in1=t8[:], op=ALU.subtract), reads=[R_s5k], writes=[R_init])
        V("dve", lambda e: e.tensor_tensor(out=Iim, in0=Lre, in1=sT_, op=ALU.mult), reads=[R_s5L, R_s5k], writes=[R_init])
        V("dve", lambda e: e.tensor_tensor(out=t8[:], in0=Lim, in1=cT, op=ALU.mult), reads=[R_s5L, R_s5k], writes=[R_s5k])
        V("dve", lambda e: e.tensor_tensor(out=Iim, in0=Iim, in1=t8[:], op=ALU.add), reads=[R_s5k], writes=[R_init])
        for cp in range(8):
            su = load_w(ev_w_in[:, cp * 256:cp * 256 + 256], 32, 256)
            for jj in range(2):
                ch = cp * 2 + jj
                pi = next_mm()
                mm_fm(su, 128 * jj, 32, xb_rhs, R_xb, ps_mm[pi], R_pmm[pi])
                uf, R_uf = W[6], R_W[6]
                ub, R_ub = B[4], R_B[4]
                V("act", lambda e, pi=pi: e.activation(out=uf[:], in_=ps_mm[pi][:], func=AF.Copy), reads=[R_pmm[pi]], writes=[R_uf])
                V("dve", lambda e, pi=pi: e.tensor_copy(out=ub[:], in_=ps_mm[pi][:]), reads=[R_pmm[pi]], writes=[R_ub])
                P.dma("pool", lambda e, s, ch=ch: e.dma_start(out=s5w[:, 0:2, :], in_=s5bd_d[ch]).then_inc(s, 16), msem("bdld"), 1, writes=[R_s5wb])
                P.dma("sp", lambda e, s, ch=ch: e.dma_start(out=s5w[:, 2:5, :], in_=s5c_d[ch]).then_inc(s, 16), msem("cld"), 1,
                      reads=[R_s5c[ch]], writes=[R_s5wc])
                for s in range(4):
                    col = ch * 4 + s
                    sl = slice(128 * s, 128 * s + 128)
                    fp = s5k[:, 0, col:col + 1]
                    mg = s5k[:, 1, col:col + 1]
                    xa, R_xa = W[0], R_W[0]
                    xi = W[1][:].bitcast(I32)
                    sinT, R_sin = W[2], R_W[2]
                    cosT, R_cos = W[3], R_W[3]
                    V("dve", lambda e, fp=fp: e.tensor_scalar(out=xa[:], in0=iota, scalar1=fp, scalar2=None, op0=ALU.mult), reads=[R_cst, R_s5k], writes=[R_xa])
                    V("dve", lambda e: e.tensor_copy(out=xi, in_=xa[:]), reads=[R_xa], writes=[R_W[1]])
                    V("dve", lambda e: e.tensor_tensor(out=xa[:], in0=xa[:], in1=xi, op=ALU.subtract), reads=[R_W[1]], writes=[R_xa])
                    V("act", lambda e: e.activation(out=sinT[:], in_=xa[:], func=AF.Sin, scale=TWO_PI), reads=[R_xa], writes=[R_sin])
                    V("dve", lambda e, fp=fp: e.tensor_scalar(out=xa[:], in0=iota, scalar1=fp, scalar2=0.25, op0=ALU.mult, op1=ALU.add),
                      reads=[R_cst, R_s5k, R_sin], writes=[R_xa])
                    V("dve", lambda e: e.tensor_copy(out=xi, in_=xa[:]), reads=[R_xa], writes=[R_W[1]])
                    V("dve", lambda e: e.tensor_tensor(out=xa[:], in0=xa[:], in1=xi, op=ALU.subtract), reads=[R_W[1]], writes=[R_xa])
                    V("act", lambda e: e.activation(out=cosT[:], in_=xa[:], func=AF.Sin, scale=TWO_PI), reads=[R_xa], writes=[R_cos])
                    V("pe", lambda e, sl=sl: e.matmul(ps_o[0][:], lhsT=s5w[:, 0, sl], rhs=ub[:], start=True, stop=True), reads=[R_s5wb, R_ub], writes=[R_po[0]])
                    V("pe", lambda e, sl=sl: e.matmul(ps_o[1][:], lhsT=s5w[:, 1, sl], rhs=ub[:], start=True, stop=True), reads=[R_s5wb, R_ub], writes=[R_po[1]])
                    m1, R_m1 = W[0], R_W[0]
                    m2, R_m2 = W[1], R_W[1]
                    m4, R_m4 = W[4], R_W[4]
                    V("dve", lambda e: e.tensor_tensor(out=m1[:], in0=cosT[:], in1=ps_o[0][:], op=ALU.mult), reads=[R_cos, R_po[0]], writes=[R_m1])
                    V("dve", lambda e: e.tensor_tensor(out=m2[:], in0=sinT[:], in1=ps_o[1][:], op=ALU.mult), reads=[R_sin, R_po[1]], writes=[R_m2])
                    V("dve", lambda e: e.tensor_tensor(out=m1[:], in0=m1[:], in1=m2[:], op=ALU.add), reads=[R_m2], writes=[R_m1])
                    V("dve", lambda e: e.tensor_tensor(out=m2[:], in0=cosT[:], in1=ps_o[1][:], op=ALU.mult), reads=[R_cos, R_po[1], R_m1], writes=[R_m2])
                    V("dve", lambda e: e.tensor_tensor(out=m4[:], in0=sinT[:], in1=ps_o[0][:], op=ALU.mult), reads=[R_sin, R_po[0]], writes=[R_m4])
                    V("dve", lambda e: e.tensor_tensor(out=m2[:], in0=m2[:], in1=m4[:], op=ALU.subtract), reads=[R_m4], writes=[R_m2])
                    mgb = W[5]
                    V("dve", lambda e, mg=mg: e.tensor_scalar(out=mgb[:], in0=rmask[:], scalar1=0.0, scalar2=mg, op0=ALU.mult, op1=ALU.add),
                      reads=[R_c2, R_s5k], writes=[R_W[5]])
                    sre, R_sre = W[4], R_W[4]
                    sim, R_sim = W[7], R_W[7]
                    V("dve", lambda e, col=col: e.tensor_tensor_scan(out=sre[:], data0=mgb[:], data1=m1[:], initial=s5L[:, 2, col:col + 1], op0=ALU.mult, op1=ALU.add),
                      reads=[R_W[5], R_m1, R_init], writes=[R_sre])
                    V("dve", lambda e, col=col: e.tensor_tensor_scan(out=sim[:], data0=mgb[:], data1=m2[:], initial=s5L[:, 3, col:col + 1], op0=ALU.mult, op1=ALU.add),
                      reads=[R_W[5], R_m2, R_init], writes=[R_sim])
                    V("act", lambda e, col=col: e.activation(out=s5L[:, 0, col:col + 1], in_=sre[:, TT - 1:TT], func=AF.Copy), reads=[R_sre], writes=[R_s5L])
                    V("act", lambda e, col=col: e.activation(out=s5L[:, 1, col:col + 1], in_=sim[:, TT - 1:TT], func=AF.Copy), reads=[R_sim], writes=[R_s5L])
                    V("dve", lambda e: e.tensor_tensor(out=B[0][:], in0=cosT[:], in1=sre[:], op=ALU.mult), reads=[R_cos, R_sre], writes=[R_B[0]])
                    V("dve", lambda e: e.tensor_tensor(out=B[1][:], in0=sinT[:], in1=sim[:], op=ALU.mult), reads=[R_sin, R_sim], writes=[R_B[1]])
                    V("dve", lambda e: e.tensor_tensor(out=B[2][:], in0=sinT[:], in1=sre[:], op=ALU.mult), reads=[R_sin, R_sre], writes=[R_B[2]])
                    V("dve", lambda e: e.tensor_tensor(out=B[3][:], in0=cosT[:], in1=sim[:], op=ALU.mult), reads=[R_cos, R_sim], writes=[R_B[3]])

                    def ymm(e, s=s, sl=sl):
                        e.matmul(ps_kv[:], lhsT=s5w[:, 2, sl], rhs=B[0][:], start=(s == 0), stop=False)
                        e.matmul(ps_kv[:], lhsT=s5w[:, 3, sl], rhs=B[1][:], start=False, stop=False)
                        e.matmul(ps_kv[:], lhsT=s5w[:, 4, sl], rhs=B[2][:], start=False, stop=False)
                        return e.matmul(ps_kv[:], lhsT=s5w[:, 4, sl], rhs=B[3][:], start=False, stop=(s == 3))
                    V("pe", ymm, reads=[R_s5wc, R_B[0], R_B[1], R_B[2], R_B[3]], writes=[R_pkv])
                yv, R_yv = W[0], R_W[0]
                tq, R_tq = W[1], R_W[1]
                dcol = PV["s5_d"] + ch
                V("dve", lambda e, dcol=dcol: e.scalar_tensor_tensor(out=yv[:], in0=uf[:], scalar=pvec[:, dcol:dcol + 1], in1=ps_kv[:], op0=ALU.mult, op1=ALU.add),
                  reads=[R_uf, R_pkv, R_pv], writes=[R_yv])
                V("act", lambda e: e.activation(out=tq[:], in_=yv[:], func=AF.Square), reads=[R_yv], writes=[R_tq])
                V("dve", lambda e: e.tensor_scalar(out=tq[:], in0=tq[:], scalar1=0.044715, scalar2=1.0, op0=ALU.mult, op1=ALU.add), reads=[R_tq], writes=[R_tq])
                V("dve", lambda e: e.tensor_tensor(out=tq[:], in0=tq[:], in1=yv[:], op=ALU.mult), reads=[R_yv], writes=[R_tq])
                V("act", lambda e: e.activation(out=tq[:], in_=tq[:], func=AF.Sigmoid, scale=2.0 * math.sqrt(2.0 / math.pi)), reads=[R_tq], writes=[R_tq])
                V("dve", lambda e, ch=ch: e.tensor_tensor(out=ybuf[:, 16 + ch, :], in0=tq[:], in1=yv[:], op=ALU.mult), reads=[R_tq, R_yv], writes=[R_y[16 + ch]])
        for mp in range(8):
            sl_ = load_w(w_glu[:, mp * 256:(mp + 1) * 256], 16, 256)
            for j in range(2):
                m = mp * 2 + j
                pi = next_mm()
                mm_fm(sl_, 128 * j, 16, lambda k: ybuf[:, 16 + k, :], R_y[16:32], ps_mm[pi], R_pmm[pi])
                V("act", lambda e, pi=pi: e.activation(out=W[7][:], in_=ps_mm[pi][:], func=AF.Sigmoid), reads=[R_pmm[pi]], writes=[R_W[7]])
                V("dve", lambda e, m=m: e.tensor_tensor(out=ybuf[:, m, :], in0=W[7][:], in1=ybuf[:, 16 + m, :], op=ALU.mult),
                  reads=[R_W[7], R_y[16 + m]], writes=[R_y[m]])
        for h in range(8):
            sqk = load_w(None, 32, 256, parts=[(ev_w_in[:, 2048 + 128 * h:2048 + 128 * h + 128], 0, 128),
                                               (ev_w_in[:, 3072 + 128 * h:3072 + 128 * h + 128], 128, 128)])
            for hh in range(2):
                pi = next_mm()
                mm_fm(sqk, 128 * hh, 32, xb_rhs, R_xb, ps_mm[pi], R_pmm[pi])
                V("act", lambda e, hh=hh, pi=pi: e.activation(out=W[hh][:], in_=ps_mm[pi][:], func=AF.Copy), reads=[R_pmm[pi]], writes=[R_W[hh]])
            V("pe", lambda e, h=h: e.matmul(ps_sc[:], lhsT=walpha[:, 128 * h:128 * h + 128], rhs=alrT[:], start=True, stop=True),
              reads=[R_wal, R_alr], writes=[R_psc])
            la, R_la = W[4], R_W[4]
            V("dve", lambda e, h=h: e.tensor_scalar(out=la[:], in0=ps_sc[:], scalar1=pvec[:, PV["gla_b"] + h:PV["gla_b"] + h + 1], scalar2=None, op0=ALU.add),
              reads=[R_psc, R_pv], writes=[R_la])
            V("act", lambda e: e.activation(out=la[:], in_=la[:], func=AF.Exp, scale=-1.0), reads=[R_la], writes=[R_la])
            V("act", lambda e: e.activation(out=la[:], in_=la[:], func=AF.Ln, bias=1.0), reads=[R_la], writes=[R_la])
            V("dve", lambda e: e.tensor_scalar(out=la[:], in0=la[:], scalar1=-1.0 / 16.0, scalar2=None, op0=ALU.mult), reads=[R_la], writes=[R_la])
            sv = load_w(ev_w_in[:, 4096 + 256 * h:4096 + 256 * h + 256], 32, 256)
            v_token_major(sv, AF.Copy)
            sg = load_w(ev_w_in[:, 6144 + 256 * h:6144 + 256 * h + 256], 32, 256)
            for hh in range(2):
                pi = next_mm()
                mm_fm(sg, 128 * hh, 32, xb_rhs, R_xb, ps_mm[pi], R_pmm[pi])
                V("act", lambda e, hh=hh, pi=pi: e.activation(out=B[5 + hh][:], in_=ps_mm[pi][:], func=AF.Silu), reads=[R_pmm[pi]], writes=[R_B[5 + hh]])
            recurrence(W[0], R_W[0], W[1], R_W[1], la, R_la, 0, 256, stg_d[h], R_stg[h], 128.0 ** -0.5,
                       lambda s: (B[5 + s][:], R_B[5 + s]), PV["gla_g"], 16 + 2 * h)
        if stop_after == 10:
            for c in range(32):
                V("act", lambda e, c=c: e.activation(out=xres[:, c, :], in_=ybuf[:, c, :], func=AF.Copy), reads=[R_y[c]], writes=[R_x[c]])
            return
        gemm_residual(ev_w_out, 32, lambda k: ybuf[:, k, :], lambda: R_y)

    finals = []
    for t in range(NT):
        load_x_tile(t)
        if stop_after == -1:
            finals += store_tile(t, out_d, scale=1.0 / ALPHA)
            continue
        even_mixer(t)
        if stop_after == 10:
            finals += store_tile(t, out_d)
            continue
        layer_norm("mix_g0", "mix_b0", final=(stop_after == 1))
        if stop_after == 1:
            finals += store_tile(t, out_d)
            continue
        ffn(0)
        layer_norm("ffn_g0", "ffn_b0", final=(stop_after == 2))
        if stop_after == 2:
            finals += store_tile(t, out_d)
            continue
        odd_mixer()
        layer_norm("mix_g1", "mix_b1", final=(stop_after == 3))
        if stop_after == 3:
            finals += store_tile(t, out_d)
            continue
        ffn(1)
        layer_norm("ffn_g1", "ffn_b1", final=True)
        finals += store_tile(t, out_d)
    P.emit(final_waits=finals)
    P.close()
    return nc, P


def _host_layouts(inp):
    f = lambda a: np.ascontiguousarray(a, dtype=np.float32)
    cols = []
    fm = lambda v, n: f(v.reshape(n, 128).T)
    for l in range(2):
        cols += [fm(inp["ln_mix_g"][l], 32), fm(inp["ln_mix_b"][l], 32), fm(inp["ln_ffn_g"][l], 32), fm(inp["ln_ffn_b"][l], 32)]
    cols += [fm(inp["hg_lb_table"][0], 32), fm(inp["hg_lb_table"][1], 32)]
    cols += [fm(inp["gla_b_alpha"][0], 8), fm(inp["gla_norm_g"][0], 2), fm(inp["hg_norm_g"][0], 1), fm(inp["s5_d"][0], 16)]
    def lamT(a):
        a = a.reshape(16, 8, 4, 16)
        return f(a.transpose(1, 3, 0, 2).reshape(128, 64))
    cols += [lamT(inp["s5_lam_re"][0]), lamT(inp["s5_lam_im"][0])]
    ld = inp["s5_log_dt"][0].reshape(16, 8)
    cols += [f(np.repeat(ld.T[:, None, :], 16, axis=1).reshape(128, 16))]
    pvec = f(np.concatenate(cols, axis=1))
    assert pvec.shape == (128, NPV)
    cst = np.zeros((128, 768), np.float32)
    cst[:, 0:128] = np.eye(128, dtype=np.float32)
    m = np.arange(128)[:, None]
    l = np.arange(128)[None, :]
    cst[:, 128:256] = ((m // 64 == l // 64) & (l >= m)).astype(np.float32)
    cst[:, 256:768] = np.arange(512, dtype=np.float32)[None, :]
    b_re = inp["s5_b_re"][0].reshape(16, 8, 4, 16, 16)
    b_im = inp["s5_b_im"][0].reshape(16, 8, 4, 16, 16)
    c_re = inp["s5_c_re"][0].reshape(16, 8, 16, 4, 16)
    c_im = inp["s5_c_im"][0].reshape(16, 8, 16, 4, 16)
    bd = np.zeros((16, 8, 16, 2, 4, 8, 16), np.float32)
    cd = np.zeros((16, 8, 16, 2, 4, 8, 16), np.float32)
    for g in range(8):
        bd[:, g, :, 0, :, g, :] = b_re[:, g].transpose(0, 3, 1, 2)
        bd[:, g, :, 1, :, g, :] = b_im[:, g].transpose(0, 3, 1, 2)
        cd[:, g, :, 0, :, g, :] = c_re[:, g].transpose(0, 3, 2, 1)
        cd[:, g, :, 1, :, g, :] = c_im[:, g].transpose(0, 3, 2, 1)
    bd = bd.reshape(16, 128, 2, 512)
    cd = cd.reshape(16, 128, 2, 512)
    return pvec, cst, bd, cd


_CACHE = {}


def kernel(**inputs):
    inp = {k: np.asarray(v) for k, v in inputs.items()}
    pvec, cst, bd, cd = _host_layouts(inp)
    if "nc" not in _CACHE:
        _CACHE["nc"] = build(4)[0]
    nc = _CACHE["nc"]
    f = lambda a: np.ascontiguousarray(a, dtype=np.float32)
    shared = {
        "ev_w_in": f(inp["ev_w_in"][0]), "ev_w_out": f(inp["ev_w_out"][0]), "s5_w_glu": f(inp["s5_w_glu"][0]),
        "od_w_in": f(inp["od_w_in"][0]), "od_w_out": f(inp["od_w_out"][0]),
        "ffn_w_gate": f(inp["ffn_w_gate"]), "ffn_w_up": f(inp["ffn_w_up"]), "ffn_w_down": f(inp["ffn_w_down"]),
        "gla_w_alpha": f(inp["gla_w_alpha"][0]), "pvec": pvec, "cst": cst, "s5bd": bd, "s5cd": cd,
    }
    in_maps = [dict(shared, x=f(inp["x"][b])) for b in range(8)]
    res = run_bass_kernel_spmd(nc, in_maps, core_ids=list(range(8)))
    return np.stack([r["out"] for r in res.results], axis=0).astype(np.float32)
```

```python
import math
import os
import numpy as np
_KD = os.environ.get('KDBG', '')
import concourse.bass as bass
import concourse.mybir as mybir
from concourse.bass_utils import run_bass_kernel_spmd

F32 = mybir.dt.float32
BF16 = mybir.dt.bfloat16
I32 = mybir.dt.int32
AF = mybir.ActivationFunctionType
ALU = mybir.AluOpType

ENGS = ["pe", "act", "dve", "pool", "sp"]
ALPHA = 4.0 ** 0.25
EPS = 1e-5
TT = 512
TWO_PI = 2.0 * math.pi * (1.0 - 1e-6)


class Res:
    __slots__ = ("name", "writer", "readers", "excl")

    def __init__(self, name, excl=False):
        self.name = name
        self.writer = None
        self.readers = []
        self.excl = excl


class DmaSem:
    __slots__ = ("sem", "count", "key")

    def __init__(self, sem, key):
        self.sem = sem
        self.count = 0
        self.key = key


class Op:
    __slots__ = ("eng", "fn", "deps", "needs_inc", "sem", "val", "is_dma", "key")


class Prog:
    def __init__(self, nc, same_engine_sync=("act", "dve", "pool")):
        self.nc = nc
        self.ops = {e: [] for e in ENGS}
        self.same = set(same_engine_sync)
        self.esem = {}
        self._stack = []

    def new_sem(self, name):
        cm = self.nc.semaphore(name)
        s = cm.__enter__()
        self._stack.append(cm)
        return s

    def dma_sem(self, name):
        return DmaSem(self.new_sem(name), ("d", name))

    def sbuf(self, name, shape, dt):
        cm = self.nc.sbuf_tensor("sb_" + name, shape, dt)
        t = cm.__enter__()
        self._stack.append(cm)
        return t

    def psum(self, name, shape, dt):
        cm = self.nc.psum_tensor("pt_" + name, shape, dt)
        t = cm.__enter__()
        self._stack.append(cm)
        return t

    def _track(self, op, reads, writes):
        ex = [r for r in reads if r.excl and r not in writes]
        if ex:
            reads = [r for r in reads if not r.excl]
            writes = list(writes) + ex
        deps = []
        for r in list(reads) + list(writes):
            if r.writer is not None:
                deps.append(r.writer)
        for w in writes:
            deps.extend(w.readers)
        for r in reads:
            r.readers.append(op)
        for w in writes:
            w.writer = op
            w.readers = []
        out = []
        seen = set()
        for d in deps:
            if d is op or id(d) in seen:
                continue
            seen.add(id(d))
            if d.eng == op.eng and not d.is_dma and op.eng not in self.same:
                continue
            out.append(d)
            d.needs_inc = True
        op.deps = out

    def op(self, eng, fn, reads=(), writes=()):
        o = Op()
        o.eng = eng
        o.fn = fn
        o.needs_inc = False
        o.is_dma = False
        o.sem = None
        o.val = None
        o.key = ("e", eng)
        self._track(o, reads, writes)
        self.ops[eng].append(o)
        return o

    def dma(self, eng, fn, dsem, n, reads=(), writes=()):
        o = Op()
        o.eng = eng
        o.fn = fn
        o.needs_inc = False
        o.is_dma = True
        dsem.count += 16 * n
        o.sem = dsem.sem
        o.val = dsem.count
        o.key = dsem.key
        self._track(o, reads, writes)
        self.ops[eng].append(o)
        return o

    def emit(self, final_waits=()):
        nc = self.nc
        for e in ENGS:
            self.esem[e] = self.new_sem("es_" + e)
        for e in ENGS:
            c = 0
            for o in self.ops[e]:
                if o.is_dma:
                    continue
                if o.needs_inc:
                    c += 1
                    o.val = c
                    o.sem = self.esem[e]
        bname = {"pe": "tensor", "act": "scalar", "dve": "vector", "pool": "gpsimd", "sp": "sync"}
        self.stats = {}
        with nc.Block() as block:
            for e in ENGS:
                def body(eng, e=e):
                    waited = {}
                    nw = 0
                    for o in self.ops[e]:
                        for d in o.deps:
                            if waited.get(d.key, 0) < d.val:
                                eng.wait_ge(d.sem, d.val)
                                waited[d.key] = d.val
                                nw += 1
                        if o.is_dma:
                            o.fn(eng, o.sem)
                        else:
                            ins = o.fn(eng)
                            if o.needs_inc:
                                ins.then_inc(o.sem, 1)
                    if e == "sp":
                        for d in final_waits:
                            eng.wait_ge(d.sem, d.val)
                    self.stats[e] = (len(self.ops[e]), nw)
                getattr(block, bname[e])(body)

    def close(self):
        for cm in reversed(self._stack):
            cm.__exit__(None, None, None)
        self._stack = []


PV = {}
_off = 0
for _n, _w in [("mix_g0", 32), ("mix_b0", 32), ("ffn_g0", 32), ("ffn_b0", 32),
               ("mix_g1", 32), ("mix_b1", 32), ("ffn_g1", 32), ("ffn_b1", 32),
               ("lbt0", 32), ("lbt1", 32), ("gla_b", 8), ("gla_g", 2), ("hg_g", 1), ("s5_d", 16),
               ("lam_re", 64), ("lam_im", 64), ("logdt", 16)]:
    PV[_n] = _off
    _off += _w
NPV = _off
NLN = 256


def build(NT=4, stop_after=None):
    nc = bass.Bass("TRN2", target_bir_lowering=False)
    P = Prog(nc)
    NTOK = NT * TT

    def din(name, shape, dt=F32):
        if 'nowt' in _KD and (name.startswith("ev_w") or name.startswith("od_w") or name.startswith("ffn_w") or name == "s5_w_glu"):
            return None
        return nc.dram_tensor(name, shape, dt, kind="ExternalInput").ap()

    x_d = din("x", [2048, 4096])
    ev_w_in = din("ev_w_in", [4096, 8208])
    ev_w_out = din("ev_w_out", [4096, 4096])
    w_glu = din("s5_w_glu", [2048, 2048])
    od_w_in = din("od_w_in", [4096, 16384])
    od_w_out = din("od_w_out", [4096, 4096])
    w_gate = din("ffn_w_gate", [2, 4096, 11008])
    w_up = din("ffn_w_up", [2, 4096, 11008])
    w_down = din("ffn_w_down", [2, 11008, 4096])
    w_alpha_d = din("gla_w_alpha", [16, 1024])
    pvec_d = din("pvec", [128, NPV])
    cst_d = din("cst", [128, 768])
    s5bd_d = din("s5bd", [16, 128, 2, 512])
    s5cd_d = din("s5cd", [16, 128, 2, 512])
    out_d = nc.dram_tensor("out", [NTOK, 4096], F32, kind="ExternalOutput").ap()
    s5c_d = nc.dram_tensor("s5c_scr", [16, 128, 3, 512], BF16, kind="Internal").ap()
    stg_d = nc.dram_tensor("st_gla", [8, 128, 256], F32, kind="Internal").ap()
    sth_d = nc.dram_tensor("st_hg", [32, 128, 128], F32, kind="Internal").ap()
    R_s5c = [Res(f"s5c{c}") for c in range(16)]
    R_stg = [Res(f"stg{h}") for h in range(8)]
    R_sth = [Res(f"sth{h}") for h in range(32)]

    xres = P.sbuf("xres", [128, 32, TT], F32)
    R_x = [Res(f"x{c}") for c in range(32)]
    xbf = P.sbuf("xbf", [128, 32, TT], BF16)
    R_xb = [Res(f"xb{c}") for c in range(32)]
    ybuf = P.sbuf("ybuf", [128, 32, TT], BF16)
    R_y = [Res(f"y{c}") for c in range(32)]
    NSLOT = 2
    wsl = [P.sbuf(f"wsl{i}", [128, 32, 256], BF16) for i in range(NSLOT)]
    R_w = [Res(f"w{i}") for i in range(NSLOT)]
    S_w = [P.dma_sem(f"wsem{i}") for i in range(NSLOT)]
    wctr = [0]

    pvec = P.sbuf("pvec", [128, NPV], F32)
    R_pv = Res("pvec")
    cst = P.sbuf("cst", [128, 768], F32)
    R_cst = Res("cst")
    ident_f = cst[:, 0:128]
    maskT = cst[:, 128:256]
    iota = cst[:, 256:768]
    ident_b = P.sbuf("identb", [128, 128], BF16)
    ones_f = P.sbuf("onesf", [128, 128], F32)
    ones_b = P.sbuf("onesb", [128, 128], BF16)
    rmask = P.sbuf("rmask", [128, TT], BF16)
    lbt = P.sbuf("lbt", [128, 3, 32], F32)
    s5k = P.sbuf("s5k", [128, 8, 64], F32)
    s5L = P.sbuf("s5L", [128, 4, 64], F32)
    R_s5k = Res("s5k")
    R_s5L = Res("s5L")
    R_init = Res("s5init")
    walpha = P.sbuf("walpha", [16, 1024], BF16)
    R_wal = Res("walpha")
    alrT = P.sbuf("alrT", [16, TT], BF16)
    R_alr = Res("alrT")
    NWB = 8
    W = [P.sbuf(f"W{i}", [128, TT], F32) for i in range(NWB)]
    R_W = [Res(f"W{i}") for i in range(NWB)]
    NBB = 7
    B = [P.sbuf(f"B{i}", [128, TT], BF16) for i in range(NBB)]
    R_B = [Res(f"B{i}") for i in range(NBB)]
    s5w = P.sbuf("s5w", [128, 5, 512], BF16)
    R_s5wb = Res("s5wb")
    R_s5wc = Res("s5wc")
    vtm = P.sbuf("vtm", [128, 4, 256], BF16)
    R_vtm = Res("vtm")
    ketm = P.sbuf("ketm", [128, 4, 128], BF16)
    R_ketm = Res("ketm")
    Sst = P.sbuf("Sst", [128, 256], F32)
    R_S = Res("Sst")
    Sbf2 = [P.sbuf("Sbf", [128, 256], BF16), P.sbuf("Sbfb", [128, 256], BF16)]
    R_Sb2 = [Res("Sbf"), Res("Sbfb")]
    smallc = P.sbuf("smallc", [128, 32], F32)
    R_sm = Res("smallc")

    ps_mm = [P.psum(f"psmm{i}", [128, 512], F32) for i in range(3)]
    R_pmm = [Res(f"psmm{i}", True) for i in range(3)]
    mmctr = [0]
    ps_o = [P.psum(f"pso{i}", [128, 512], F32) for i in range(2)]
    R_po = [Res(f"pso{i}", True) for i in range(2)]
    ps_sc = P.psum("pssc", [128, 512], F32)
    R_psc = Res("pssc", True)
    ps_kv = P.psum("pskv", [128, 512], F32)
    R_pkv = Res("pskv", True)
    ps_tr = P.psum("pstr", [128, 512], F32)
    R_ptr = Res("pstr", True)

    S_misc = {}

    def msem(name):
        if name not in S_misc:
            S_misc[name] = P.dma_sem(name)
        return S_misc[name]

    def V(eng, fn, reads=(), writes=()):
        return P.op(eng, fn, reads, writes)

    def load_w(src, kc, ncols, parts=None):
        i = wctr[0] % NSLOT
        wctr[0] += 1
        if parts is None:
            parts = [(src, 0, ncols)]
        n = len(parts)

        def fn(e, s, i=i, parts=parts, kc=kc):
            for (sv, c0, ncl) in parts:
                e.dma_start(out=wsl[i][:, 0:kc, c0:c0 + ncl],
                            in_=sv.rearrange("(kc p) n -> p kc n", p=128)).then_inc(s, 16)
        P.dma("pool", fn, S_w[i], n, writes=[R_w[i]])
        return i

    def next_mm():
        i = mmctr[0] % 3
        mmctr[0] += 1
        return i

    def mm_fm(slot, col0, kc, rhs, rhs_res, ps, ps_res, m=128):
        def fn(e):
            ins = None
            for k in range(kc):
                ins = e.matmul(ps[0:m, :], lhsT=wsl[slot][:, k, col0:col0 + m], rhs=rhs(k),
                               start=(k == 0), stop=(k == kc - 1))
            return ins
        V("pe", fn, reads=[R_w[slot]] + list(rhs_res), writes=[ps_res])

    xb_rhs = lambda k: xbf[:, k, :]

    P.dma("sp", lambda e, s: e.dma_start(out=pvec[:], in_=pvec_d).then_inc(s, 16), msem("pv"), 1, writes=[R_pv])
    P.dma("sp", lambda e, s: e.dma_start(out=cst[:], in_=cst_d).then_inc(s, 16), msem("cst"), 1, writes=[R_cst])
    if 'nowal' not in _KD:
        P.dma("pool", lambda e, s: e.dma_start(out=walpha[:], in_=w_alpha_d).then_inc(s, 16), msem("wal"), 1, writes=[R_wal])
    R_c2 = Res("consts2")
    V("dve", lambda e: e.tensor_copy(out=ident_b[:], in_=ident_f), reads=[R_cst], writes=[R_c2])
    V("dve", lambda e: e.memset(ones_f[:], 1.0), writes=[R_c2])
    V("dve", lambda e: e.memset(ones_b[:], 1.0), writes=[R_c2])
    V("dve", lambda e: e.memset(rmask[:], 1.0), writes=[R_c2])
    V("dve", lambda e: e.memset(rmask[:].rearrange("p (c l) -> p c l", l=64)[:, :, 0:1], 0.0), writes=[R_c2])
    V("dve", lambda e: e.tensor_tensor(out=lbt[:, 0, :], in0=pvec[:, PV["lbt1"]:PV["lbt1"] + 32],
                                       in1=pvec[:, PV["lbt0"]:PV["lbt0"] + 32], op=ALU.subtract), reads=[R_pv], writes=[R_c2])
    V("act", lambda e: e.activation(out=lbt[:, 0, :], in_=lbt[:, 0, :], func=AF.Sigmoid), reads=[R_c2], writes=[R_c2])
    V("dve", lambda e: e.tensor_scalar(out=lbt[:, 1, :], in0=lbt[:, 0, :], scalar1=-1.0, scalar2=1.0, op0=ALU.mult, op1=ALU.add),
      reads=[R_c2], writes=[R_c2])
    V("dve", lambda e: e.tensor_scalar(out=lbt[:, 2, :], in0=lbt[:, 1, :], scalar1=-1.0, scalar2=None, op0=ALU.mult),
      reads=[R_c2], writes=[R_c2])

    lre = pvec[:, PV["lam_re"]:PV["lam_re"] + 64]
    lim = pvec[:, PV["lam_im"]:PV["lam_im"] + 64]
    K = lambda i: s5k[:, i, :]
    t6, t7 = K(6), K(7)
    xi64 = P.sbuf("xi64", [128, 64], I32)
    t8 = P.sbuf("t8", [128, 64], F32)
    t9 = P.sbuf("t9", [128, 64], F32)

    def frac_sin(out, xin, eng_reads):
        V("dve", lambda e: e.tensor_copy(out=xi64[:], in_=xin), reads=eng_reads, writes=[R_s5k])
        V("dve", lambda e: e.tensor_tensor(out=t9[:], in0=xin, in1=xi64[:], op=ALU.subtract), reads=[R_s5k], writes=[R_s5k])
        V("act", lambda e: e.activation(out=out, in_=t9[:], func=AF.Sin, scale=TWO_PI), reads=[R_s5k], writes=[R_s5k])

    if 'nos5k' not in _KD:
        V("act", lambda e: e.activation(out=t8[:, 0:16], in_=pvec[:, PV["logdt"]:PV["logdt"] + 16], func=AF.Exp), reads=[R_pv], writes=[R_s5k])
        V("dve", lambda e: e.tensor_copy(out=t7.rearrange("p (c s) -> p c s", s=4),
                                         in_=t8[:, 0:16].unsqueeze(2).to_broadcast([128, 16, 4])), reads=[R_s5k], writes=[R_s5k])
        V("dve", lambda e: e.tensor_tensor(out=t6, in0=lre, in1=t7, op=ALU.mult), reads=[R_s5k, R_pv], writes=[R_s5k])
        V("act", lambda e: e.activation(out=K(1), in_=t6, func=AF.Exp), reads=[R_s5k], writes=[R_s5k])
        V("dve", lambda e: e.tensor_tensor(out=t6, in0=lim, in1=t7, op=ALU.mult), reads=[R_s5k, R_pv], writes=[R_s5k])
        V("dve", lambda e: e.tensor_scalar(out=t6, in0=t6, scalar1=1.0 / (2.0 * math.pi), scalar2=None, op0=ALU.mult), reads=[R_s5k], writes=[R_s5k])
        V("dve", lambda e: e.tensor_scalar(out=K(0), in0=t6, scalar1=1.0, scalar2=None, op0=ALU.add), reads=[R_s5k], writes=[R_s5k])
        frac_sin(K(3), t6, [R_s5k])
        V("dve", lambda e: e.tensor_scalar(out=t8[:], in0=t6, scalar1=0.25, scalar2=None, op0=ALU.add), reads=[R_s5k], writes=[R_s5k])
        frac_sin(K(2), t8[:], [R_s5k])
        V("dve", lambda e: e.tensor_tensor(out=K(2), in0=K(2), in1=K(1), op=ALU.mult), reads=[R_s5k], writes=[R_s5k])
        V("dve", lambda e: e.tensor_scalar(out=K(2), in0=K(2), scalar1=-1.0, scalar2=None, op0=ALU.add), reads=[R_s5k], writes=[R_s5k])
        V("dve", lambda e: e.tensor_tensor(out=K(3), in0=K(3), in1=K(1), op=ALU.mult), reads=[R_s5k], writes=[R_s5k])
        V("dve", lambda e: e.tensor_tensor(out=t7, in0=lre, in1=lre, op=ALU.mult), reads=[R_s5k, R_pv], writes=[R_s5k])
        V("dve", lambda e: e.tensor_tensor(out=t8[:], in0=lim, in1=lim, op=ALU.mult), reads=[R_s5k, R_pv], writes=[R_s5k])
        V("dve", lambda e: e.tensor_tensor(out=t7, in0=t7, in1=t8[:], op=ALU.add), reads=[R_s5k], writes=[R_s5k])
        V("dve", lambda e: e.reciprocal(out=t7, in_=t7), reads=[R_s5k], writes=[R_s5k])
        V("dve", lambda e: e.tensor_tensor(out=K(4), in0=K(2), in1=lre, op=ALU.mult), reads=[R_s5k, R_pv], writes=[R_s5k])
        V("dve", lambda e: e.tensor_tensor(out=t8[:], in0=K(3), in1=lim, op=ALU.mult), reads=[R_s5k, R_pv], writes=[R_s5k])
        V("dve", lambda e: e.tensor_tensor(out=K(4), in0=K(4), in1=t8[:], op=ALU.add), reads=[R_s5k], writes=[R_s5k])
        V("dve", lambda e: e.tensor_tensor(out=K(4), in0=K(4), in1=t7, op=ALU.mult), reads=[R_s5k], writes=[R_s5k])
        V("dve", lambda e: e.tensor_tensor(out=K(5), in0=K(3), in1=lre, op=ALU.mult), reads=[R_s5k, R_pv], writes=[R_s5k])
        V("dve", lambda e: e.tensor_tensor(out=t8[:], in0=K(2), in1=lim, op=ALU.mult), reads=[R_s5k, R_pv], writes=[R_s5k])
        V("dve", lambda e: e.tensor_tensor(out=K(5), in0=K(5), in1=t8[:], op=ALU.subtract), reads=[R_s5k], writes=[R_s5k])
        V("dve", lambda e: e.tensor_tensor(out=K(5), in0=K(5), in1=t7, op=ALU.mult), reads=[R_s5k], writes=[R_s5k])
        V("dve", lambda e: e.tensor_scalar(out=t6, in0=K(0), scalar1=float(TT), scalar2=None, op0=ALU.mult), reads=[R_s5k], writes=[R_s5k])
        frac_sin(K(3), t6, [R_s5k])
        V("dve", lambda e: e.tensor_scalar(out=t8[:], in0=t6, scalar1=0.25, scalar2=None, op0=ALU.add), reads=[R_s5k], writes=[R_s5k])
        frac_sin(K(2), t8[:], [R_s5k])
    V("dve", lambda e: e.memset(s5L[:], 0.0), writes=[R_s5L, R_init])

    for ch in range(0 if 'nocd' in _KD else 16):
        cdr, cdi = W[0], W[1]
        P.dma("sp", lambda e, s, ch=ch: (e.dma_start(out=W[0][:], in_=s5cd_d[ch, :, 0, :]).then_inc(s, 16),
                                        e.dma_start(out=W[1][:], in_=s5cd_d[ch, :, 1, :]).then_inc(s, 16)),
              msem("cdld"), 2, writes=[R_W[0], R_W[1]])
        for s in range(4):
            col = ch * 4 + s
            sl = slice(128 * s, 128 * s + 128)
            fr = s5k[:, 4, col:col + 1]
            fi = s5k[:, 5, col:col + 1]
            V("dve", lambda e, sl=sl, fi=fi: e.tensor_scalar(out=W[2][:, sl], in0=W[1][:, sl], scalar1=fi, scalar2=None, op0=ALU.mult),
              reads=[R_W[1], R_s5k], writes=[R_W[2]])
            V("dve", lambda e, sl=sl, fr=fr: e.scalar_tensor_tensor(out=W[3][:, sl], in0=W[0][:, sl], scalar=fr, in1=W[2][:, sl],
                                                                   op0=ALU.mult, op1=ALU.subtract),
              reads=[R_W[0], R_W[2], R_s5k], writes=[R_W[3]])
            V("dve", lambda e, sl=sl, fi=fi: e.tensor_scalar(out=W[2][:, sl], in0=W[0][:, sl], scalar1=fi, scalar2=None, op0=ALU.mult),
              reads=[R_W[0], R_s5k], writes=[R_W[2]])
            V("dve", lambda e, sl=sl, fr=fr: e.scalar_tensor_tensor(out=W[4][:, sl], in0=W[1][:, sl], scalar=fr, in1=W[2][:, sl],
                                                                   op0=ALU.mult, op1=ALU.add),
              reads=[R_W[1], R_W[2], R_s5k], writes=[R_W[4]])
        V("act", lambda e: e.activation(out=s5w[:, 2, :], in_=W[3][:], func=AF.Copy), reads=[R_W[3]], writes=[R_s5wc])
        V("act", lambda e: e.activation(out=s5w[:, 3, :], in_=W[3][:], func=AF.Copy, scale=-1.0), reads=[R_W[3]], writes=[R_s5wc])
        V("act", lambda e: e.activation(out=s5w[:, 4, :], in_=W[4][:], func=AF.Copy, scale=-1.0), reads=[R_W[4]], writes=[R_s5wc])
        P.dma("sp", lambda e, s, ch=ch: e.dma_start(out=s5c_d[ch], in_=s5w[:, 2:5, :]).then_inc(s, 16), msem("cdst"), 1,
              reads=[R_s5wc], writes=[R_s5c[ch]])
    V("dve", lambda e: e.memset(Sst[:], 0.0), writes=[R_S])
    for h in range(0 if 'nostz' in _KD else 8):
        P.dma("sp", lambda e, s, h=h: e.dma_start(out=stg_d[h], in_=Sst[:]).then_inc(s, 16), msem("stz"), 1, reads=[R_S], writes=[R_stg[h]])
    for h in range(0 if 'nostz' in _KD else 32):
        P.dma("sp", lambda e, s, h=h: e.dma_start(out=sth_d[h], in_=Sst[:, 0:128]).then_inc(s, 16), msem("stz"), 1, reads=[R_S], writes=[R_sth[h]])

    def load_x_tile(t):
        k = 0
        for tb in range(int(os.environ.get('LOOPT', '4'))):
            for cb in range(int(os.environ.get('LOOPN', '8'))):
                wi = k % 2
                k += 1
                r0 = t * TT + tb * 128
                P.dma("sp", lambda e, s, wi=wi, r0=r0, cb=cb: e.dma_start(out=W[wi][:], in_=x_d[r0:r0 + 128, cb * 512:(cb + 1) * 512]).then_inc(s, 16),
                      msem(f"xin{wi}"), 1, writes=[R_W[wi]])

                def tr(e, wi=wi):
                    ins = None
                    for j in range(4):
                        ins = e.transpose(out=ps_tr[:, 128 * j:128 * j + 128], in_=W[wi][:, 128 * j:128 * j + 128], identity=ident_f)
                    return ins
                V("pe", tr, reads=[R_W[wi], R_cst], writes=[R_ptr])
                dst = xres[:, cb * 4:cb * 4 + 4, tb * 128:tb * 128 + 128]
                dstb = xbf[:, cb * 4:cb * 4 + 4, tb * 128:tb * 128 + 128]
                src = ps_tr[:].rearrange("p (j l) -> p j l", l=128)
                V("act", lambda e, dst=dst, src=src: e.activation(out=dst, in_=src, func=AF.Copy, scale=ALPHA),
                  reads=[R_ptr], writes=R_x[cb * 4:cb * 4 + 4])
                V("dve", lambda e, dstb=dstb, src=src: e.tensor_copy(out=dstb, in_=src), reads=[R_ptr], writes=R_xb[cb * 4:cb * 4 + 4])

    def store_tile(t, dram, scale=1.0):
        k = 0
        last = []
        for tb in range(int(os.environ.get('LOOPT', '4'))):
            for cb in range(int(os.environ.get('STN', '8'))):
                wi = k % 2
                k += 1

                def tr(e, cb=cb, tb=tb):
                    ins = None
                    for j in range(4):
                        ins = e.transpose(out=ps_tr[:, 128 * j:128 * j + 128], in_=xres[:, cb * 4 + j, tb * 128:tb * 128 + 128], identity=ident_f)
                    return ins
                V("pe", tr, reads=R_x[cb * 4:cb * 4 + 4] + [R_cst], writes=[R_ptr])
                V("act", lambda e, wi=wi: e.activation(out=W[wi][:], in_=ps_tr[:], func=AF.Copy, scale=scale), reads=[R_ptr], writes=[R_W[wi]])
                r0 = t * TT + tb * 128
                o = P.dma("sp", lambda e, s, wi=wi, r0=r0, cb=cb: e.dma_start(out=dram[r0:r0 + 128, cb * 512:(cb + 1) * 512], in_=W[wi][:]).then_inc(s, 16),
                          msem(f"xout{wi}"), 1, reads=[R_W[wi]])
                last.append(o)
        return last[-2:]

    def layer_norm(gname, bname, final=False):
        s1, s2 = ps_o[0], ps_o[1]
        for c in range(32):
            wi = c % 2
            V("act", lambda e, c=c, wi=wi: e.activation(out=W[wi][:], in_=xres[:, c, :], func=AF.Square), reads=[R_x[c]], writes=[R_W[wi]])
            V("pe", lambda e, c=c: e.matmul(s1[:], lhsT=ones_f[:], rhs=xres[:, c, :], start=(c == 0), stop=(c == 31)),
              reads=[R_x[c], R_c2], writes=[R_po[0]])
            V("pe", lambda e, c=c, wi=wi: e.matmul(s2[:], lhsT=ones_f[:], rhs=W[wi][:], start=(c == 0), stop=(c == 31)),
              reads=[R_W[wi], R_c2], writes=[R_po[1]])
        mean, rstd, tmp = W[2], W[3], W[4]
        V("act", lambda e: e.activation(out=mean[:], in_=s1[:], func=AF.Copy, scale=1.0 / 4096.0), reads=[R_po[0]], writes=[R_W[2]])
        V("dve", lambda e: e.tensor_tensor(out=tmp[:], in0=mean[:], in1=mean[:], op=ALU.mult), reads=[R_W[2]], writes=[R_W[4]])
        V("dve", lambda e: e.scalar_tensor_tensor(out=rstd[:], in0=s2[:], scalar=1.0 / 4096.0, in1=tmp[:], op0=ALU.mult, op1=ALU.subtract),
          reads=[R_po[1], R_W[4]], writes=[R_W[3]])
        V("act", lambda e: e.activation(out=rstd[:], in_=rstd[:], func=AF.Sqrt, bias=EPS), reads=[R_W[3]], writes=[R_W[3]])
        V("dve", lambda e: e.reciprocal(out=rstd[:], in_=rstd[:]), reads=[R_W[3]], writes=[R_W[3]])
        g0, b0 = PV[gname], PV[bname]
        for c in range(32):
            wi = 5 + (c % 2)
            V("dve", lambda e, c=c, wi=wi: e.tensor_tensor(out=W[wi][:], in0=xres[:, c, :], in1=mean[:], op=ALU.subtract),
              reads=[R_x[c], R_W[2]], writes=[R_W[wi]])
            V("dve", lambda e, wi=wi: e.tensor_tensor(out=W[wi][:], in0=W[wi][:], in1=rstd[:], op=ALU.mult),
              reads=[R_W[3]], writes=[R_W[wi]])
            V("act", lambda e, c=c, wi=wi: e.activation(out=xres[:, c, :], in_=W[wi][:], func=AF.Identity,
                                                        scale=pvec[:, g0 + c:g0 + c + 1], bias=pvec[:, b0 + c:b0 + c + 1]),
              reads=[R_W[wi], R_pv], writes=[R_x[c]])
            V("act", lambda e, c=c: e.activation(out=xbf[:, c, :], in_=xres[:, c, :], func=AF.Copy), reads=[R_x[c]], writes=[R_xb[c]])
            if not final:
                V("dve", lambda e, c=c: e.tensor_scalar(out=xres[:, c, :], in0=xres[:, c, :], scalar1=ALPHA, scalar2=None, op0=ALU.mult),
                  reads=[R_xb[c]], writes=[R_x[c]])

    def gemm_residual(wsrc, nk, rhs, rhs_res_fn, krows0=0):
        for mp in range(16):
            sl = load_w(wsrc[krows0:krows0 + nk * 128, mp * 256:(mp + 1) * 256], nk, 256)
            for j in range(2):
                m = mp * 2 + j
                pi = next_mm()
                mm_fm(sl, 128 * j, nk, rhs, rhs_res_fn(), ps_mm[pi], R_pmm[pi])
                V("dve", lambda e, m=m, pi=pi: e.tensor_tensor(out=xres[:, m, :], in0=xres[:, m, :], in1=ps_mm[pi][:], op=ALU.add),
                  reads=[R_pmm[pi]], writes=[R_x[m]])

    def ffn(l):
        groups = [(0, 22), (22, 44), (44, 65), (65, 86)]
        for (g0, g1) in groups:
            c = g0
            while c < g1:
                nch = min(2, g1 - c)
                sg = load_w(w_gate[l, :, c * 128:(c + nch) * 128], 32, nch * 128)
                su = load_w(w_up[l, :, c * 128:(c + nch) * 128], 32, nch * 128)
                pgs = []
                for j in range(nch):
                    pg = next_mm()
                    mm_fm(sg, 128 * j, 32, xb_rhs, R_xb, ps_mm[pg], R_pmm[pg])
                    pgs.append(pg)
                for j in range(nch):
                    pg = pgs[j]
                    wi = 6 + (j % 2)
                    V("act", lambda e, pg=pg, wi=wi: e.activation(out=W[wi][:], in_=ps_mm[pg][:], func=AF.Silu), reads=[R_pmm[pg]], writes=[R_W[wi]])
                for j in range(nch):
                    pu = next_mm()
                    mm_fm(su, 128 * j, 32, xb_rhs, R_xb, ps_mm[pu], R_pmm[pu])
                    wi = 6 + (j % 2)
                    hi = c + j - g0
                    V("dve", lambda e, pu=pu, wi=wi, hi=hi: e.tensor_tensor(out=ybuf[:, hi, :], in0=W[wi][:], in1=ps_mm[pu][:], op=ALU.mult),
                      reads=[R_W[wi], R_pmm[pu]], writes=[R_y[hi]])
                c += nch
            nk = g1 - g0
            gemm_residual(w_down[l], nk, lambda k: ybuf[:, k, :], lambda nk=nk: R_y[0:nk], krows0=g0 * 128)

    def recurrence(qf, R_qf, kf, R_kf, la, R_la, vt_col0, dv, st_dram, R_st, qscale, gate_fn, gain_col, ychunk0):
        nsl = dv // 128
        P.dma("sp", lambda e, s: e.dma_start(out=Sst[:, 0:dv], in_=st_dram).then_inc(s, 16), msem("stld"), 1, reads=[R_st], writes=[R_S])
        V("act", lambda e: e.activation(out=Sbf2[0][:, 0:dv], in_=Sst[:, 0:dv], func=AF.Copy), reads=[R_S], writes=[R_Sb2[0]])
        bb, R_bb = la, R_la
        V("dve", lambda e: e.tensor_tensor_scan(out=bb[:], data0=rmask[:], data1=la[:], initial=0.0, op0=ALU.mult, op1=ALU.add),
          reads=[R_c2], writes=[R_bb])
        Et, R_Et = W[6], R_W[6]
        qd, R_qd = B[0], R_B[0]
        ki, R_ki = B[1], R_B[1]
        ke, R_ke = B[2], R_B[2]
        V("act", lambda e: e.activation(out=Et[:], in_=bb[:], func=AF.Exp), reads=[R_bb], writes=[R_Et])
        V("dve", lambda e: e.scalar_tensor_tensor(out=qd[:], in0=qf[:], scalar=qscale, in1=Et[:], op0=ALU.mult, op1=ALU.mult),
          reads=[R_qf, R_Et], writes=[R_qd])
        V("act", lambda e: e.activation(out=Et[:], in_=bb[:], func=AF.Exp, scale=-1.0), reads=[R_bb, R_qd], writes=[R_Et])
        V("dve", lambda e: e.tensor_tensor(out=ki[:], in0=kf[:], in1=Et[:], op=ALU.mult), reads=[R_kf, R_Et], writes=[R_ki])
        b3 = bb[:].rearrange("p (c l) -> p c l", l=64)
        V("dve", lambda e: e.tensor_copy(out=smallc[:, 0:8], in_=b3[:, :, 63]), reads=[R_bb], writes=[R_sm])
        V("act", lambda e: e.activation(out=smallc[:, 8:16], in_=smallc[:, 0:8], func=AF.Exp), reads=[R_sm], writes=[R_sm])
        V("dve", lambda e: e.tensor_tensor(out=Et[:].rearrange("p (c l) -> p c l", l=64),
                                           in0=smallc[:, 0:8].unsqueeze(2).to_broadcast([128, 8, 64]), in1=b3, op=ALU.subtract),
          reads=[R_sm, R_bb, R_ki], writes=[R_Et])
        V("act", lambda e: e.activation(out=Et[:], in_=Et[:], func=AF.Exp), reads=[R_Et], writes=[R_Et])
        V("dve", lambda e: e.tensor_tensor(out=ke[:], in0=kf[:], in1=Et[:], op=ALU.mult), reads=[R_kf, R_Et], writes=[R_ke])
        ptb = ps_tr[:].bitcast(BF16)

        def trk(e):
            ins = None
            for j in range(4):
                ins = e.transpose(out=ptb[:, 128 * j:128 * j + 128], in_=ke[:, 128 * j:128 * j + 128], identity=ident_b[:])
            return ins
        V("pe", trk, reads=[R_ke, R_c2], writes=[R_ptr])
        V("act", lambda e: e.activation(out=ketm[:].rearrange("p j d -> p (j d)"), in_=ptb[:, 0:512], func=AF.Copy), reads=[R_ptr], writes=[R_ketm])
        def sc(e):
            ins = None
            for j in range(4):
                cs = slice(128 * j, 128 * j + 128)
                ins = e.matmul(ps_sc[:, cs], lhsT=ki[:, cs], rhs=qd[:, cs], start=True, stop=True)
            return ins
        V("pe", sc, reads=[R_ki, R_qd], writes=[R_psc])
        sT, R_sT = B[3], R_B[3]
        V("dve", lambda e: e.tensor_tensor(out=sT[:].rearrange("p (j l) -> p j l", l=128), in0=ps_sc[:].rearrange("p (j l) -> p j l", l=128),
                                           in1=maskT.unsqueeze(1).to_broadcast([128, 4, 128]), op=ALU.mult),
          reads=[R_psc, R_cst], writes=[R_sT])
        for c in range(8):
            j, hb = c // 2, (c % 2) * 64
            cs = slice(64 * c, 64 * c + 64)

            kvb, R_kvb = (ps_kv, R_pkv) if c % 2 == 0 else (ps_tr, R_ptr)
            V("pe", lambda e, j=j, hb=hb, kvb=kvb: e.matmul(kvb[:, 0:dv], lhsT=ketm[hb:hb + 64, j, :], rhs=vtm[hb:hb + 64, j, vt_col0:vt_col0 + dv], start=True, stop=True),
              reads=[R_ketm, R_vtm], writes=[R_kvb])

            Sbf, R_Sb = Sbf2[c % 2], R_Sb2[c % 2]
            Sbn, R_Sbn = Sbf2[(c + 1) % 2], R_Sb2[(c + 1) % 2]

            def inter(e, cs=cs, c=c, j=j, hb=hb, Sbf=Sbf):
                ins = None
                for s in range(nsl):
                    e.matmul(ps_o[s][:, cs], lhsT=vtm[hb:hb + 64, j, vt_col0 + 128 * s:vt_col0 + 128 * s + 128], rhs=sT[hb:hb + 64, cs],
                             start=True, stop=False)
                    ins = e.matmul(ps_o[s][:, cs], lhsT=Sbf[:, 128 * s:128 * s + 128], rhs=qd[:, cs], start=False, stop=True)
                return ins
            V("pe", inter, reads=[R_Sb, R_qd, R_vtm, R_sT], writes=R_po[0:nsl])
            V("dve", lambda e, c=c, kvb=kvb: e.scalar_tensor_tensor(out=Sst[:, 0:dv], in0=Sst[:, 0:dv], scalar=smallc[:, 8 + c:9 + c], in1=kvb[:, 0:dv],
                                                                    op0=ALU.mult, op1=ALU.add),
              reads=[R_kvb, R_sm], writes=[R_S])
            if c < 7:
                V("act", lambda e, Sbn=Sbn: e.activation(out=Sbn[:, 0:dv], in_=Sst[:, 0:dv], func=AF.Copy), reads=[R_S], writes=[R_Sbn])
        P.dma("sp", lambda e, s: e.dma_start(out=st_dram, in_=Sst[:, 0:dv]).then_inc(s, 16), msem("stst"), 1, reads=[R_S], writes=[R_st])
        osq, R_osq = B[4], R_B[4]
        for s in range(nsl):
            V("act", lambda e, s=s: e.activation(out=osq[:], in_=ps_o[s][:], func=AF.Square), reads=[R_po[s]], writes=[R_osq])
            V("pe", lambda e, s=s: e.matmul(ps_sc[:], lhsT=ones_b[:], rhs=osq[:], start=(s == 0), stop=(s == nsl - 1)),
              reads=[R_osq, R_c2], writes=[R_psc])
        rs, R_rs = W[6], R_W[6]
        V("act", lambda e: e.activation(out=rs[:], in_=ps_sc[:], func=AF.Sqrt, bias=EPS, scale=1.0 / dv), reads=[R_psc], writes=[R_rs])
        V("dve", lambda e: e.reciprocal(out=rs[:], in_=rs[:]), reads=[R_rs], writes=[R_rs])
        for s in range(nsl):
            gs, R_gs = gate_fn(s)
            yt, R_yt = W[7], R_W[7]
            V("dve", lambda e, s=s: e.scalar_tensor_tensor(out=yt[:], in0=ps_o[s][:], scalar=pvec[:, gain_col + s:gain_col + s + 1], in1=rs[:],
                                                           op0=ALU.mult, op1=ALU.mult),
              reads=[R_po[s], R_rs, R_pv], writes=[R_yt])
            V("dve", lambda e, s=s, gs=gs: e.tensor_tensor(out=ybuf[:, ychunk0 + s, :], in0=yt[:], in1=gs, op=ALU.mult),
              reads=[R_yt, R_gs], writes=[R_y[ychunk0 + s]])

    def v_token_major(slot, func):
        for tb in range(4):
            pi = next_mm()

            def fn(e, tb=tb, pi=pi):
                ins = None
                for k in range(32):
                    ins = e.matmul(ps_mm[pi][:, 0:256], lhsT=xbf[:, k, tb * 128:tb * 128 + 128], rhs=wsl[slot][:, k, 0:256],
                                   start=(k == 0), stop=(k == 31))
                return ins
            V("pe", fn, reads=[R_w[slot]] + R_xb, writes=[R_pmm[pi]])
            V("act", lambda e, tb=tb, pi=pi: e.activation(out=vtm[:, tb, :], in_=ps_mm[pi][:, 0:256], func=func), reads=[R_pmm[pi]], writes=[R_vtm])

    def odd_mixer():
        for hp in range(16):
            c0 = hp * 256
            sq_ = load_w(od_w_in[:, c0:c0 + 256], 32, 256)
            for hh in range(2):
                pi = next_mm()
                mm_fm(sq_, 128 * hh, 32, xb_rhs, R_xb, ps_mm[pi], R_pmm[pi])
                V("act", lambda e, hh=hh, pi=pi: e.activation(out=W[hh][:], in_=ps_mm[pi][:], func=AF.Copy), reads=[R_pmm[pi]], writes=[R_W[hh]])
            sf = load_w(od_w_in[:, 4096 + c0:4096 + c0 + 256], 32, 256)
            for hh in range(2):
                h = hp * 2 + hh
                pi = next_mm()
                mm_fm(sf, 128 * hh, 32, xb_rhs, R_xb, ps_mm[pi], R_pmm[pi])
                sg_, R_sg = W[7], R_W[7]
                V("act", lambda e, pi=pi: e.activation(out=sg_[:], in_=ps_mm[pi][:], func=AF.Sigmoid), reads=[R_pmm[pi]], writes=[R_sg])
                V("dve", lambda e, hh=hh, h=h: e.tensor_scalar(out=W[2 + hh][:], in0=sg_[:], scalar1=lbt[:, 2, h:h + 1], scalar2=lbt[:, 1, h:h + 1],
                                                               op0=ALU.mult, op1=ALU.add), reads=[R_sg, R_c2], writes=[R_W[2 + hh]])
                V("dve", lambda e, hh=hh, h=h: e.tensor_scalar(out=W[4 + hh][:], in0=sg_[:], scalar1=lbt[:, 1, h:h + 1], scalar2=lbt[:, 0, h:h + 1],
                                                               op0=ALU.mult, op1=ALU.add), reads=[R_sg, R_c2], writes=[R_W[4 + hh]])
                V("act", lambda e, hh=hh: e.activation(out=W[4 + hh][:], in_=W[4 + hh][:], func=AF.Ln), reads=[R_W[4 + hh]], writes=[R_W[4 + hh]])
            si = load_w(od_w_in[:, 8192 + c0:8192 + c0 + 256], 32, 256)
            v_token_major(si, AF.Silu)
            sg = load_w(od_w_in[:, 12288 + c0:12288 + c0 + 256], 32, 256)
            for hh in range(2):
                pi = next_mm()
                mm_fm(sg, 128 * hh, 32, xb_rhs, R_xb, ps_mm[pi], R_pmm[pi])
                V("act", lambda e, hh=hh, pi=pi: e.activation(out=B[5 + hh][:], in_=ps_mm[pi][:], func=AF.Silu), reads=[R_pmm[pi]], writes=[R_B[5 + hh]])
            for hh in range(2):
                h = hp * 2 + hh
                recurrence(W[hh], R_W[hh], W[2 + hh], R_W[2 + hh], W[4 + hh], R_W[4 + hh], 128 * hh, 128, sth_d[h], R_sth[h], 1.0,
                           lambda s, hh=hh: (B[5 + hh][:], R_B[5 + hh]), PV["hg_g"], h)
        gemm_residual(od_w_out, 32, lambda k: ybuf[:, k, :], lambda: R_y)

    def even_mixer(t):
        sa = load_w(ev_w_in[:, 8192:8208], 32, 16)
        pi = next_mm()
        mm_fm(sa, 0, 32, xb_rhs, R_xb, ps_mm[pi], R_pmm[pi], m=16)
        V("act", lambda e, pi=pi: e.activation(out=alrT[:], in_=ps_mm[pi][0:16, :], func=AF.Copy), reads=[R_pmm[pi]], writes=[R_alr])
        Lre, Lim, Ire, Iim = s5L[:, 0, :], s5L[:, 1, :], s5L[:, 2, :], s5L[:, 3, :]
        cT, sT_ = s5k[:, 2, :], s5k[:, 3, :]
        V("dve", lambda e: e.tensor_tensor(out=Ire, in0=Lre, in1=cT, op=ALU.mult), reads=[R_s5L, R_s5k], writes=[R_init])
        V("dve", lambda e: e.tensor_tensor(out=t8[:], in0=Lim, in1=sT_, op=ALU.mult), reads=[R_s5L, R_s5k], writes=[R_s5k])
        V("dve", lambda e: e.tensor_tensor(out=Ire, in0=Ire, in1=t8[:], op=ALU.subtract), reads=[R_s5k], writes=[R_init])
        V("dve", lambda e: e.tensor_tensor(out=Iim, in0=Lre, in1=sT_, op=ALU.mult), reads=[R_s5L, R_s5k], writes=[R_init])
        V("dve", lambda e: e.tensor_tensor(out=t8[:], in0=Lim, in1=cT, op=ALU.mult), reads=[R_s5L, R_s5k], writes=[R_s5k])
        V("dve", lambda e: e.tensor_tensor(out=Iim, in0=Iim, in1=t8[:], op=ALU.add), reads=[R_s5k], writes=[R_init])
        for cp in range(8):
            su = load_w(ev_w_in[:, cp * 256:cp * 256 + 256], 32, 256)
            for jj in range(2):
                ch = cp * 2 + jj
                pi = next_mm()
                mm_fm(su, 128 * jj, 32, xb_rhs, R_xb, ps_mm[pi], R_pmm[pi])
                uf, R_uf = W[6], R_W[6]
                ub, R_ub = B[4], R_B[4]
                V("act", lambda e, pi=pi: e.activation(out=uf[:], in_=ps_mm[pi][:], func=AF.Copy), reads=[R_pmm[pi]], writes=[R_uf])
                V("dve", lambda e, pi=pi: e.tensor_copy(out=ub[:], in_=ps_mm[pi][:]), reads=[R_pmm[pi]], writes=[R_ub])
                P.dma("pool", lambda e, s, ch=ch: e.dma_start(out=s5w[:, 0:2, :], in_=s5bd_d[ch]).then_inc(s, 16), msem("bdld"), 1, writes=[R_s5wb])
                P.dma("sp", lambda e, s, ch=ch: e.dma_start(out=s5w[:, 2:5, :], in_=s5c_d[ch]).then_inc(s, 16), msem("cld"), 1,
                      reads=[R_s5c[ch]], writes=[R_s5wc])
                for s in range(4):
                    col = ch * 4 + s
                    sl = slice(128 * s, 128 * s + 128)
                    fp = s5k[:, 0, col:col + 1]
                    mg = s5k[:, 1, col:col + 1]
                    xa, R_xa = W[0], R_W[0]
                    xi = W[1][:].bitcast(I32)
                    sinT, R_sin = W[2], R_W[2]
                    cosT, R_cos = W[3], R_W[3]
                    V("dve", lambda e, fp=fp: e.tensor_scalar(out=xa[:], in0=iota, scalar1=fp, scalar2=None, op0=ALU.mult), reads=[R_cst, R_s5k], writes=[R_xa])
                    V("dve", lambda e: e.tensor_copy(out=xi, in_=xa[:]), reads=[R_xa], writes=[R_W[1]])
                    V("dve", lambda e: e.tensor_tensor(out=xa[:], in0=xa[:], in1=xi, op=ALU.subtract), reads=[R_W[1]], writes=[R_xa])
                    V("act", lambda e: e.activation(out=sinT[:], in_=xa[:], func=AF.Sin, scale=TWO_PI), reads=[R_xa], writes=[R_sin])
                    V("dve", lambda e, fp=fp: e.tensor_scalar(out=xa[:], in0=iota, scalar1=fp, scalar2=0.25, op0=ALU.mult, op1=ALU.add),
                      reads=[R_cst, R_s5k, R_sin], writes=[R_xa])
                    V("dve", lambda e: e.tensor_copy(out=xi, in_=xa[:]), reads=[R_xa], writes=[R_W[1]])
                    V("dve", lambda e: e.tensor_tensor(out=xa[:], in0=xa[:], in1=xi, op=ALU.subtract), reads=[R_W[1]], writes=[R_xa])
                    V("act", lambda e: e.activation(out=cosT[:], in_=xa[:], func=AF.Sin, scale=TWO_PI), reads=[R_xa], writes=[R_cos])
                    V("pe", lambda e, sl=sl: e.matmul(ps_o[0][:], lhsT=s5w[:, 0, sl], rhs=ub[:], start=True, stop=True), reads=[R_s5wb, R_ub], writes=[R_po[0]])
                    V("pe", lambda e, sl=sl: e.matmul(ps_o[1][:], lhsT=s5w[:, 1, sl], rhs=ub[:], start=True, stop=True), reads=[R_s5wb, R_ub], writes=[R_po[1]])
                    m1, R_m1 = W[0], R_W[0]
                    m2, R_m2 = W[1], R_W[1]
                    m4, R_m4 = W[4], R_W[4]
                    V("dve", lambda e: e.tensor_tensor(out=m1[:], in0=cosT[:], in1=ps_o[0][:], op=ALU.mult), reads=[R_cos, R_po[0]], writes=[R_m1])
                    V("dve", lambda e: e.tensor_tensor(out=m2[:], in0=sinT[:], in1=ps_o[1][:], op=ALU.mult), reads=[R_sin, R_po[1]], writes=[R_m2])
                    V("dve", lambda e: e.tensor_tensor(out=m1[:], in0=m1[:], in1=m2[:], op=ALU.add), reads=[R_m2], writes=[R_m1])
                    V("dve", lambda e: e.tensor_tensor(out=m2[:], in0=cosT[:], in1=ps_o[1][:], op=ALU.mult), reads=[R_cos, R_po[1], R_m1], writes=[R_m2])
                    V("dve", lambda e: e.tensor_tensor(out=m4[:], in0=sinT[:], in1=ps_o[0][:], op=ALU.mult), reads=[R_sin, R_po[0]], writes=[R_m4])
                    V("dve", lambda e: e.tensor_tensor(out=m2[:], in0=m2[:], in1=m4[:], op=ALU.subtract), reads=[R_m4], writes=[R_m2])
                    mgb = W[5]
                    V("dve", lambda e, mg=mg: e.tensor_scalar(out=mgb[:], in0=rmask[:], scalar1=0.0, scalar2=mg, op0=ALU.mult, op1=ALU.add),
                      reads=[R_c2, R_s5k], writes=[R_W[5]])
                    sre, R_sre = W[4], R_W[4]
                    sim, R_sim = W[7], R_W[7]
                    V("dve", lambda e, col=col: e.tensor_tensor_scan(out=sre[:], data0=mgb[:], data1=m1[:], initial=s5L[:, 2, col:col + 1], op0=ALU.mult, op1=ALU.add),
                      reads=[R_W[5], R_m1, R_init], writes=[R_sre])
                    V("dve", lambda e, col=col: e.tensor_tensor_scan(out=sim[:], data0=mgb[:], data1=m2[:], initial=s5L[:, 3, col:col + 1], op0=ALU.mult, op1=ALU.add),
                      reads=[R_W[5], R_m2, R_init], writes=[R_sim])
                    V("act", lambda e, col=col: e.activation(out=s5L[:, 0, col:col + 1], in_=sre[:, TT - 1:TT], func=AF.Copy), reads=[R_sre], writes=[R_s5L])
                    V("act", lambda e, col=col: e.activation(out=s5L[:, 1, col:col + 1], in_=sim[:, TT - 1:TT], func=AF.Copy), reads=[R_sim], writes=[R_s5L])
                    V("dve", lambda e: e.tensor_tensor(out=B[0][:], in0=cosT[:], in1=sre[:], op=ALU.mult), reads=[R_cos, R_sre], writes=[R_B[0]])
                    V("dve", lambda e: e.tensor_tensor(out=B[1][:], in0=sinT[:], in1=sim[:], op=ALU.mult), reads=[R_sin, R_sim], writes=[R_B[1]])
                    V("dve", lambda e: e.tensor_tensor(out=B[2][:], in0=sinT[:], in1=sre[:], op=ALU.mult), reads=[R_sin, R_sre], writes=[R_B[2]])
                    V("dve", lambda e: e.tensor_tensor(out=B[3][:], in0=cosT[:], in1=sim[:], op=ALU.mult), reads=[R_cos, R_sim], writes=[R_B[3]])

                    def ymm(e, s=s, sl=sl):
                        e.matmul(ps_kv[:], lhsT=s5w[:, 2, sl], rhs=B[0][:], start=(s == 0), stop=False)
                        e.matmul(ps_kv[:], lhsT=s5w[:, 3, sl], rhs=B[1][:], start=False, stop=False)
                        e.matmul(ps_kv[:], lhsT=s5w[:, 4, sl], rhs=B[2][:], start=False, stop=False)
                        return e.matmul(ps_kv[:], lhsT=s5w[:, 4, sl], rhs=B[3][:], start=False, stop=(s == 3))
                    V("pe", ymm, reads=[R_s5wc, R_B[0], R_B[1], R_B[2], R_B[3]], writes=[R_pkv])
                yv, R_yv = W[0], R_W[0]
                tq, R_tq = W[1], R_W[1]
                dcol = PV["s5_d"] + ch
                V("dve", lambda e, dcol=dcol: e.scalar_tensor_tensor(out=yv[:], in0=uf[:], scalar=pvec[:, dcol:dcol + 1], in1=ps_kv[:], op0=ALU.mult, op1=ALU.add),
                  reads=[R_uf, R_pkv, R_pv], writes=[R_yv])
                V("act", lambda e: e.activation(out=tq[:], in_=yv[:], func=AF.Square), reads=[R_yv], writes=[R_tq])
                V("dve", lambda e: e.tensor_scalar(out=tq[:], in0=tq[:], scalar1=0.044715, scalar2=1.0, op0=ALU.mult, op1=ALU.add), reads=[R_tq], writes=[R_tq])
                V("dve", lambda e: e.tensor_tensor(out=tq[:], in0=tq[:], in1=yv[:], op=ALU.mult), reads=[R_yv], writes=[R_tq])
                V("act", lambda e: e.activation(out=tq[:], in_=tq[:], func=AF.Sigmoid, scale=2.0 * math.sqrt(2.0 / math.pi)), reads=[R_tq], writes=[R_tq])
                V("dve", lambda e, ch=ch: e.tensor_tensor(out=ybuf[:, 16 + ch, :], in0=tq[:], in1=yv[:], op=ALU.mult), reads=[R_tq, R_yv], writes=[R_y[16 + ch]])
        for mp in range(8):
            sl_ = load_w(w_glu[:, mp * 256:(mp + 1) * 256], 16, 256)
            for j in range(2):
                m = mp * 2 + j
                pi = next_mm()
                mm_fm(sl_, 128 * j, 16, lambda k: ybuf[:, 16 + k, :], R_y[16:32], ps_mm[pi], R_pmm[pi])
                V("act", lambda e, pi=pi: e.activation(out=W[7][:], in_=ps_mm[pi][:], func=AF.Sigmoid), reads=[R_pmm[pi]], writes=[R_W[7]])
                V("dve", lambda e, m=m: e.tensor_tensor(out=ybuf[:, m, :], in0=W[7][:], in1=ybuf[:, 16 + m, :], op=ALU.mult),
                  reads=[R_W[7], R_y[16 + m]], writes=[R_y[m]])
        for h in range(8):
            sqk = load_w(None, 32, 256, parts=[(ev_w_in[:, 2048 + 128 * h:2048 + 128 * h + 128], 0, 128),
                                               (ev_w_in[:, 3072 + 128 * h:3072 + 128 * h + 128], 128, 128)])
            for hh in range(2):
                pi = next_mm()
                mm_fm(sqk, 128 * hh, 32, xb_rhs, R_xb, ps_mm[pi], R_pmm[pi])
                V("act", lambda e, hh=hh, pi=pi: e.activation(out=W[hh][:], in_=ps_mm[pi][:], func=AF.Copy), reads=[R_pmm[pi]], writes=[R_W[hh]])
            V("pe", lambda e, h=h: e.matmul(ps_sc[:], lhsT=walpha[:, 128 * h:128 * h + 128], rhs=alrT[:], start=True, stop=True),
              reads=[R_wal, R_alr], writes=[R_psc])
            la, R_la = W[4], R_W[4]
            V("dve", lambda e, h=h: e.tensor_scalar(out=la[:], in0=ps_sc[:], scalar1=pvec[:, PV["gla_b"] + h:PV["gla_b"] + h + 1], scalar2=None, op0=ALU.add),
              reads=[R_psc, R_pv], writes=[R_la])
            V("act", lambda e: e.activation(out=la[:], in_=la[:], func=AF.Exp, scale=-1.0), reads=[R_la], writes=[R_la])
            V("act", lambda e: e.activation(out=la[:], in_=la[:], func=AF.Ln, bias=1.0), reads=[R_la], writes=[R_la])
            V("dve", lambda e: e.tensor_scalar(out=la[:], in0=la[:], scalar1=-1.0 / 16.0, scalar2=None, op0=ALU.mult), reads=[R_la], writes=[R_la])
            sv = load_w(ev_w_in[:, 4096 + 256 * h:4096 + 256 * h + 256], 32, 256)
            v_token_major(sv, AF.Copy)
            sg = load_w(ev_w_in[:, 6144 + 256 * h:6144 + 256 * h + 256], 32, 256)
            for hh in range(2):
                pi = next_mm()
                mm_fm(sg, 128 * hh, 32, xb_rhs, R_xb, ps_mm[pi], R_pmm[pi])
                V("act", lambda e, hh=hh, pi=pi: e.activation(out=B[5 + hh][:], in_=ps_mm[pi][:], func=AF.Silu), reads=[R_pmm[pi]], writes=[R_B[5 + hh]])
            recurrence(W[0], R_W[0], W[1], R_W[1], la, R_la, 0, 256, stg_d[h], R_stg[h], 128.0 ** -0.5,
                       lambda s: (B[5 + s][:], R_B[5 + s]), PV["gla_g"], 16 + 2 * h)
        if stop_after == 10:
            for c in range(32):
                V("act", lambda e, c=c: e.activation(out=xres[:, c, :], in_=ybuf[:, c, :], func=AF.Copy), reads=[R_y[c]], writes=[R_x[c]])
            return
        gemm_residual(ev_w_out, 32, lambda k: ybuf[:, k, :], lambda: R_y)

    finals = []
    for t in range(NT):
        load_x_tile(t)
        if stop_after == -1:
            finals += store_tile(t, out_d, scale=1.0 / ALPHA)
            continue
        even_mixer(t)
        if stop_after == 10:
            finals += store_tile(t, out_d)
            continue
        layer_norm("mix_g0", "mix_b0", final=(stop_after == 1))
        if stop_after == 1:
            finals += store_tile(t, out_d)
            continue
        ffn(0)
        layer_norm("ffn_g0", "ffn_b0", final=(stop_after == 2))
        if stop_after == 2:
            finals += store_tile(t, out_d)
            continue
        odd_mixer()
        layer_norm("mix_g1", "mix_b1", final=(stop_after == 3))
        if stop_after == 3:
            finals += store_tile(t, out_d)
            continue
        ffn(1)
        layer_norm("ffn_g1", "ffn_b1", final=True)
        finals += store_tile(t, out_d)
    P.emit(final_waits=finals)
    P.close()
    return nc, P


def _host_layouts(inp):
    f = lambda a: np.ascontiguousarray(a, dtype=np.float32)
    cols = []
    fm = lambda v, n: f(v.reshape(n, 128).T)
    for l in range(2):
        cols += [fm(inp["ln_mix_g"][l], 32), fm(inp["ln_mix_b"][l], 32), fm(inp["ln_ffn_g"][l], 32), fm(inp["ln_ffn_b"][l], 32)]
    cols += [fm(inp["hg_lb_table"][0], 32), fm(inp["hg_lb_table"][1], 32)]
    cols += [fm(inp["gla_b_alpha"][0], 8), fm(inp["gla_norm_g"][0], 2), fm(inp["hg_norm_g"][0], 1), fm(inp["s5_d"][0], 16)]
    def lamT(a):
        a = a.reshape(16, 8, 4, 16)
        return f(a.transpose(1, 3, 0, 2).reshape(128, 64))
    cols += [lamT(inp["s5_lam_re"][0]), lamT(inp["s5_lam_im"][0])]
    ld = inp["s5_log_dt"][0].reshape(16, 8)
    cols += [f(np.repeat(ld.T[:, None, :], 16, axis=1).reshape(128, 16))]
    pvec = f(np.concatenate(cols, axis=1))
    assert pvec.shape == (128, NPV)
    cst = np.zeros((128, 768), np.float32)
    cst[:, 0:128] = np.eye(128, dtype=np.float32)
    m = np.arange(128)[:, None]
    l = np.arange(128)[None, :]
    cst[:, 128:256] = ((m // 64 == l // 64) & (l >= m)).astype(np.float32)
    cst[:, 256:768] = np.arange(512, dtype=np.float32)[None, :]
    b_re = inp["s5_b_re"][0].reshape(16, 8, 4, 16, 16)
    b_im = inp["s5_b_im"][0].reshape(16, 8, 4, 16, 16)
    c_re = inp["s5_c_re"][0].reshape(16, 8, 16, 4, 16)
    c_im = inp["s5_c_im"][0].reshape(16, 8, 16, 4, 16)
    bd = np.zeros((16, 8, 16, 2, 4, 8, 16), np.float32)
    cd = np.zeros((16, 8, 16, 2, 4, 8, 16), np.float32)
    for g in range(8):
        bd[:, g, :, 0, :, g, :] = b_re[:, g].transpose(0, 3, 1, 2)
        bd[:, g, :, 1, :, g, :] = b_im[:, g].transpose(0, 3, 1, 2)
        cd[:, g, :, 0, :, g, :] = c_re[:, g].transpose(0, 3, 2, 1)
        cd[:, g, :, 1, :, g, :] = c_im[:, g].transpose(0, 3, 2, 1)
    bd = bd.reshape(16, 128, 2, 512)
    cd = cd.reshape(16, 128, 2, 512)
    return pvec, cst, bd, cd


_CACHE = {}


def kernel(**inputs):
    inp = {k: np.asarray(v) for k, v in inputs.items()}
    pvec, cst, bd, cd = _host_layouts(inp)
    if "nc" not in _CACHE:
        _CACHE["nc"] = build(4)[0]
    nc = _CACHE["nc"]
    f = lambda a: np.ascontiguousarray(a, dtype=np.float32)
    shared = {
        "ev_w_in": f(inp["ev_w_in"][0]), "ev_w_out": f(inp["ev_w_out"][0]), "s5_w_glu": f(inp["s5_w_glu"][0]),
        "od_w_in": f(inp["od_w_in"][0]), "od_w_out": f(inp["od_w_out"][0]),
        "ffn_w_gate": f(inp["ffn_w_gate"]), "ffn_w_up": f(inp["ffn_w_up"]), "ffn_w_down": f(inp["ffn_w_down"]),
        "gla_w_alpha": f(inp["gla_w_alpha"][0]), "pvec": pvec, "cst": cst, "s5bd": bd, "s5cd": cd,
    }
    in_maps = [dict(shared, x=f(inp["x"][b])) for b in range(8)]
    res = run_bass_kernel_spmd(nc, in_maps, core_ids=list(range(8)))
    return np.stack([r["out"] for r in res.results], axis=0).astype(np.float32)
```

```python
import math
import os
import numpy as np
_KD = os.environ.get('KDBG', '')
import concourse.bass as bass
import concourse.mybir as mybir
from concourse.bass_utils import run_bass_kernel_spmd

F32 = mybir.dt.float32
BF16 = mybir.dt.bfloat16
I32 = mybir.dt.int32
AF = mybir.ActivationFunctionType
ALU = mybir.AluOpType

ENGS = ["pe", "act", "dve", "pool", "sp"]
ALPHA = 4.0 ** 0.25
EPS = 1e-5
TT = 512
TWO_PI = 2.0 * math.pi * (1.0 - 1e-6)


class Res:
    __slots__ = ("name", "writer", "readers", "excl")

    def __init__(self, name, excl=False):
        self.name = name
        self.writer = None
        self.readers = []
        self.excl = excl


class DmaSem:
    __slots__ = ("sem", "count", "key")

    def __init__(self, sem, key):
        self.sem = sem
        self.count = 0
        self.key = key


class Op:
    __slots__ = ("eng", "fn", "deps", "needs_inc", "sem", "val", "is_dma", "key")


class Prog:
    def __init__(self, nc, same_engine_sync=("act", "dve", "pool")):
        self.nc = nc
        self.ops = {e: [] for e in ENGS}
        self.same = set(same_engine_sync)
        self.esem = {}
        self._stack = []

    def new_sem(self, name):
        cm = self.nc.semaphore(name)
        s = cm.__enter__()
        self._stack.append(cm)
        return s

    def dma_sem(self, name):
        return DmaSem(self.new_sem(name), ("d", name))

    def sbuf(self, name, shape, dt):
        cm = self.nc.sbuf_tensor("sb_" + name, shape, dt)
        t = cm.__enter__()
        self._stack.append(cm)
        return t

    def psum(self, name, shape, dt):
        cm = self.nc.psum_tensor("pt_" + name, shape, dt)
        t = cm.__enter__()
        self._stack.append(cm)
        return t

    def _track(self, op, reads, writes):
        ex = [r for r in reads if r.excl and r not in writes]
        if ex:
            reads = [r for r in reads if not r.excl]
            writes = list(writes) + ex
        deps = []
        for r in list(reads) + list(writes):
            if r.writer is not None:
                deps.append(r.writer)
        for w in writes:
            deps.extend(w.readers)
        for r in reads:
            r.readers.append(op)
        for w in writes:
            w.writer = op
            w.readers = []
        out = []
        seen = set()
        for d in deps:
            if d is op or id(d) in seen:
                continue
            seen.add(id(d))
            if d.eng == op.eng and not d.is_dma and op.eng not in self.same:
                continue
            out.append(d)
            d.needs_inc = True
        op.deps = out

    def op(self, eng, fn, reads=(), writes=()):
        o = Op()
        o.eng = eng
        o.fn = fn
        o.needs_inc = False
        o.is_dma = False
        o.sem = None
        o.val = None
        o.key = ("e", eng)
        self._track(o, reads, writes)
        self.ops[eng].append(o)
        return o

    def dma(self, eng, fn, dsem, n, reads=(), writes=()):
        o = Op()
        o.eng = eng
        o.fn = fn
        o.needs_inc = False
        o.is_dma = True
        dsem.count += 16 * n
        o.sem = dsem.sem
        o.val = dsem.count
        o.key = dsem.key
        self._track(o, reads, writes)
        self.ops[eng].append(o)
        return o

    def emit(self, final_waits=()):
        nc = self.nc
        for e in ENGS:
            self.esem[e] = self.new_sem("es_" + e)
        for e in ENGS:
            c = 0
            for o in self.ops[e]:
                if o.is_dma:
                    continue
                if o.needs_inc:
                    c += 1
                    o.val = c
                    o.sem = self.esem[e]
        bname = {"pe": "tensor", "act": "scalar", "dve": "vector", "pool": "gpsimd", "sp": "sync"}
        self.stats = {}
        with nc.Block() as block:
            for e in ENGS:
                def body(eng, e=e):
                    waited = {}
                    nw = 0
                    for o in self.ops[e]:
                        for d in o.deps:
                            if waited.get(d.key, 0) < d.val:
                                eng.wait_ge(d.sem, d.val)
                                waited[d.key] = d.val
                                nw += 1
                        if o.is_dma:
                            o.fn(eng, o.sem)
                        else:
                            ins = o.fn(eng)
                            if o.needs_inc:
                                ins.then_inc(o.sem, 1)
                    if e == "sp":
                        for d in final_waits:
                            eng.wait_ge(d.sem, d.val)
                    self.stats[e] = (len(self.ops[e]), nw)
                getattr(block, bname[e])(body)

    def close(self):
        for cm in reversed(self._stack):
            cm.__exit__(None, None, None)
        self._stack = []


PV = {}
_off = 0
for _n, _w in [("mix_g0", 32), ("mix_b0", 32), ("ffn_g0", 32), ("ffn_b0", 32),
               ("mix_g1", 32), ("mix_b1", 32), ("ffn_g1", 32), ("ffn_b1", 32),
               ("lbt0", 32), ("lbt1", 32), ("gla_b", 8), ("gla_g", 2), ("hg_g", 1), ("s5_d", 16),
               ("lam_re", 64), ("lam_im", 64), ("logdt", 16)]:
    PV[_n] = _off
    _off += _w
NPV = _off
NLN = 256


def build(NT=4, stop_after=None):
    nc = bass.Bass("TRN2", target_bir_lowering=False)
    P = Prog(nc)
    NTOK = NT * TT

    def din(name, shape, dt=F32):
        if 'nowt' in _KD and (name.startswith("ev_w") or name.startswith("od_w") or name.startswith("ffn_w") or name == "s5_w_glu"):
            return None
        return nc.dram_tensor(name, shape, dt, kind="ExternalInput").ap()

    x_d = din("x", [2048, 4096])
    ev_w_in = din("ev_w_in", [4096, 8208])
    ev_w_out = din("ev_w_out", [4096, 4096])
    w_glu = din("s5_w_glu", [2048, 2048])
    od_w_in = din("od_w_in", [4096, 16384])
    od_w_out = din("od_w_out", [4096, 4096])
    w_gate = din("ffn_w_gate", [2, 4096, 11008])
    w_up = din("ffn_w_up", [2, 4096, 11008])
    w_down = din("ffn_w_down", [2, 11008, 4096])
    w_alpha_d = din("gla_w_alpha", [16, 1024])
    pvec_d = din("pvec", [128, NPV])
    cst_d = din("cst", [128, 768])
    s5bd_d = din("s5bd", [16, 128, 2, 512])
    s5cd_d = din("s5cd", [16, 128, 2, 512])
    out_d = nc.dram_tensor("out", [NTOK, 4096], F32, kind="ExternalOutput").ap()
    s5c_d = nc.dram_tensor("s5c_scr", [16, 128, 3, 512], BF16, kind="Internal").ap()
    stg_d = nc.dram_tensor("st_gla", [8, 128, 256], F32, kind="Internal").ap()
    sth_d = nc.dram_tensor("st_hg", [32, 128, 128], F32, kind="Internal").ap()
    R_s5c = [Res(f"s5c{c}") for c in range(16)]
    R_stg = [Res(f"stg{h}") for h in range(8)]
    R_sth = [Res(f"sth{h}") for h in range(32)]

    xres = P.sbuf("xres", [128, 32, TT], F32)
    R_x = [Res(f"x{c}") for c in range(32)]
    xbf = P.sbuf("xbf", [128, 32, TT], BF16)
    R_xb = [Res(f"xb{c}") for c in range(32)]
    ybuf = P.sbuf("ybuf", [128, 32, TT], BF16)
    R_y = [Res(f"y{c}") for c in range(32)]
    NSLOT = 2
    wsl = [P.sbuf(f"wsl{i}", [128, 32, 256], BF16) for i in range(NSLOT)]
    R_w = [Res(f"w{i}") for i in range(NSLOT)]
    S_w = [P.dma_sem(f"wsem{i}") for i in range(NSLOT)]
    wctr = [0]

    pvec = P.sbuf("pvec", [128, NPV], F32)
    R_pv = Res("pvec")
    cst = P.sbuf("cst", [128, 768], F32)
    R_cst = Res("cst")
    ident_f = cst[:, 0:128]
    maskT = cst[:, 128:256]
    iota = cst[:, 256:768]
    ident_b = P.sbuf("identb", [128, 128], BF16)
    ones_f = P.sbuf("onesf", [128, 128], F32)
    ones_b = P.sbuf("onesb", [128, 128], BF16)
    rmask = P.sbuf("rmask", [128, TT], BF16)
    lbt = P.sbuf("lbt", [128, 3, 32], F32)
    s5k = P.sbuf("s5k", [128, 8, 64], F32)
    s5L = P.sbuf("s5L", [128, 4, 64], F32)
    R_s5k = Res("s5k")
    R_s5L = Res("s5L")
    R_init = Res("s5init")
    walpha = P.sbuf("walpha", [16, 1024], BF16)
    R_wal = Res("walpha")
    alrT = P.sbuf("alrT", [16, TT], BF16)
    R_alr = Res("alrT")
    NWB = 8
    W = [P.sbuf(f"W{i}", [128, TT], F32) for i in range(NWB)]
    R_W = [Res(f"W{i}") for i in range(NWB)]
    NBB = 7
    B = [P.sbuf(f"B{i}", [128, TT], BF16) for i in range(NBB)]
    R_B = [Res(f"B{i}") for i in range(NBB)]
    s5w = P.sbuf("s5w", [128, 5, 512], BF16)
    R_s5wb = Res("s5wb")
    R_s5wc = Res("s5wc")
    vtm = P.sbuf("vtm", [128, 4, 256], BF16)
    R_vtm = Res("vtm")
    ketm = P.sbuf("ketm", [128, 4, 128], BF16)
    R_ketm = Res("ketm")
    Sst = P.sbuf("Sst", [128, 256], F32)
    R_S = Res("Sst")
    Sbf = P.sbuf("Sbf", [128, 256], BF16)
    R_Sb = Res("Sbf")
    smallc = P.sbuf("smallc", [128, 32], F32)
    R_sm = Res("smallc")

    ps_mm = [P.psum(f"psmm{i}", [128, 512], F32) for i in range(3)]
    R_pmm = [Res(f"psmm{i}", True) for i in range(3)]
    mmctr = [0]
    ps_o = [P.psum(f"pso{i}", [128, 512], F32) for i in range(2)]
    R_po = [Res(f"pso{i}", True) for i in range(2)]
    ps_sc = P.psum("pssc", [128, 512], F32)
    R_psc = Res("pssc", True)
    ps_kv = P.psum("pskv", [128, 512], F32)
    R_pkv = Res("pskv", True)
    ps_tr = P.psum("pstr", [128, 512], F32)
    R_ptr = Res("pstr", True)

    S_misc = {}

    def msem(name):
        if name not in S_misc:
            S_misc[name] = P.dma_sem(name)
        return S_misc[name]

    def V(eng, fn, reads=(), writes=()):
        return P.op(eng, fn, reads, writes)

    def load_w(src, kc, ncols, parts=None):
        i = wctr[0] % NSLOT
        wctr[0] += 1
        if parts is None:
            parts = [(src, 0, ncols)]
        n = len(parts)

        def fn(e, s, i=i, parts=parts, kc=kc):
            for (sv, c0, ncl) in parts:
                e.dma_start(out=wsl[i][:, 0:kc, c0:c0 + ncl],
                            in_=sv.rearrange("(kc p) n -> p kc n", p=128)).then_inc(s, 16)
        P.dma("pool", fn, S_w[i], n, writes=[R_w[i]])
        return i

    def next_mm():
        i = mmctr[0] % 3
        mmctr[0] += 1
        return i

    def mm_fm(slot, col0, kc, rhs, rhs_res, ps, ps_res, m=128):
        def fn(e):
            ins = None
            for k in range(kc):
                ins = e.matmul(ps[0:m, :], lhsT=wsl[slot][:, k, col0:col0 + m], rhs=rhs(k),
                               start=(k == 0), stop=(k == kc - 1))
            return ins
        V("pe", fn, reads=[R_w[slot]] + list(rhs_res), writes=[ps_res])

    xb_rhs = lambda k: xbf[:, k, :]

    P.dma("sp", lambda e, s: e.dma_start(out=pvec[:], in_=pvec_d).then_inc(s, 16), msem("pv"), 1, writes=[R_pv])
    P.dma("sp", lambda e, s: e.dma_start(out=cst[:], in_=cst_d).then_inc(s, 16), msem("cst"), 1, writes=[R_cst])
    if 'nowal' not in _KD:
        P.dma("pool", lambda e, s: e.dma_start(out=walpha[:], in_=w_alpha_d).then_inc(s, 16), msem("wal"), 1, writes=[R_wal])
    R_c2 = Res("consts2")
    V("dve", lambda e: e.tensor_copy(out=ident_b[:], in_=ident_f), reads=[R_cst], writes=[R_c2])
    V("dve", lambda e: e.memset(ones_f[:], 1.0), writes=[R_c2])
    V("dve", lambda e: e.memset(ones_b[:], 1.0), writes=[R_c2])
    V("dve", lambda e: e.memset(rmask[:], 1.0), writes=[R_c2])
    V("dve", lambda e: e.memset(rmask[:].rearrange("p (c l) -> p c l", l=64)[:, :, 0:1], 0.0), writes=[R_c2])
    V("dve", lambda e: e.tensor_tensor(out=lbt[:, 0, :], in0=pvec[:, PV["lbt1"]:PV["lbt1"] + 32],
                                       in1=pvec[:, PV["lbt0"]:PV["lbt0"] + 32], op=ALU.subtract), reads=[R_pv], writes=[R_c2])
    V("act", lambda e: e.activation(out=lbt[:, 0, :], in_=lbt[:, 0, :], func=AF.Sigmoid), reads=[R_c2], writes=[R_c2])
    V("dve", lambda e: e.tensor_scalar(out=lbt[:, 1, :], in0=lbt[:, 0, :], scalar1=-1.0, scalar2=1.0, op0=ALU.mult, op1=ALU.add),
      reads=[R_c2], writes=[R_c2])
    V("dve", lambda e: e.tensor_scalar(out=lbt[:, 2, :], in0=lbt[:, 1, :], scalar1=-1.0, scalar2=None, op0=ALU.mult),
      reads=[R_c2], writes=[R_c2])

    lre = pvec[:, PV["lam_re"]:PV["lam_re"] + 64]
    lim = pvec[:, PV["lam_im"]:PV["lam_im"] + 64]
    K = lambda i: s5k[:, i, :]
    t6, t7 = K(6), K(7)
    xi64 = P.sbuf("xi64", [128, 64], I32)
    t8 = P.sbuf("t8", [128, 64], F32)
    t9 = P.sbuf("t9", [128, 64], F32)

    def frac_sin(out, xin, eng_reads):
        V("dve", lambda e: e.tensor_copy(out=xi64[:], in_=xin), reads=eng_reads, writes=[R_s5k])
        V("dve", lambda e: e.tensor_tensor(out=t9[:], in0=xin, in1=xi64[:], op=ALU.subtract), reads=[R_s5k], writes=[R_s5k])
        V("act", lambda e: e.activation(out=out, in_=t9[:], func=AF.Sin, scale=TWO_PI), reads=[R_s5k], writes=[R_s5k])

    if 'nos5k' not in _KD:
        V("act", lambda e: e.activation(out=t8[:, 0:16], in_=pvec[:, PV["logdt"]:PV["logdt"] + 16], func=AF.Exp), reads=[R_pv], writes=[R_s5k])
        V("dve", lambda e: e.tensor_copy(out=t7.rearrange("p (c s) -> p c s", s=4),
                                         in_=t8[:, 0:16].unsqueeze(2).to_broadcast([128, 16, 4])), reads=[R_s5k], writes=[R_s5k])
        V("dve", lambda e: e.tensor_tensor(out=t6, in0=lre, in1=t7, op=ALU.mult), reads=[R_s5k, R_pv], writes=[R_s5k])
        V("act", lambda e: e.activation(out=K(1), in_=t6, func=AF.Exp), reads=[R_s5k], writes=[R_s5k])
        V("dve", lambda e: e.tensor_tensor(out=t6, in0=lim, in1=t7, op=ALU.mult), reads=[R_s5k, R_pv], writes=[R_s5k])
        V("dve", lambda e: e.tensor_scalar(out=t6, in0=t6, scalar1=1.0 / (2.0 * math.pi), scalar2=None, op0=ALU.mult), reads=[R_s5k], writes=[R_s5k])
        V("dve", lambda e: e.tensor_scalar(out=K(0), in0=t6, scalar1=1.0, scalar2=None, op0=ALU.add), reads=[R_s5k], writes=[R_s5k])
        frac_sin(K(3), t6, [R_s5k])
        V("dve", lambda e: e.tensor_scalar(out=t8[:], in0=t6, scalar1=0.25, scalar2=None, op0=ALU.add), reads=[R_s5k], writes=[R_s5k])
        frac_sin(K(2), t8[:], [R_s5k])
        V("dve", lambda e: e.tensor_tensor(out=K(2), in0=K(2), in1=K(1), op=ALU.mult), reads=[R_s5k], writes=[R_s5k])
        V("dve", lambda e: e.tensor_scalar(out=K(2), in0=K(2), scalar1=-1.0, scalar2=None, op0=ALU.add), reads=[R_s5k], writes=[R_s5k])
        V("dve", lambda e: e.tensor_tensor(out=K(3), in0=K(3), in1=K(1), op=ALU.mult), reads=[R_s5k], writes=[R_s5k])
        V("dve", lambda e: e.tensor_tensor(out=t7, in0=lre, in1=lre, op=ALU.mult), reads=[R_s5k, R_pv], writes=[R_s5k])
        V("dve", lambda e: e.tensor_tensor(out=t8[:], in0=lim, in1=lim, op=ALU.mult), reads=[R_s5k, R_pv], writes=[R_s5k])
        V("dve", lambda e: e.tensor_tensor(out=t7, in0=t7, in1=t8[:], op=ALU.add), reads=[R_s5k], writes=[R_s5k])
        V("dve", lambda e: e.reciprocal(out=t7, in_=t7), reads=[R_s5k], writes=[R_s5k])
        V("dve", lambda e: e.tensor_tensor(out=K(4), in0=K(2), in1=lre, op=ALU.mult), reads=[R_s5k, R_pv], writes=[R_s5k])
        V("dve", lambda e: e.tensor_tensor(out=t8[:], in0=K(3), in1=lim, op=ALU.mult), reads=[R_s5k, R_pv], writes=[R_s5k])
        V("dve", lambda e: e.tensor_tensor(out=K(4), in0=K(4), in1=t8[:], op=ALU.add), reads=[R_s5k], writes=[R_s5k])
        V("dve", lambda e: e.tensor_tensor(out=K(4), in0=K(4), in1=t7, op=ALU.mult), reads=[R_s5k], writes=[R_s5k])
        V("dve", lambda e: e.tensor_tensor(out=K(5), in0=K(3), in1=lre, op=ALU.mult), reads=[R_s5k, R_pv], writes=[R_s5k])
        V("dve", lambda e: e.tensor_tensor(out=t8[:], in0=K(2), in1=lim, op=ALU.mult), reads=[R_s5k, R_pv], writes=[R_s5k])
        V("dve", lambda e: e.tensor_tensor(out=K(5), in0=K(5), in1=t8[:], op=ALU.subtract), reads=[R_s5k], writes=[R_s5k])
        V("dve", lambda e: e.tensor_tensor(out=K(5), in0=K(5), in1=t7, op=ALU.mult), reads=[R_s5k], writes=[R_s5k])
        V("dve", lambda e: e.tensor_scalar(out=t6, in0=K(0), scalar1=float(TT), scalar2=None, op0=ALU.mult), reads=[R_s5k], writes=[R_s5k])
        frac_sin(K(3), t6, [R_s5k])
        V("dve", lambda e: e.tensor_scalar(out=t8[:], in0=t6, scalar1=0.25, scalar2=None, op0=ALU.add), reads=[R_s5k], writes=[R_s5k])
        frac_sin(K(2), t8[:], [R_s5k])
    V("dve", lambda e: e.memset(s5L[:], 0.0), writes=[R_s5L, R_init])

    for ch in range(0 if 'nocd' in _KD else 16):
        cdr, cdi = W[0], W[1]
        P.dma("sp", lambda e, s, ch=ch: (e.dma_start(out=W[0][:], in_=s5cd_d[ch, :, 0, :]).then_inc(s, 16),
                                        e.dma_start(out=W[1][:], in_=s5cd_d[ch, :, 1, :]).then_inc(s, 16)),
              msem("cdld"), 2, writes=[R_W[0], R_W[1]])
        for s in range(4):
            col = ch * 4 + s
            sl = slice(128 * s, 128 * s + 128)
            fr = s5k[:, 4, col:col + 1]
            fi = s5k[:, 5, col:col + 1]
            V("dve", lambda e, sl=sl, fi=fi: e.tensor_scalar(out=W[2][:, sl], in0=W[1][:, sl], scalar1=fi, scalar2=None, op0=ALU.mult),
              reads=[R_W[1], R_s5k], writes=[R_W[2]])
            V("dve", lambda e, sl=sl, fr=fr: e.scalar_tensor_tensor(out=W[3][:, sl], in0=W[0][:, sl], scalar=fr, in1=W[2][:, sl],
                                                                   op0=ALU.mult, op1=ALU.subtract),
              reads=[R_W[0], R_W[2], R_s5k], writes=[R_W[3]])
            V("dve", lambda e, sl=sl, fi=fi: e.tensor_scalar(out=W[2][:, sl], in0=W[0][:, sl], scalar1=fi, scalar2=None, op0=ALU.mult),
              reads=[R_W[0], R_s5k], writes=[R_W[2]])
            V("dve", lambda e, sl=sl, fr=fr: e.scalar_tensor_tensor(out=W[4][:, sl], in0=W[1][:, sl], scalar=fr, in1=W[2][:, sl],
                                                                   op0=ALU.mult, op1=ALU.add),
              reads=[R_W[1], R_W[2], R_s5k], writes=[R_W[4]])
        V("act", lambda e: e.activation(out=s5w[:, 2, :], in_=W[3][:], func=AF.Copy), reads=[R_W[3]], writes=[R_s5wc])
        V("act", lambda e: e.activation(out=s5w[:, 3, :], in_=W[3][:], func=AF.Copy, scale=-1.0), reads=[R_W[3]], writes=[R_s5wc])
        V("act", lambda e: e.activation(out=s5w[:, 4, :], in_=W[4][:], func=AF.Copy, scale=-1.0), reads=[R_W[4]], writes=[R_s5wc])
        P.dma("sp", lambda e, s, ch=ch: e.dma_start(out=s5c_d[ch], in_=s5w[:, 2:5, :]).then_inc(s, 16), msem("cdst"), 1,
              reads=[R_s5wc], writes=[R_s5c[ch]])
    V("dve", lambda e: e.memset(Sst[:], 0.0), writes=[R_S])
    for h in range(0 if 'nostz' in _KD else 8):
        P.dma("sp", lambda e, s, h=h: e.dma_start(out=stg_d[h], in_=Sst[:]).then_inc(s, 16), msem("stz"), 1, reads=[R_S], writes=[R_stg[h]])
    for h in range(0 if 'nostz' in _KD else 32):
        P.dma("sp", lambda e, s, h=h: e.dma_start(out=sth_d[h], in_=Sst[:, 0:128]).then_inc(s, 16), msem("stz"), 1, reads=[R_S], writes=[R_sth[h]])

    def load_x_tile(t):
        k = 0
        for tb in range(int(os.environ.get('LOOPT', '4'))):
            for cb in range(int(os.environ.get('LOOPN', '8'))):
                wi = k % 2
                k += 1
                r0 = t * TT + tb * 128
                P.dma("sp", lambda e, s, wi=wi, r0=r0, cb=cb: e.dma_start(out=W[wi][:], in_=x_d[r0:r0 + 128, cb * 512:(cb + 1) * 512]).then_inc(s, 16),
                      msem(f"xin{wi}"), 1, writes=[R_W[wi]])

                def tr(e, wi=wi):
                    ins = None
                    for j in range(4):
                        ins = e.transpose(out=ps_tr[:, 128 * j:128 * j + 128], in_=W[wi][:, 128 * j:128 * j + 128], identity=ident_f)
                    return ins
                V("pe", tr, reads=[R_W[wi], R_cst], writes=[R_ptr])
                dst = xres[:, cb * 4:cb * 4 + 4, tb * 128:tb * 128 + 128]
                dstb = xbf[:, cb * 4:cb * 4 + 4, tb * 128:tb * 128 + 128]
                src = ps_tr[:].rearrange("p (j l) -> p j l", l=128)
                V("act", lambda e, dst=dst, src=src: e.activation(out=dst, in_=src, func=AF.Copy, scale=ALPHA),
                  reads=[R_ptr], writes=R_x[cb * 4:cb * 4 + 4])
                V("dve", lambda e, dstb=dstb, src=src: e.tensor_copy(out=dstb, in_=src), reads=[R_ptr], writes=R_xb[cb * 4:cb * 4 + 4])

    def store_tile(t, dram, scale=1.0):
        k = 0
        last = []
        for tb in range(int(os.environ.get('LOOPT', '4'))):
            for cb in range(int(os.environ.get('STN', '8'))):
                wi = k % 2
                k += 1

                def tr(e, cb=cb, tb=tb):
                    ins = None
                    for j in range(4):
                        ins = e.transpose(out=ps_tr[:, 128 * j:128 * j + 128], in_=xres[:, cb * 4 + j, tb * 128:tb * 128 + 128], identity=ident_f)
                    return ins
                V("pe", tr, reads=R_x[cb * 4:cb * 4 + 4] + [R_cst], writes=[R_ptr])
                V("act", lambda e, wi=wi: e.activation(out=W[wi][:], in_=ps_tr[:], func=AF.Copy, scale=scale), reads=[R_ptr], writes=[R_W[wi]])
                r0 = t * TT + tb * 128
                o = P.dma("sp", lambda e, s, wi=wi, r0=r0, cb=cb: e.dma_start(out=dram[r0:r0 + 128, cb * 512:(cb + 1) * 512], in_=W[wi][:]).then_inc(s, 16),
                          msem(f"xout{wi}"), 1, reads=[R_W[wi]])
                last.append(o)
        return last[-2:]

    def layer_norm(gname, bname, final=False):
        s1, s2 = ps_o[0], ps_o[1]
        for c in range(32):
            wi = c % 2
            V("act", lambda e, c=c, wi=wi: e.activation(out=W[wi][:], in_=xres[:, c, :], func=AF.Square), reads=[R_x[c]], writes=[R_W[wi]])
            V("pe", lambda e, c=c: e.matmul(s1[:], lhsT=ones_f[:], rhs=xres[:, c, :], start=(c == 0), stop=(c == 31)),
              reads=[R_x[c], R_c2], writes=[R_po[0]])
            V("pe", lambda e, c=c, wi=wi: e.matmul(s2[:], lhsT=ones_f[:], rhs=W[wi][:], start=(c == 0), stop=(c == 31)),
              reads=[R_W[wi], R_c2], writes=[R_po[1]])
        mean, rstd, tmp = W[2], W[3], W[4]
        V("act", lambda e: e.activation(out=mean[:], in_=s1[:], func=AF.Copy, scale=1.0 / 4096.0), reads=[R_po[0]], writes=[R_W[2]])
        V("dve", lambda e: e.tensor_tensor(out=tmp[:], in0=mean[:], in1=mean[:], op=ALU.mult), reads=[R_W[2]], writes=[R_W[4]])
        V("dve", lambda e: e.scalar_tensor_tensor(out=rstd[:], in0=s2[:], scalar=1.0 / 4096.0, in1=tmp[:], op0=ALU.mult, op1=ALU.subtract),
          reads=[R_po[1], R_W[4]], writes=[R_W[3]])
        V("act", lambda e: e.activation(out=rstd[:], in_=rstd[:], func=AF.Sqrt, bias=EPS), reads=[R_W[3]], writes=[R_W[3]])
        V("dve", lambda e: e.reciprocal(out=rstd[:], in_=rstd[:]), reads=[R_W[3]], writes=[R_W[3]])
        g0, b0 = PV[gname], PV[bname]
        for c in range(32):
            wi = 5 + (c % 2)
            V("dve", lambda e, c=c, wi=wi: e.tensor_tensor(out=W[wi][:], in0=xres[:, c, :], in1=mean[:], op=ALU.subtract),
              reads=[R_x[c], R_W[2]], writes=[R_W[wi]])
            V("dve", lambda e, wi=wi: e.tensor_tensor(out=W[wi][:], in0=W[wi][:], in1=rstd[:], op=ALU.mult),
              reads=[R_W[3]], writes=[R_W[wi]])
            V("act", lambda e, c=c, wi=wi: e.activation(out=xres[:, c, :], in_=W[wi][:], func=AF.Identity,
                                                        scale=pvec[:, g0 + c:g0 + c + 1], bias=pvec[:, b0 + c:b0 + c + 1]),
              reads=[R_W[wi], R_pv], writes=[R_x[c]])
            V("act", lambda e, c=c: e.activation(out=xbf[:, c, :], in_=xres[:, c, :], func=AF.Copy), reads=[R_x[c]], writes=[R_xb[c]])
            if not final:
                V("dve", lambda e, c=c: e.tensor_scalar(out=xres[:, c, :], in0=xres[:, c, :], scalar1=ALPHA, scalar2=None, op0=ALU.mult),
                  reads=[R_xb[c]], writes=[R_x[c]])

    def gemm_residual(wsrc, nk, rhs, rhs_res_fn, krows0=0):
        for mp in range(16):
            sl = load_w(wsrc[krows0:krows0 + nk * 128, mp * 256:(mp + 1) * 256], nk, 256)
            for j in range(2):
                m = mp * 2 + j
                pi = next_mm()
                mm_fm(sl, 128 * j, nk, rhs, rhs_res_fn(), ps_mm[pi], R_pmm[pi])
                V("dve", lambda e, m=m, pi=pi: e.tensor_tensor(out=xres[:, m, :], in0=xres[:, m, :], in1=ps_mm[pi][:], op=ALU.add),
                  reads=[R_pmm[pi]], writes=[R_x[m]])

    def ffn(l):
        groups = [(0, 22), (22, 44), (44, 65), (65, 86)]
        for (g0, g1) in groups:
            c = g0
            while c < g1:
                nch = min(2, g1 - c)
                sg = load_w(w_gate[l, :, c * 128:(c + nch) * 128], 32, nch * 128)
                su = load_w(w_up[l, :, c * 128:(c + nch) * 128], 32, nch * 128)
                pgs = []
                for j in range(nch):
                    pg = next_mm()
                    mm_fm(sg, 128 * j, 32, xb_rhs, R_xb, ps_mm[pg], R_pmm[pg])
                    pgs.append(pg)
                for j in range(nch):
                    pg = pgs[j]
                    wi = 6 + (j % 2)
                    V("act", lambda e, pg=pg, wi=wi: e.activation(out=W[wi][:], in_=ps_mm[pg][:], func=AF.Silu), reads=[R_pmm[pg]], writes=[R_W[wi]])
                for j in range(nch):
                    pu = next_mm()
                    mm_fm(su, 128 * j, 32, xb_rhs, R_xb, ps_mm[pu], R_pmm[pu])
                    wi = 6 + (j % 2)
                    hi = c + j - g0
                    V("dve", lambda e, pu=pu, wi=wi, hi=hi: e.tensor_tensor(out=ybuf[:, hi, :], in0=W[wi][:], in1=ps_mm[pu][:], op=ALU.mult),
                      reads=[R_W[wi], R_pmm[pu]], writes=[R_y[hi]])
                c += nch
            nk = g1 - g0
            gemm_residual(w_down[l], nk, lambda k: ybuf[:, k, :], lambda nk=nk: R_y[0:nk], krows0=g0 * 128)

    def recurrence(qf, R_qf, kf, R_kf, la, R_la, vt_col0, dv, st_dram, R_st, qscale, gate_fn, gain_col, ychunk0, mid_fn=None):
        nsl = dv // 128
        P.dma("sp", lambda e, s: e.dma_start(out=Sst[:, 0:dv], in_=st_dram).then_inc(s, 16), msem("stld"), 1, reads=[R_st], writes=[R_S])
        V("act", lambda e: e.activation(out=Sbf[:, 0:dv], in_=Sst[:, 0:dv], func=AF.Copy), reads=[R_S], writes=[R_Sb])
        bb, R_bb = la, R_la
        V("dve", lambda e: e.tensor_tensor_scan(out=bb[:], data0=rmask[:], data1=la[:], initial=0.0, op0=ALU.mult, op1=ALU.add),
          reads=[R_c2], writes=[R_bb])
        Et, R_Et = W[6], R_W[6]
        qd, R_qd = B[0], R_B[0]
        ki, R_ki = B[1], R_B[1]
        ke, R_ke = B[2], R_B[2]
        V("act", lambda e: e.activation(out=Et[:], in_=bb[:], func=AF.Exp), reads=[R_bb], writes=[R_Et])
        V("dve", lambda e: e.scalar_tensor_tensor(out=qd[:], in0=qf[:], scalar=qscale, in1=Et[:], op0=ALU.mult, op1=ALU.mult),
          reads=[R_qf, R_Et], writes=[R_qd])
        V("act", lambda e: e.activation(out=Et[:], in_=bb[:], func=AF.Exp, scale=-1.0), reads=[R_bb, R_qd], writes=[R_Et])
        V("dve", lambda e: e.tensor_tensor(out=ki[:], in0=kf[:], in1=Et[:], op=ALU.mult), reads=[R_kf, R_Et], writes=[R_ki])
        b3 = bb[:].rearrange("p (c l) -> p c l", l=64)
        V("dve", lambda e: e.tensor_copy(out=smallc[:, 0:8], in_=b3[:, :, 63]), reads=[R_bb], writes=[R_sm])
        V("act", lambda e: e.activation(out=smallc[:, 8:16], in_=smallc[:, 0:8], func=AF.Exp), reads=[R_sm], writes=[R_sm])
        V("dve", lambda e: e.tensor_tensor(out=Et[:].rearrange("p (c l) -> p c l", l=64),
                                           in0=smallc[:, 0:8].unsqueeze(2).to_broadcast([128, 8, 64]), in1=b3, op=ALU.subtract),
          reads=[R_sm, R_bb, R_ki], writes=[R_Et])
        V("act", lambda e: e.activation(out=Et[:], in_=Et[:], func=AF.Exp), reads=[R_Et], writes=[R_Et])
        V("dve", lambda e: e.tensor_tensor(out=ke[:], in0=kf[:], in1=Et[:], op=ALU.mult), reads=[R_kf, R_Et], writes=[R_ke])
        if mid_fn is not None:
            mid_fn()
        ptb = ps_tr[:].bitcast(BF16)

        def trk(e):
            ins = None
            for j in range(4):
                ins = e.transpose(out=ptb[:, 128 * j:128 * j + 128], in_=ke[:, 128 * j:128 * j + 128], identity=ident_b[:])
            return ins
        V("pe", trk, reads=[R_ke, R_c2], writes=[R_ptr])
        V("act", lambda e: e.activation(out=ketm[:].rearrange("p j d -> p (j d)"), in_=ptb[:, 0:512], func=AF.Copy), reads=[R_ptr], writes=[R_ketm])
        def sc(e):
            ins = None
            for j in range(4):
                cs = slice(128 * j, 128 * j + 128)
                ins = e.matmul(ps_sc[:, cs], lhsT=ki[:, cs], rhs=qd[:, cs], start=True, stop=True)
            return ins
        V("pe", sc, reads=[R_ki, R_qd], writes=[R_psc])
        sT, R_sT = B[3], R_B[3]
        V("dve", lambda e: e.tensor_tensor(out=sT[:].rearrange("p (j l) -> p j l", l=128), in0=ps_sc[:].rearrange("p (j l) -> p j l", l=128),
                                           in1=maskT.unsqueeze(1).to_broadcast([128, 4, 128]), op=ALU.mult),
          reads=[R_psc, R_cst], writes=[R_sT])
        for c in range(8):
            j, hb = c // 2, (c % 2) * 64
            cs = slice(64 * c, 64 * c + 64)

            kvb, R_kvb = (ps_kv, R_pkv) if c % 2 == 0 else (ps_tr, R_ptr)
            V("pe", lambda e, j=j, hb=hb, kvb=kvb: e.matmul(kvb[:, 0:dv], lhsT=ketm[hb:hb + 64, j, :], rhs=vtm[hb:hb + 64, j, vt_col0:vt_col0 + dv], start=True, stop=True),
              reads=[R_ketm, R_vtm], writes=[R_kvb])

            def inter(e, cs=cs, c=c, j=j, hb=hb):
                ins = None
                for s in range(nsl):
                    e.matmul(ps_o[s][:, cs], lhsT=vtm[hb:hb + 64, j, vt_col0 + 128 * s:vt_col0 + 128 * s + 128], rhs=sT[hb:hb + 64, cs],
                             start=True, stop=False)
                    ins = e.matmul(ps_o[s][:, cs], lhsT=Sbf[:, 128 * s:128 * s + 128], rhs=qd[:, cs], start=False, stop=True)
                return ins
            V("pe", inter, reads=[R_Sb, R_qd, R_vtm, R_sT], writes=R_po[0:nsl])
            V("dve", lambda e, c=c, kvb=kvb: e.scalar_tensor_tensor(out=Sst[:, 0:dv], in0=Sst[:, 0:dv], scalar=smallc[:, 8 + c:9 + c], in1=kvb[:, 0:dv],
                                                                    op0=ALU.mult, op1=ALU.add),
              reads=[R_kvb, R_sm], writes=[R_S])
            if c < 7:
                V("act", lambda e: e.activation(out=Sbf[:, 0:dv], in_=Sst[:, 0:dv], func=AF.Copy), reads=[R_S], writes=[R_Sb])
        P.dma("sp", lambda e, s: e.dma_start(out=st_dram, in_=Sst[:, 0:dv]).then_inc(s, 16), msem("stst"), 1, reads=[R_S], writes=[R_st])
        osq, R_osq = B[4], R_B[4]
        for s in range(nsl):
            V("act", lambda e, s=s: e.activation(out=osq[:], in_=ps_o[s][:], func=AF.Square), reads=[R_po[s]], writes=[R_osq])
            V("pe", lambda e, s=s: e.matmul(ps_sc[:], lhsT=ones_b[:], rhs=osq[:], start=(s == 0), stop=(s == nsl - 1)),
              reads=[R_osq, R_c2], writes=[R_psc])
        rs, R_rs = W[6], R_W[6]
        V("act", lambda e: e.activation(out=rs[:], in_=ps_sc[:], func=AF.Sqrt, bias=EPS, scale=1.0 / dv), reads=[R_psc], writes=[R_rs])
        V("dve", lambda e: e.reciprocal(out=rs[:], in_=rs[:]), reads=[R_rs], writes=[R_rs])
        for s in range(nsl):
            gs, R_gs = gate_fn(s)
            yt, R_yt = W[7], R_W[7]
            V("dve", lambda e, s=s: e.scalar_tensor_tensor(out=yt[:], in0=ps_o[s][:], scalar=pvec[:, gain_col + s:gain_col + s + 1], in1=rs[:],
                                                           op0=ALU.mult, op1=ALU.mult),
              reads=[R_po[s], R_rs, R_pv], writes=[R_yt])
            V("dve", lambda e, s=s, gs=gs: e.tensor_tensor(out=ybuf[:, ychunk0 + s, :], in0=yt[:], in1=gs, op=ALU.mult),
              reads=[R_yt, R_gs], writes=[R_y[ychunk0 + s]])

    def v_token_major(slot, func):
        for tb in range(4):
            pi = next_mm()

            def fn(e, tb=tb, pi=pi):
                ins = None
                for k in range(32):
                    ins = e.matmul(ps_mm[pi][:, 0:256], lhsT=xbf[:, k, tb * 128:tb * 128 + 128], rhs=wsl[slot][:, k, 0:256],
                                   start=(k == 0), stop=(k == 31))
                return ins
            V("pe", fn, reads=[R_w[slot]] + R_xb, writes=[R_pmm[pi]])
            V("act", lambda e, tb=tb, pi=pi: e.activation(out=vtm[:, tb, :], in_=ps_mm[pi][:, 0:256], func=func), reads=[R_pmm[pi]], writes=[R_vtm])

    def odd_mixer():
        for hp in range(16):
            c0 = hp * 256
            sq_ = load_w(od_w_in[:, c0:c0 + 256], 32, 256)
            for hh in range(2):
                pi = next_mm()
                mm_fm(sq_, 128 * hh, 32, xb_rhs, R_xb, ps_mm[pi], R_pmm[pi])
                V("act", lambda e, hh=hh, pi=pi: e.activation(out=W[hh][:], in_=ps_mm[pi][:], func=AF.Copy), reads=[R_pmm[pi]], writes=[R_W[hh]])
            sf = load_w(od_w_in[:, 4096 + c0:4096 + c0 + 256], 32, 256)
            for hh in range(2):
                h = hp * 2 + hh
                pi = next_mm()
                mm_fm(sf, 128 * hh, 32, xb_rhs, R_xb, ps_mm[pi], R_pmm[pi])
                sg_, R_sg = W[7], R_W[7]
                V("act", lambda e, pi=pi: e.activation(out=sg_[:], in_=ps_mm[pi][:], func=AF.Sigmoid), reads=[R_pmm[pi]], writes=[R_sg])
                V("dve", lambda e, hh=hh, h=h: e.tensor_scalar(out=W[2 + hh][:], in0=sg_[:], scalar1=lbt[:, 2, h:h + 1], scalar2=lbt[:, 1, h:h + 1],
                                                               op0=ALU.mult, op1=ALU.add), reads=[R_sg, R_c2], writes=[R_W[2 + hh]])
                V("dve", lambda e, hh=hh, h=h: e.tensor_scalar(out=W[4 + hh][:], in0=sg_[:], scalar1=lbt[:, 1, h:h + 1], scalar2=lbt[:, 0, h:h + 1],
                                                               op0=ALU.mult, op1=ALU.add), reads=[R_sg, R_c2], writes=[R_W[4 + hh]])
                V("act", lambda e, hh=hh: e.activation(out=W[4 + hh][:], in_=W[4 + hh][:], func=AF.Ln), reads=[R_W[4 + hh]], writes=[R_W[4 + hh]])
            def vg(c0=c0):
                si = load_w(od_w_in[:, 8192 + c0:8192 + c0 + 256], 32, 256)
                v_token_major(si, AF.Silu)
                sg = load_w(od_w_in[:, 12288 + c0:12288 + c0 + 256], 32, 256)
                for hh in range(2):
                    pi = next_mm()
                    mm_fm(sg, 128 * hh, 32, xb_rhs, R_xb, ps_mm[pi], R_pmm[pi])
                    V("act", lambda e, hh=hh, pi=pi: e.activation(out=B[5 + hh][:], in_=ps_mm[pi][:], func=AF.Silu), reads=[R_pmm[pi]], writes=[R_B[5 + hh]])
            for hh in range(2):
                h = hp * 2 + hh
                recurrence(W[hh], R_W[hh], W[2 + hh], R_W[2 + hh], W[4 + hh], R_W[4 + hh], 128 * hh, 128, sth_d[h], R_sth[h], 1.0,
                           lambda s, hh=hh: (B[5 + hh][:], R_B[5 + hh]), PV["hg_g"], h, mid_fn=(vg if hh == 0 else None))
        gemm_residual(od_w_out, 32, lambda k: ybuf[:, k, :], lambda: R_y)

    def even_mixer(t):
        sa = load_w(ev_w_in[:, 8192:8208], 32, 16)
        pi = next_mm()
        mm_fm(sa, 0, 32, xb_rhs, R_xb, ps_mm[pi], R_pmm[pi], m=16)
        V("act", lambda e, pi=pi: e.activation(out=alrT[:], in_=ps_mm[pi][0:16, :], func=AF.Copy), reads=[R_pmm[pi]], writes=[R_alr])
        Lre, Lim, Ire, Iim = s5L[:, 0, :], s5L[:, 1, :], s5L[:, 2, :], s5L[:, 3, :]
        cT, sT_ = s5k[:, 2, :], s5k[:, 3, :]
        V("dve", lambda e: e.tensor_tensor(out=Ire, in0=Lre, in1=cT, op=ALU.mult), reads=[R_s5L, R_s5k], writes=[R_init])
        V("dve", lambda e: e.tensor_tensor(out=t8[:], in0=Lim, in1=sT_, op=ALU.mult), reads=[R_s5L, R_s5k], writes=[R_s5k])
        V("dve", lambda e: e.tensor_tensor(out=Ire, in0=Ire, in1=t8[:], op=ALU.subtract), reads=[R_s5k], writes=[R_init])
        V("dve", lambda e: e.tensor_tensor(out=Iim, in0=Lre, in1=sT_, op=ALU.mult), reads=[R_s5L, R_s5k], writes=[R_init])
        V("dve", lambda e: e.tensor_tensor(out=t8[:], in0=Lim, in1=cT, op=ALU.mult), reads=[R_s5L, R_s5k], writes=[R_s5k])
        V("dve", lambda e: e.tensor_tensor(out=Iim, in0=Iim, in1=t8[:], op=ALU.add), reads=[R_s5k], writes=[R_init])
        for cp in range(8):
            su = load_w(ev_w_in[:, cp * 256:cp * 256 + 256], 32, 256)
            for jj in range(2):
                ch = cp * 2 + jj
                pi = next_mm()
                mm_fm(su, 128 * jj, 32, xb_rhs, R_xb, ps_mm[pi], R_pmm[pi])
                uf, R_uf = W[6], R_W[6]
                ub, R_ub = B[4], R_B[4]
                V("act", lambda e, pi=pi: e.activation(out=uf[:], in_=ps_mm[pi][:], func=AF.Copy), reads=[R_pmm[pi]], writes=[R_uf])
                V("dve", lambda e, pi=pi: e.tensor_copy(out=ub[:], in_=ps_mm[pi][:]), reads=[R_pmm[pi]], writes=[R_ub])
                P.dma("pool", lambda e, s, ch=ch: e.dma_start(out=s5w[:, 0:2, :], in_=s5bd_d[ch]).then_inc(s, 16), msem("bdld"), 1, writes=[R_s5wb])
                P.dma("sp", lambda e, s, ch=ch: e.dma_start(out=s5w[:, 2:5, :], in_=s5c_d[ch]).then_inc(s, 16), msem("cld"), 1,
                      reads=[R_s5c[ch]], writes=[R_s5wc])
                for s in range(4):
                    col = ch * 4 + s
                    sl = slice(128 * s, 128 * s + 128)
                    fp = s5k[:, 0, col:col + 1]
                    mg = s5k[:, 1, col:col + 1]
                    xa, R_xa = W[0], R_W[0]
                    xi = W[1][:].bitcast(I32)
                    sinT, R_sin = W[2], R_W[2]
                    cosT, R_cos = W[3], R_W[3]
                    V("dve", lambda e, fp=fp: e.tensor_scalar(out=xa[:], in0=iota, scalar1=fp, scalar2=None, op0=ALU.mult), reads=[R_cst, R_s5k], writes=[R_xa])
                    V("dve", lambda e: e.tensor_copy(out=xi, in_=xa[:]), reads=[R_xa], writes=[R_W[1]])
                    V("dve", lambda e: e.tensor_tensor(out=xa[:], in0=xa[:], in1=xi, op=ALU.subtract), reads=[R_W[1]], writes=[R_xa])
                    V("act", lambda e: e.activation(out=sinT[:], in_=xa[:], func=AF.Sin, scale=TWO_PI), reads=[R_xa], writes=[R_sin])
                    V("dve", lambda e, fp=fp: e.tensor_scalar(out=xa[:], in0=iota, scalar1=fp, scalar2=0.25, op0=ALU.mult, op1=ALU.add),
                      reads=[R_cst, R_s5k, R_sin], writes=[R_xa])
                    V("dve", lambda e: e.tensor_copy(out=xi, in_=xa[:]), reads=[R_xa], writes=[R_W[1]])
                    V("dve", lambda e: e.tensor_tensor(out=xa[:], in0=xa[:], in1=xi, op=ALU.subtract), reads=[R_W[1]], writes=[R_xa])
                    V("act", lambda e: e.activation(out=cosT[:], in_=xa[:], func=AF.Sin, scale=TWO_PI), reads=[R_xa], writes=[R_cos])
                    V("pe", lambda e, sl=sl: e.matmul(ps_o[0][:], lhsT=s5w[:, 0, sl], rhs=ub[:], start=True, stop=True), reads=[R_s5wb, R_ub], writes=[R_po[0]])
                    V("pe", lambda e, sl=sl: e.matmul(ps_o[1][:], lhsT=s5w[:, 1, sl], rhs=ub[:], start=True, stop=True), reads=[R_s5wb, R_ub], writes=[R_po[1]])
                    m1, R_m1 = W[0], R_W[0]
                    m2, R_m2 = W[1], R_W[1]
                    m4, R_m4 = W[4], R_W[4]
                    V("dve", lambda e: e.tensor_tensor(out=m1[:], in0=cosT[:], in1=ps_o[0][:], op=ALU.mult), reads=[R_cos, R_po[0]], writes=[R_m1])
                    V("dve", lambda e: e.tensor_tensor(out=m2[:], in0=sinT[:], in1=ps_o[1][:], op=ALU.mult), reads=[R_sin, R_po[1]], writes=[R_m2])
                    V("dve", lambda e: e.tensor_tensor(out=m1[:], in0=m1[:], in1=m2[:], op=ALU.add), reads=[R_m2], writes=[R_m1])
                    V("dve", lambda e: e.tensor_tensor(out=m2[:], in0=cosT[:], in1=ps_o[1][:], op=ALU.mult), reads=[R_cos, R_po[1], R_m1], writes=[R_m2])
                    V("dve", lambda e: e.tensor_tensor(out=m4[:], in0=sinT[:], in1=ps_o[0][:], op=ALU.mult), reads=[R_sin, R_po[0]], writes=[R_m4])
                    V("dve", lambda e: e.tensor_tensor(out=m2[:], in0=m2[:], in1=m4[:], op=ALU.subtract), reads=[R_m4], writes=[R_m2])
                    mgb = W[5]
                    V("dve", lambda e, mg=mg: e.tensor_scalar(out=mgb[:], in0=rmask[:], scalar1=0.0, scalar2=mg, op0=ALU.mult, op1=ALU.add),
                      reads=[R_c2, R_s5k], writes=[R_W[5]])
                    sre, R_sre = W[4], R_W[4]
                    sim, R_sim = W[7], R_W[7]
                    V("dve", lambda e, col=col: e.tensor_tensor_scan(out=sre[:], data0=mgb[:], data1=m1[:], initial=s5L[:, 2, col:col + 1], op0=ALU.mult, op1=ALU.add),
                      reads=[R_W[5], R_m1, R_init], writes=[R_sre])
                    V("dve", lambda e, col=col: e.tensor_tensor_scan(out=sim[:], data0=mgb[:], data1=m2[:], initial=s5L[:, 3, col:col + 1], op0=ALU.mult, op1=ALU.add),
                      reads=[R_W[5], R_m2, R_init], writes=[R_sim])
                    V("act", lambda e, col=col: e.activation(out=s5L[:, 0, col:col + 1], in_=sre[:, TT - 1:TT], func=AF.Copy), reads=[R_sre], writes=[R_s5L])
                    V("act", lambda e, col=col: e.activation(out=s5L[:, 1, col:col + 1], in_=sim[:, TT - 1:TT], func=AF.Copy), reads=[R_sim], writes=[R_s5L])
                    V("dve", lambda e: e.tensor_tensor(out=B[0][:], in0=cosT[:], in1=sre[:], op=ALU.mult), reads=[R_cos, R_sre], writes=[R_B[0]])
                    V("dve", lambda e: e.tensor_tensor(out=B[1][:], in0=sinT[:], in1=sim[:], op=ALU.mult), reads=[R_sin, R_sim], writes=[R_B[1]])
                    V("dve", lambda e: e.tensor_tensor(out=B[2][:], in0=sinT[:], in1=sre[:], op=ALU.mult), reads=[R_sin, R_sre], writes=[R_B[2]])
                    V("dve", lambda e: e.tensor_tensor(out=B[3][:], in0=cosT[:], in1=sim[:], op=ALU.mult), reads=[R_cos, R_sim], writes=[R_B[3]])

                    def ymm(e, s=s, sl=sl):
                        e.matmul(ps_kv[:], lhsT=s5w[:, 2, sl], rhs=B[0][:], start=(s == 0), stop=False)
                        e.matmul(ps_kv[:], lhsT=s5w[:, 3, sl], rhs=B[1][:], start=False, stop=False)
                        e.matmul(ps_kv[:], lhsT=s5w[:, 4, sl], rhs=B[2][:], start=False, stop=False)
                        return e.matmul(ps_kv[:], lhsT=s5w[:, 4, sl], rhs=B[3][:], start=False, stop=(s == 3))
                    V("pe", ymm, reads=[R_s5wc, R_B[0], R_B[1], R_B[2], R_B[3]], writes=[R_pkv])
                yv, R_yv = W[0], R_W[0]
                tq, R_tq = W[1], R_W[1]
                dcol = PV["s5_d"] + ch
                V("dve", lambda e, dcol=dcol: e.scalar_tensor_tensor(out=yv[:], in0=uf[:], scalar=pvec[:, dcol:dcol + 1], in1=ps_kv[:], op0=ALU.mult, op1=ALU.add),
                  reads=[R_uf, R_pkv, R_pv], writes=[R_yv])
                V("act", lambda e: e.activation(out=tq[:], in_=yv[:], func=AF.Square), reads=[R_yv], writes=[R_tq])
                V("dve", lambda e: e.tensor_scalar(out=tq[:], in0=tq[:], scalar1=0.044715, scalar2=1.0, op0=ALU.mult, op1=ALU.add), reads=[R_tq], writes=[R_tq])
                V("dve", lambda e: e.tensor_tensor(out=tq[:], in0=tq[:], in1=yv[:], op=ALU.mult), reads=[R_yv], writes=[R_tq])
                V("act", lambda e: e.activation(out=tq[:], in_=tq[:], func=AF.Sigmoid, scale=2.0 * math.sqrt(2.0 / math.pi)), reads=[R_tq], writes=[R_tq])
                V("dve", lambda e, ch=ch: e.tensor_tensor(out=ybuf[:, 16 + ch, :], in0=tq[:], in1=yv[:], op=ALU.mult), reads=[R_tq, R_yv], writes=[R_y[16 + ch]])
        for mp in range(8):
            sl_ = load_w(w_glu[:, mp * 256:(mp + 1) * 256], 16, 256)
            for j in range(2):
                m = mp * 2 + j
                pi = next_mm()
                mm_fm(sl_, 128 * j, 16, lambda k: ybuf[:, 16 + k, :], R_y[16:32], ps_mm[pi], R_pmm[pi])
                V("act", lambda e, pi=pi: e.activation(out=W[7][:], in_=ps_mm[pi][:], func=AF.Sigmoid), reads=[R_pmm[pi]], writes=[R_W[7]])
                V("dve", lambda e, m=m: e.tensor_tensor(out=ybuf[:, m, :], in0=W[7][:], in1=ybuf[:, 16 + m, :], op=ALU.mult),
                  reads=[R_W[7], R_y[16 + m]], writes=[R_y[m]])
        for h in range(8):
            sqk = load_w(None, 32, 256, parts=[(ev_w_in[:, 2048 + 128 * h:2048 + 128 * h + 128], 0, 128),
                                               (ev_w_in[:, 3072 + 128 * h:3072 + 128 * h + 128], 128, 128)])
            for hh in range(2):
                pi = next_mm()
                mm_fm(sqk, 128 * hh, 32, xb_rhs, R_xb, ps_mm[pi], R_pmm[pi])
                V("act", lambda e, hh=hh, pi=pi: e.activation(out=W[hh][:], in_=ps_mm[pi][:], func=AF.Copy), reads=[R_pmm[pi]], writes=[R_W[hh]])
            V("pe", lambda e, h=h: e.matmul(ps_sc[:], lhsT=walpha[:, 128 * h:128 * h + 128], rhs=alrT[:], start=True, stop=True),
              reads=[R_wal, R_alr], writes=[R_psc])
            la, R_la = W[4], R_W[4]
            V("dve", lambda e, h=h: e.tensor_scalar(out=la[:], in0=ps_sc[:], scalar1=pvec[:, PV["gla_b"] + h:PV["gla_b"] + h + 1], scalar2=None, op0=ALU.add),
              reads=[R_psc, R_pv], writes=[R_la])
            V("act", lambda e: e.activation(out=la[:], in_=la[:], func=AF.Exp, scale=-1.0), reads=[R_la], writes=[R_la])
            V("act", lambda e: e.activation(out=la[:], in_=la[:], func=AF.Ln, bias=1.0), reads=[R_la], writes=[R_la])
            V("dve", lambda e: e.tensor_scalar(out=la[:], in0=la[:], scalar1=-1.0 / 16.0, scalar2=None, op0=ALU.mult), reads=[R_la], writes=[R_la])
            sv = load_w(ev_w_in[:, 4096 + 256 * h:4096 + 256 * h + 256], 32, 256)
            v_token_major(sv, AF.Copy)
            sg = load_w(ev_w_in[:, 6144 + 256 * h:6144 + 256 * h + 256], 32, 256)
            for hh in range(2):
                pi = next_mm()
                mm_fm(sg, 128 * hh, 32, xb_rhs, R_xb, ps_mm[pi], R_pmm[pi])
                V("act", lambda e, hh=hh, pi=pi: e.activation(out=B[5 + hh][:], in_=ps_mm[pi][:], func=AF.Silu), reads=[R_pmm[pi]], writes=[R_B[5 + hh]])
            recurrence(W[0], R_W[0], W[1], R_W[1], la, R_la, 0, 256, stg_d[h], R_stg[h], 128.0 ** -0.5,
                       lambda s: (B[5 + s][:], R_B[5 + s]), PV["gla_g"], 16 + 2 * h)
        if stop_after == 10:
            for c in range(32):
                V("act", lambda e, c=c: e.activation(out=xres[:, c, :], in_=ybuf[:, c, :], func=AF.Copy), reads=[R_y[c]], writes=[R_x[c]])
            return
        gemm_residual(ev_w_out, 32, lambda k: ybuf[:, k, :], lambda: R_y)

    finals = []
    for t in range(NT):
        load_x_tile(t)
        if stop_after == -1:
            finals += store_tile(t, out_d, scale=1.0 / ALPHA)
            continue
        even_mixer(t)
        if stop_after == 10:
            finals += store_tile(t, out_d)
            continue
        layer_norm("mix_g0", "mix_b0", final=(stop_after == 1))
        if stop_after == 1:
            finals += store_tile(t, out_d)
            continue
        ffn(0)
        layer_norm("ffn_g0", "ffn_b0", final=(stop_after == 2))
        if stop_after == 2:
            finals += store_tile(t, out_d)
            continue
        odd_mixer()
        layer_norm("mix_g1", "mix_b1", final=(stop_after == 3))
        if stop_after == 3:
            finals += store_tile(t, out_d)
            continue
        ffn(1)
        layer_norm("ffn_g1", "ffn_b1", final=True)
        finals += store_tile(t, out_d)
    P.emit(final_waits=finals)
    P.close()
    return nc, P


def _host_layouts(inp):
    f = lambda a: np.ascontiguousarray(a, dtype=np.float32)
    cols = []
    fm = lambda v, n: f(v.reshape(n, 128).T)
    for l in range(2):
        cols += [fm(inp["ln_mix_g"][l], 32), fm(inp["ln_mix_b"][l], 32), fm(inp["ln_ffn_g"][l], 32), fm(inp["ln_ffn_b"][l], 32)]
    cols += [fm(inp["hg_lb_table"][0], 32), fm(inp["hg_lb_table"][1], 32)]
    cols += [fm(inp["gla_b_alpha"][0], 8), fm(inp["gla_norm_g"][0], 2), fm(inp["hg_norm_g"][0], 1), fm(inp["s5_d"][0], 16)]
    def lamT(a):
        a = a.reshape(16, 8, 4, 16)
        return f(a.transpose(1, 3, 0, 2).reshape(128, 64))
    cols += [lamT(inp["s5_lam_re"][0]), lamT(inp["s5_lam_im"][0])]
    ld = inp["s5_log_dt"][0].reshape(16, 8)
    cols += [f(np.repeat(ld.T[:, None, :], 16, axis=1).reshape(128, 16))]
    pvec = f(np.concatenate(cols, axis=1))
    assert pvec.shape == (128, NPV)
    cst = np.zeros((128, 768), np.float32)
    cst[:, 0:128] = np.eye(128, dtype=np.float32)
    m = np.arange(128)[:, None]
    l = np.arange(128)[None, :]
    cst[:, 128:256] = ((m // 64 == l // 64) & (l >= m)).astype(np.float32)
    cst[:, 256:768] = np.arange(512, dtype=np.float32)[None, :]
    b_re = inp["s5_b_re"][0].reshape(16, 8, 4, 16, 16)
    b_im = inp["s5_b_im"][0].reshape(16, 8, 4, 16, 16)
    c_re = inp["s5_c_re"][0].reshape(16, 8, 16, 4, 16)
    c_im = inp["s5_c_im"][0].reshape(16, 8, 16, 4, 16)
    bd = np.zeros((16, 8, 16, 2, 4, 8, 16), np.float32)
    cd = np.zeros((16, 8, 16, 2, 4, 8, 16), np.float32)
    for g in range(8):
        bd[:, g, :, 0, :, g, :] = b_re[:, g].transpose(0, 3, 1, 2)
        bd[:, g, :, 1, :, g, :] = b_im[:, g].transpose(0, 3, 1, 2)
        cd[:, g, :, 0, :, g, :] = c_re[:, g].transpose(0, 3, 2, 1)
        cd[:, g, :, 1, :, g, :] = c_im[:, g].transpose(0, 3, 2, 1)
    bd = bd.reshape(16, 128, 2, 512)
    cd = cd.reshape(16, 128, 2, 512)
    return pvec, cst, bd, cd


_CACHE = {}


def kernel(**inputs):
    inp = {k: np.asarray(v) for k, v in inputs.items()}
    pvec, cst, bd, cd = _host_layouts(inp)
    if "nc" not in _CACHE:
        _CACHE["nc"] = build(4)[0]
    nc = _CACHE["nc"]
    f = lambda a: np.ascontiguousarray(a, dtype=np.float32)
    shared = {
        "ev_w_in": f(inp["ev_w_in"][0]), "ev_w_out": f(inp["ev_w_out"][0]), "s5_w_glu": f(inp["s5_w_glu"][0]),
        "od_w_in": f(inp["od_w_in"][0]), "od_w_out": f(inp["od_w_out"][0]),
        "ffn_w_gate": f(inp["ffn_w_gate"]), "ffn_w_up": f(inp["ffn_w_up"]), "ffn_w_down": f(inp["ffn_w_down"]),
        "gla_w_alpha": f(inp["gla_w_alpha"][0]), "pvec": pvec, "cst": cst, "s5bd": bd, "s5cd": cd,
    }
    in_maps = [dict(shared, x=f(inp["x"][b])) for b in range(8)]
    res = run_bass_kernel_spmd(nc, in_maps, core_ids=list(range(8)))
    return np.stack([r["out"] for r in res.results], axis=0).astype(np.float32)
```
